# Optimizing a Trainium2 kernel written in Bass

```python
import jax, jax.numpy as jnp
from jax import lax
import numpy as np

D_MODEL = 1024
BATCH = 8
SEQ = 2048
DEPTH = 2
DEC_BATCH = 32
DEC_SEQ = 4
PAST_LEN = 8192
PAGE_SIZE = 128

HEAD_DIM = 64
MIX_WIDTH = D_MODEL
SGU_WIDTH = MIX_WIDTH // 4
SGU_GROUPS = 4
SGU_GROUP_DIM = SGU_WIDTH // SGU_GROUPS
ATT_WIDTH = MIX_WIDTH - SGU_WIDTH
N_ATT_HEADS = ATT_WIDTH // HEAD_DIM
DIL_PATTERNS = ((128, 1), (512, 4), (2048, 16))
HEADS_PER_PATTERN = N_ATT_HEADS // len(DIL_PATTERNS)
ROT_DIM = HEAD_DIM // 4
ROPE_THETA = 500000.0
CHUNK = 128
QBLOCK = 128
N_MEM = 256
MEM_HEADS = 4
MEM_HEAD_DIM = D_MODEL // MEM_HEADS
D_FF = 4 * D_MODEL
IN_COLS = 3 * ATT_WIDTH + 2 * SGU_WIDTH
DEEPNORM_ALPHA = (2 * DEPTH) ** 0.25
DEEPNORM_BETA = (8 * DEPTH) ** -0.25
LN_EPS = 1e-5
NEG = -1e30

kernel_name = 'hymba_style_dilated_attn_chunk_sgu_deepnorm_step'


def layer_norm(x, g, b):
    xf = x.astype(jnp.float32)
    mu = jnp.mean(xf, -1, keepdims=True)
    var = jnp.mean(jnp.square(xf - mu), -1, keepdims=True)
    y = (xf - mu) * lax.rsqrt(var + LN_EPS) * g.astype(jnp.float32) + b.astype(jnp.float32)
    return y.astype(x.dtype)


def rope_partial(x, pos):
    half = ROT_DIM // 2
    inv = ROPE_THETA ** (-jnp.arange(half, dtype=jnp.float32) / half)
    ang = pos.astype(jnp.float32)[:, None] * inv[None, :]
    cos = jnp.cos(ang)[None, :, None, :]
    sin = jnp.sin(ang)[None, :, None, :]
    xr = x[..., :ROT_DIM].astype(jnp.float32)
    x1, x2 = xr[..., :half], xr[..., half:]
    rot = jnp.concatenate([x1 * cos - x2 * sin, x2 * cos + x1 * sin], -1)
    return jnp.concatenate([rot.astype(x.dtype), x[..., ROT_DIM:]], -1)


def dilated_window_attention(q, k_all, v_all, n_prefix, window, dilation):
    B, T, H, Dh = q.shape
    qb = min(T, QBLOCK)
    nb = T // qb
    dist = dilation * jnp.arange(window // dilation + 1)
    scale = HEAD_DIM ** -0.5
    q_blocks = jnp.moveaxis(q.reshape(B, nb, qb, H, Dh), 1, 0)

    def block(args):
        qc, c = args
        rows = n_prefix + c * qb + jnp.arange(qb)
        idx = rows[:, None] - dist[None, :]
        valid = idx >= 0
        idx = jnp.maximum(idx, 0)
        kg = jnp.take(k_all, idx, axis=1)
        vg = jnp.take(v_all, idx, axis=1)
        s = jnp.einsum('bqhd,bqjhd->bqhj', qc, kg).astype(jnp.float32) * scale
        s = jnp.where(valid[None, :, None, :], s, NEG)
        m = jnp.max(s, -1, keepdims=True)
        pr = jnp.exp(s - m)
        den = jnp.sum(pr, -1)
        o = jnp.einsum('bqhj,bqjhd->bqhd', pr, vg.astype(jnp.float32)) / den[..., None]
        return o, m[..., 0] + jnp.log(den)

    o, lse = lax.map(block, (q_blocks, jnp.arange(nb)))
    o = jnp.moveaxis(o, 0, 1).reshape(B, T, H, Dh)
    lse = jnp.moveaxis(lse, 0, 1).reshape(B, T, H)
    return o, lse


def chunk_spatial_gate(u, v, w_s, b_s):
    B, T, _ = u.shape
    c = min(T, CHUNK)
    n = T // c
    mask = jnp.tril(jnp.ones((c, c), dtype=bool))
    w = jnp.where(mask[None], w_s[:, :c, :c], 0.0).astype(v.dtype)
    vr = v.reshape(B, n, c, SGU_GROUPS, SGU_GROUP_DIM)
    mixed = jnp.einsum('gts,bnsgd->bntgd', w, vr) + b_s[:, :c].T[None, None, :, :, None]
    return u * mixed.reshape(B, T, SGU_WIDTH).astype(u.dtype)


def decoder_layer(x, pos, past_kv, mem_k, mem_v, p):
    B, T, _ = x.shape
    proj = x @ p['w_in']
    a = ATT_WIDTH
    q = rope_partial(proj[..., :a].reshape(B, T, N_ATT_HEADS, HEAD_DIM), pos)
    k = rope_partial(proj[..., a:2 * a].reshape(B, T, N_ATT_HEADS, HEAD_DIM), pos)
    v = proj[..., 2 * a:3 * a].reshape(B, T, N_ATT_HEADS, HEAD_DIM)
    u = jax.nn.gelu(proj[..., 3 * a:3 * a + SGU_WIDTH])
    gate_in = jax.nn.gelu(proj[..., 3 * a + SGU_WIDTH:])

    outs, lses, win_rows = [], [], []
    for gi, (win, dil) in enumerate(DIL_PATTERNS):
        hs = slice(gi * HEADS_PER_PATTERN, (gi + 1) * HEADS_PER_PATTERN)
        new_kv = jnp.stack([k[:, :, hs], v[:, :, hs]], axis=2)
        if past_kv is None:
            kv_all = new_kv
            n_prefix = 0
            win_rows.append(new_kv[:, T - min(win, T):])
        else:
            past = past_kv[gi].astype(new_kv.dtype)
            kv_all = jnp.concatenate([past, new_kv], axis=1)
            n_prefix = past.shape[1]
            win_rows.append(new_kv)
        o, lse = dilated_window_attention(q[:, :, hs], kv_all[:, :, 0], kv_all[:, :, 1], n_prefix, win, dil)
        outs.append(o)
        lses.append(lse)
    mixw = jax.nn.softmax(jnp.stack(lses, 0), axis=0)
    att = jnp.concatenate([mixw[i][..., None] * outs[i] for i in range(len(DIL_PATTERNS))], axis=2)
    att = att.reshape(B, T, ATT_WIDTH).astype(x.dtype)

    gv = layer_norm(gate_in, p['sgu_ln_g'], p['sgu_ln_b'])
    sgu = chunk_spatial_gate(u, gv, p['w_spatial'], p['b_spatial'])

    mixed = jnp.concatenate([att, sgu], -1) @ p['w_mix_out']
    x = layer_norm(DEEPNORM_ALPHA * x + mixed, p['ln1_g'], p['ln1_b'])

    qx = (x @ p['w_xq']).reshape(B, T, MEM_HEADS, MEM_HEAD_DIM)
    s = jnp.einsum('bthd,bmhd->bhtm', qx, mem_k).astype(jnp.float32) * MEM_HEAD_DIM ** -0.5
    pr = jax.nn.softmax(s, axis=-1)
    ox = jnp.einsum('bhtm,bmhd->bthd', pr, mem_v.astype(jnp.float32)).reshape(B, T, D_MODEL).astype(x.dtype)
    x = layer_norm(DEEPNORM_ALPHA * x + ox @ p['w_xo'], p['ln2_g'], p['ln2_b'])

    h = jnp.square(jax.nn.relu(x @ p['w_up']))
    x = layer_norm(DEEPNORM_ALPHA * x + h @ p['w_down'], p['ln3_g'], p['ln3_b'])
    return x, win_rows, gv


def setup_inputs(seed: int = 0) -> dict:
    key = jax.random.key(seed)
    ks = jax.random.split(key, 32)
    f32 = jnp.float32

    def nrm(k, shape, scale=1.0):
        return jax.random.normal(k, shape, f32) * scale

    hg = HEADS_PER_PATTERN
    inp = {}
    inp['x_prompt'] = nrm(ks[0], (BATCH, SEQ, D_MODEL))
    inp['x_sample'] = nrm(ks[1], (DEC_BATCH, DEC_SEQ, D_MODEL))
    inp['cache_kv_w128'] = nrm(ks[2], (DEPTH, DEC_BATCH, min(128, PAST_LEN), 2, hg, HEAD_DIM))
    inp['cache_kv_w512'] = nrm(ks[3], (DEPTH, DEC_BATCH, min(512, PAST_LEN), 2, hg, HEAD_DIM))
    inp['cache_kv_w2048'] = nrm(ks[4], (DEPTH, DEC_BATCH, min(2048, PAST_LEN), 2, hg, HEAD_DIM))
    inp['cache_mem_kv'] = nrm(ks[5], (DEPTH, DEC_BATCH, N_MEM, 2, MEM_HEADS, MEM_HEAD_DIM))
    inp['mem_prompt'] = nrm(ks[6], (BATCH, N_MEM, D_MODEL))
    inp['w_in'] = nrm(ks[7], (DEPTH, D_MODEL, IN_COLS), D_MODEL ** -0.5)
    inp['sgu_ln_g'] = 1.0 + nrm(ks[8], (DEPTH, SGU_WIDTH), 0.01)
    inp['sgu_ln_b'] = nrm(ks[9], (DEPTH, SGU_WIDTH), 0.01)
    inp['w_spatial'] = nrm(ks[10], (DEPTH, SGU_GROUPS, CHUNK, CHUNK), CHUNK ** -0.5)
    inp['b_spatial'] = 1.0 + nrm(ks[11], (DEPTH, SGU_GROUPS, CHUNK), 0.01)
    inp['w_mix_out'] = nrm(ks[12], (DEPTH, MIX_WIDTH, D_MODEL), DEEPNORM_BETA * MIX_WIDTH ** -0.5)
    inp['ln1_g'] = 1.0 + nrm(ks[13], (DEPTH, D_MODEL), 0.01)
    inp['ln1_b'] = nrm(ks[14], (DEPTH, D_MODEL), 0.01)
    inp['w_xq'] = nrm(ks[15], (DEPTH, D_MODEL, D_MODEL), D_MODEL ** -0.5)
    inp['w_xkv'] = nrm(ks[16], (DEPTH, D_MODEL, 2 * D_MODEL), D_MODEL ** -0.5)
    inp['w_xo'] = nrm(ks[17], (DEPTH, D_MODEL, D_MODEL), DEEPNORM_BETA * D_MODEL ** -0.5)
    inp['ln2_g'] = 1.0 + nrm(ks[18], (DEPTH, D_MODEL), 0.01)
    inp['ln2_b'] = nrm(ks[19], (DEPTH, D_MODEL), 0.01)
    inp['w_up'] = nrm(ks[20], (DEPTH, D_MODEL, D_FF), D_MODEL ** -0.5)
    inp['w_down'] = nrm(ks[21], (DEPTH, D_FF, D_MODEL), DEEPNORM_BETA * D_FF ** -0.5)
    inp['ln3_g'] = 1.0 + nrm(ks[22], (DEPTH, D_MODEL), 0.01)
    inp['ln3_b'] = nrm(ks[23], (DEPTH, D_MODEL), 0.01)
    return inp


def reference(x_prompt, x_sample, cache_kv_w128, cache_kv_w512, cache_kv_w2048, cache_mem_kv, mem_prompt,
              w_in, sgu_ln_g, sgu_ln_b, w_spatial, b_spatial, w_mix_out, ln1_g, ln1_b,
              w_xq, w_xkv, w_xo, ln2_g, ln2_b, w_up, w_down, ln3_g, ln3_b):
    pos_p = jnp.arange(x_prompt.shape[1], dtype=jnp.int32)
    pos_s = PAST_LEN + jnp.arange(x_sample.shape[1], dtype=jnp.int32)
    hp, hs = x_prompt, x_sample
    rows_p = [[], [], []]
    rows_s = [[], [], []]
    mem_p = []
    chunk_v = []
    for l in range(DEPTH):
        p = {'w_in': w_in[l], 'sgu_ln_g': sgu_ln_g[l], 'sgu_ln_b': sgu_ln_b[l],
             'w_spatial': w_spatial[l], 'b_spatial': b_spatial[l], 'w_mix_out': w_mix_out[l],
             'ln1_g': ln1_g[l], 'ln1_b': ln1_b[l], 'w_xq': w_xq[l], 'w_xo': w_xo[l],
             'ln2_g': ln2_g[l], 'ln2_b': ln2_b[l], 'w_up': w_up[l], 'w_down': w_down[l],
             'ln3_g': ln3_g[l], 'ln3_b': ln3_b[l]}
        mkv = (mem_prompt @ w_xkv[l]).reshape(mem_prompt.shape[0], mem_prompt.shape[1], 2, MEM_HEADS, MEM_HEAD_DIM)
        mem_p.append(mkv)
        hp, wp, _ = decoder_layer(hp, pos_p, None, mkv[:, :, 0], mkv[:, :, 1], p)
        hs, ws, gvs = decoder_layer(hs, pos_s, (cache_kv_w128[l], cache_kv_w512[l], cache_kv_w2048[l]),
                                    cache_mem_kv[l, :, :, 0], cache_mem_kv[l, :, :, 1], p)
        for gi in range(len(DIL_PATTERNS)):
            rows_p[gi].append(wp[gi])
            rows_s[gi].append(ws[gi])
        chunk_v.append(gvs)
    return (hp, hs,
            jnp.stack(rows_p[0]), jnp.stack(rows_p[1]), jnp.stack(rows_p[2]), jnp.stack(mem_p),
            jnp.stack(rows_s[0]), jnp.stack(rows_s[1]), jnp.stack(rows_s[2]), jnp.stack(chunk_v))
```

```python
import contextlib
import numpy as np
import concourse.bass as bass
import concourse.mybir as mybir
from concourse.bass_utils import run_bass_kernel_spmd

F32 = mybir.dt.float32
BF16 = mybir.dt.bfloat16
AF = mybir.ActivationFunctionType
ALU = mybir.AluOpType

D = 1024
T = 2048
NT = 16
DEPTH = 2
NS = 16
NCORES = 8
IN_COLS = 2816
DFF = 4096
ALPHA = float((2 * DEPTH) ** 0.25)
EPS = 1e-5
PAST = 8192
GROUPS = ((128, 1), (512, 4), (2048, 16))
NSLOT = 4
N_DMA_SEMS = 12
ENGS = ("pe", "act", "dve", "pool", "sp")

CB_ID, CB_MASK, CB_ONES, CB_MP0, CB_MN0, CB_MN1, CB_SGM, NCB = 0, 128, 384, 512, 544, 576, 608, 624
CF_CS1, CF_CS2, CF_SS1, CF_SS2, NCF = 0, 256, 512, 528, 544


class Op:
    __slots__ = ("eng", "fn", "reads", "writes", "dma", "idx", "deps", "sig", "semslot", "semcnt", "n_dma", "raw")


class Prog:
    def __init__(self, nc, same_engine_sync=True):
        self.nc = nc
        self.ops = []
        self.same_engine_sync = same_engine_sync
        self.last_writer = {}
        self.readers = {}
        self.final_ops = []
        self.fences = {}

    def op(self, eng, fn, reads=(), writes=(), dma=False, n_dma=1, final=False):
        o = Op()
        o.eng, o.fn, o.reads, o.writes, o.dma, o.n_dma = eng, fn, tuple(reads), tuple(writes), dma, n_dma
        o.sig = None
        o.semslot = None
        o.semcnt = None
        o.idx = len(self.ops)
        deps = set()
        for k in o.reads:
            w = self.last_writer.get(k)
            if w is not None:
                deps.add(w)
            f = self.fences.get(k[0])
            if f is not None:
                deps.add(f)
            if k[0] == "ps":
                for r in self.readers.get(k, ()):
                    if self.ops[r].eng != eng:
                        deps.add(r)
        for k in o.writes:
            w = self.last_writer.get(k)
            if w is not None:
                deps.add(w)
            for r in self.readers.get(k, ()):
                deps.add(r)
            f = self.fences.get(k[0])
            if f is not None:
                deps.add(f)
        deps.discard(o.idx)
        o.deps = sorted(deps)
        o.raw = set()
        for k in o.reads:
            w = self.last_writer.get(k)
            if w is not None:
                o.raw.add(w)
        for k in o.reads:
            self.readers.setdefault(k, []).append(o.idx)
        for k in o.writes:
            self.last_writer[k] = o.idx
            self.readers[k] = []
        self.ops.append(o)
        if final:
            self.final_ops.append(o.idx)
        return o

    def fence(self, eng, fn, old_prefixes, new_prefixes):
        deps = set()
        for k, w in self.last_writer.items():
            if k[0] in old_prefixes and w is not None:
                deps.add(w)
        for k, rs in self.readers.items():
            if k[0] in old_prefixes:
                deps.update(rs)
        for p in old_prefixes:
            f = self.fences.get(p)
            if f is not None:
                deps.add(f)
        o = Op()
        o.eng, o.fn, o.reads, o.writes, o.dma, o.n_dma = eng, fn, (), (), False, 1
        o.sig = None
        o.semslot = None
        o.semcnt = None
        o.idx = len(self.ops)
        o.deps = sorted(deps)
        o.raw = set(deps)
        self.ops.append(o)
        for p in new_prefixes:
            self.fences[p] = o.idx
        for k in [k for k in self.last_writer if k[0] in old_prefixes]:
            del self.last_writer[k]
        for k in [k for k in self.readers if k[0] in old_prefixes]:
            del self.readers[k]
        return o

    def pe(self, fn, reads=(), writes=()):
        return self.op("pe", fn, reads, writes)

    def act(self, fn, reads=(), writes=()):
        return self.op("act", fn, reads, writes)

    def dve(self, fn, reads=(), writes=()):
        return self.op("dve", fn, reads, writes)

    def pool(self, fn, reads=(), writes=()):
        return self.op("pool", fn, reads, writes)

    def dma(self, q, fn, reads=(), writes=(), final=False, n_dma=1):
        return self.op(q, fn, reads, writes, dma=True, n_dma=n_dma, final=final)

    def emit(self):
        nc = self.nc
        ops = self.ops
        needed = [False] * len(ops)
        for o in ops:
            for d in o.deps:
                p = ops[d]
                if p.dma or p.eng != o.eng or (self.same_engine_sync and p.eng != "pe" and d in o.raw):
                    needed[d] = True
        for i in self.final_ops:
            needed[i] = True
        with contextlib.ExitStack() as st:
            eng_sem = {e: st.enter_context(nc.semaphore("s_" + e)) for e in ENGS}
            dma_sems = {q: [st.enter_context(nc.semaphore("d_%s%d" % (q, i))) for i in range(N_DMA_SEMS)]
                        for q in ("sp", "act", "pool")}
            block = st.enter_context(nc.Block())
            cnt = {e: 0 for e in ENGS}
            dcnt = {q: [0] * N_DMA_SEMS for q in dma_sems}
            drr = {q: 0 for q in dma_sems}
            for o in ops:
                if o.dma:
                    q = o.eng
                    s = drr[q] % N_DMA_SEMS
                    drr[q] += 1
                    o.semslot = s
                    o.semcnt = dcnt[q][s]
                    dcnt[q][s] += 16 * o.n_dma
                    o.sig = (dma_sems[q][s], dcnt[q][s])
                elif needed[o.idx]:
                    cnt[o.eng] += 1
                    o.sig = (eng_sem[o.eng], cnt[o.eng])
            per_eng = {e: [o for o in ops if o.eng == e] for e in ENGS}
            final_ops = [ops[i] for i in self.final_ops]
            same = self.same_engine_sync

            def run(e, engobj):
                waited = {}

                def wait(sem, val):
                    key = id(sem)
                    if waited.get(key, 0) >= val:
                        return
                    waited[key] = val
                    engobj.wait_ge(sem, val)

                for o in per_eng[e]:
                    if o.dma and o.semcnt > 0:
                        wait(dma_sems[e][o.semslot], o.semcnt)
                    for d in o.deps:
                        p = ops[d]
                        if p.sig is None:
                            continue
                        if (not p.dma) and p.eng == e and (e == "pe" or not same or d not in o.raw):
                            continue
                        wait(*p.sig)
                    ins = o.fn(engobj)
                    if o.dma:
                        if isinstance(ins, (list, tuple)):
                            assert len(ins) == o.n_dma
                            for i_ in ins:
                                i_.then_inc(o.sig[0], 16)
                        else:
                            assert o.n_dma == 1
                            ins.then_inc(o.sig[0], 16)
                    elif o.sig is not None:
                        ins.then_inc(o.sig[0], 1)
                if e == "sp":
                    for o in final_ops:
                        wait(*o.sig)

            @block.tensor
            def _(eng):
                run("pe", eng)

            @block.scalar
            def _(eng):
                run("act", eng)

            @block.vector
            def _(eng):
                run("dve", eng)

            @block.gpsimd
            def _(eng):
                run("pool", eng)

            @block.sync
            def _(eng):
                run("sp", eng)


def psk(bank, c0, n):
    return [("ps", bank)]


def build_program(stop_after=None):
    nc = bass.Bass("TRN2", target_bir_lowering=False)

    def din(name, shape):
        return nc.dram_tensor(name, list(shape), F32, kind="ExternalInput").ap()

    def dout(name, shape):
        return nc.dram_tensor(name, list(shape), F32, kind="ExternalOutput").ap()

    xp = din("xp", [T, D])
    xs = din("xs", [NS, D])
    cpast = [din("c128", [DEPTH, 4, 128, 512]), din("c512", [DEPTH, 4, 512, 512]), din("c2048", [DEPTH, 4, 2048, 512])]
    cmem = din("cmem", [DEPTH, 4, 256, 2048])
    memp = din("memp", [256, D])
    w_in = din("w_in", [DEPTH, D, IN_COLS])
    sgu_g = din("sgu_ln_g", [DEPTH, 256])
    sgu_b = din("sgu_ln_b", [DEPTH, 256])
    w_sp = din("w_spatial", [DEPTH, 4, 128, 128])
    b_sp = din("b_spatial", [DEPTH, 4, 128])
    w_mix = din("w_mix_out", [DEPTH, D, D])
    ln_g = [din("ln1_g", [DEPTH, D]), din("ln2_g", [DEPTH, D]), din("ln3_g", [DEPTH, D])]
    ln_b = [din("ln1_b", [DEPTH, D]), din("ln2_b", [DEPTH, D]), din("ln3_b", [DEPTH, D])]
    w_xq = din("w_xq", [DEPTH, D, D])
    w_xkv = din("w_xkv", [DEPTH, D, 2 * D])
    w_xo = din("w_xo", [DEPTH, D, D])
    w_up = din("w_up", [DEPTH, D, DFF])
    w_down = din("w_down", [DEPTH, DFF, D])
    cb_d = din("cb", [128, NCB])
    cf_d = din("cf", [128, NCF])

    yp = dout("yp", [T, D])
    ys = dout("ys", [NS, D])
    kvp = [dout("kv128p", [DEPTH, 128, 512]), dout("kv512p", [DEPTH, 512, 512]), dout("kv2048p", [DEPTH, 2048, 512])]
    memkvp = dout("memkvp", [DEPTH, 256, 2 * D])
    kvs = [dout("kv128s", [DEPTH, NS, 512]), dout("kv512s", [DEPTH, NS, 512]), dout("kv2048s", [DEPTH, NS, 512])]
    chunkv = dout("chunkv", [DEPTH, NS, 256])
    xscr = nc.dram_tensor("xscr", [T, D], F32).ap()

    st = contextlib.ExitStack()

    def sb(name, shape, dt):
        return st.enter_context(nc.sbuf_tensor(name, list(shape), dt))

    with st:
        XT = sb("XT", [128, 8, T], BF16)
        XST = sb("XST", [128, 8, NS], BF16)
        XS = sb("XS", [NS, D], F32)
        Yt = sb("Y", [128, 16384], BF16)
        Zt = sb("Z", [128, 16384], F32)
        St = sb("S", [128, 6144], F32)
        WR = sb("WR", [128, NSLOT, 4096], BF16)
        LNT = sb("LNT", [128, 2, 1024], F32)
        CB = sb("CB", [128, NCB], BF16)
        CF = sb("CF", [128, NCF], F32)
        XB = sb("XB", [128, 2, 1024], BF16)
        STAT = sb("STAT", [128, 4, 16], F32)
        STATG = sb("STATG", [128, 4, 16], F32)
        STATM = sb("STATM", [128, 5, 16], F32)
        HTS = sb("HTS", [128, 32, NS], BF16)
        SDEN = sb("SDEN", [128, 32], F32)
        SO = sb("SO", [128, 3, 32], F32)
        ATTS = sb("ATTS", [128, 8, NS], BF16)
        QXST = sb("QXST", [128, 8, NS], BF16)
        SPM = sb("SPM", [128, 2, 16], BF16)
        SRD2 = sb("SRD2", [128, 16], F32)
        GVF = sb("GVF", [NS, 256], F32)
        DUMMY = sb("DUMMY", [128, 8], F32)
        EPSB = sb("EPSB", [128, 1], F32)
        MHALF = sb("MHALF", [128, 1], F32)
        PSA = st.enter_context(nc.psum_tensor("PSA", [128, 4096], F32))
        Zb = Zt[:].bitcast(BF16)
        Sb = St[:].bitcast(BF16)

        def carve(base_f32, base_b16, state, limit, shape, dt):
            npart = shape[0]
            nel = 1
            for d_ in shape[1:]:
                nel *= d_
            esz = 4 if dt == F32 else 2
            o = state["o"]
            state["o"] = o + ((nel * esz + 31) // 32) * 32
            assert state["o"] <= limit, (state["o"], limit)
            if dt == F32:
                v = base_f32[0:npart, o // 4: o // 4 + nel]
            else:
                v = base_b16[0:npart, o // 2: o // 2 + nel]
            fd = shape[1:]
            if len(fd) == 2:
                v = v.rearrange("p (a b) -> p a b", b=fd[1])
            elif len(fd) == 3:
                v = v.rearrange("p (a b c) -> p a b c", b=fd[1], c=fd[2])
            elif len(fd) == 4:
                v = v.rearrange("p (a b c d) -> p a b c d", b=fd[1], c=fd[2], d=fd[3])
            return v

        zstate = {"o": 7168 * 4}
        AB_NAMES = []

        def zal(name, shape, dt):
            AB_NAMES.append(name)
            return carve(Zt, Zb, zstate, 65536, shape, dt)

        hole1 = {"o": 4096}
        hole2 = {"o": 14336}

        def zal1(name, shape, dt):
            AB_NAMES.append(name)
            return carve(Zt, Zb, hole1, 8192, shape, dt)

        def zal2(name, shape, dt):
            AB_NAMES.append(name)
            return carve(Zt, Zb, hole2, 20480, shape, dt)

        PKV = zal("PKV", [128, 4, 4, 2, 128], BF16)
        PKT = zal("PKT", [128, 4, 4, 128], BF16)
        QKB = zal1("QKB", [128, 4, 256], BF16)
        RT = zal1("RT", [128, 4, 2, 64], F32)
        KF = zal2("KF", [128, 2, 128], F32)
        VF = zal2("VF", [128, 2, 128], F32)
        PT = zal("PT", [128, 3, 512], BF16)
        SGW = zal("SGW", [128, 4, 128], BF16)
        SGWS = zal("SGWS", [NS, 4, NS], BF16)
        SGWSF = zal("SGWSF", [NS, 4, NS], F32)
        SGL = zal2("SGL", [128, 4, 128], BF16)
        GAM = zal("GAM", [128, 2, 256], F32)
        BSP = zal("BSP", [128, 4, 128], F32)
        BSPS = zal("BSPS", [128, 4, NS], F32)
        UG = zal("UG", [128, 3, 256], F32)
        GG = zal("GG", [128, 3, 256], F32)
        GTMP = zal("GTMP", [128, 3, 256], F32)
        GV = zal("GV", [128, 3, 256], BF16)
        STMP = zal2("STMP", [128, 2, 128], F32)
        SQKB = zal("SQKB", [NS, 256], BF16)
        SRT = zal("SRT", [NS, 2, 64], F32)
        SKF = zal("SKF", [NS, 128], F32)
        SVF = zal("SVF", [NS, 128], F32)
        SVB = zal("SVB", [NS, 128], BF16)
        SQT = zal("SQT", [128, NS], BF16)
        SKT = zal("SKT", [128, NS], BF16)
        SPP = zal("SPP", [128, 32], BF16)
        SPN = zal("SPN", [NS, 32], BF16)

        def sal(slot, shape, dt):
            return carve(St, Sb, {"o": slot * 4096}, 24576, shape, dt)

        MEMT = sal(0, [128, 8, 256], BF16)
        PM = sal(0, [128, 2, 1024], BF16)
        CMK = sal(0, [128, 2, 1024], BF16)
        RELU = sal(0, [128, 4, 512], F32)
        MKF = sal(1, [128, 2, 512], F32)
        RDEN = sal(1, [128, 2, 512], F32)
        MEMKT = sal(2, [128, 8, 256], BF16)
        MEMV = sal(3, [128, 2, 1024], BF16)
        SCV = sal(4, [128, 2, 1024], BF16)
        MKTS = sal(5, [128, 8, 256], BF16)
        XBM = sal(2, [128, 4, 8, 128], BF16)
        S_M, S_X, S_3 = ("MEMT", "MKF"), ("PM", "CMK", "RDEN", "MEMKT", "MEMV"), ("RELU", "XBM")
        mlp_pend = []
        mlp_ln_pend = []
        XSB = XB[0:NS, 0, :]

        PSB = PSA[:].bitcast(BF16)
        Zb = Zt[:].bitcast(BF16)
        X = Zt[:].rearrange("p (t d) -> p t d", d=D)
        KT = [Zb[:, i * 2048:(i + 1) * 2048] for i in range(2)]
        VAUG = [Zb[:, 4096 + i * 3072: 4096 + i * 3072 + 2048].rearrange("p (c v) -> p c v", v=128) for i in range(2)]
        DEN = Zt[:, 5120:5120 + 2048]
        ATT = Yt[:, 0:12288].rearrange("p (c t) -> p c t", t=T)
        SGUT = Yt[:, 12288:16384].rearrange("p (c t) -> p c t", t=T)
        ATTALL = Yt[:].rearrange("p (c t) -> p c t", t=T)
        QXT = Yt[:].rearrange("p (c t) -> p c t", t=T)
        HT = Yt[:].rearrange("p (c t) -> p c t", t=512)
        IDENT = CB[:, CB_ID:CB_ID + 128]
        MASK = CB[:, CB_MASK:CB_MASK + 256]
        ONES = CB[:, CB_ONES:CB_ONES + 128]
        AB_PREF = tuple(["KT", "VAUG", "DEN"] + AB_NAMES)
        Y_AB = ("ATT",)

        P = Prog(nc)

        def bank(b, c0=0, n=512):
            return PSA[:, b * 512 + c0: b * 512 + c0 + n]

        def bankb(b, c0=0, n=1024):
            return PSB[:, b * 1024 + c0: b * 1024 + c0 + n]

        P.dma("pool", lambda e: e.dma_start(out=CB[:], in_=cb_d), writes=[("CB",)])
        P.dma("sp", lambda e: e.dma_start(out=CF[:], in_=cf_d), writes=[("CF",)])
        P.dma("sp", lambda e: e.dma_start(out=XS[:], in_=xs), writes=[("XS",)])
        P.pool(lambda e: e.memset(EPSB[:], EPS), writes=[("EPSB",)])
        P.pool(lambda e: e.memset(MHALF[:], -0.5), writes=[("EPSB",)])

        ring = {"n": 0}

        def take_slots(k):
            s = [(ring["n"] + i) % NSLOT for i in range(k)]
            ring["n"] += k
            return s

        def wview(slot, ncols, nk=8):
            return WR[:, slot, 0:nk * ncols].rearrange("p (k n) -> p k n", n=ncols)

        def wsrc(w2d, c0, c1, k0=0, nk=8):
            return w2d[k0 * 128:(k0 + nk) * 128, c0:c1].rearrange("(k p) n -> p k n", p=128)

        tr = {"n": 0, "banks": [6, 7]}

        def transpose_to(srcs, dsts, nrows, reads, writes):
            assert len(srcs) <= 8
            B = tr["banks"][tr["n"] % len(tr["banks"])]
            tr["n"] += 1
            idn = IDENT[0:nrows, 0:nrows]

            def f(e, srcs=srcs, B=B, idn=idn):
                ins = None
                for i, a_ in enumerate(srcs):
                    ins = e.transpose(out=bankb(B, i * nrows, nrows), in_=a_, identity=idn)
                return ins
            P.pe(f, reads=list(reads) + [("CB",)], writes=[("ps", B)])
            for (d, i0, n) in dsts:
                src = bankb(B, i0 * nrows, n * nrows)
                if n > 1:
                    src = src.rearrange("p (a r) -> p a r", r=nrows)
                P.act(lambda e, d=d, src=src: e.copy(out=d, in_=src), reads=[("ps", B)], writes=writes)

        def mm_group(out, pairs, reads, writes):
            n = len(pairs)

            def f(e, out=out, pairs=pairs, n=n):
                ins = None
                for i, (l, r) in enumerate(pairs):
                    ins = e.matmul(out, lhsT=l, rhs=r, start=(i == 0), stop=(i == n - 1))
                return ins
            P.pe(f, reads=reads, writes=writes)

        def make_xt_from_dram(src, t, buf):
            P.dma("pool", lambda e: e.dma_start(out=XB[:, buf, :], in_=src[t * 128:(t + 1) * 128, :]),
                  writes=[("XB", buf)])
            transpose_to([XB[:, buf, k * 128:(k + 1) * 128] for k in range(8)],
                         [(XT[:, :, t * 128:(t + 1) * 128], 0, 8)], 128,
                         reads=[("XB", buf)], writes=[("XT", t)])

        import os
        KD = os.environ.get("KDBG", "xs")
        if "x" in KD:
            for t in range(int(os.environ.get("KNT", NT))):
                make_xt_from_dram(xp, t, t % 2)
        if "s" in KD:
            P.dma("pool", lambda e: e.dma_start(out=XSB[:], in_=xs), writes=[("XB", 0)])
            transpose_to([XSB[:, k * 128:(k + 1) * 128] for k in range(8)], [(XST[:], 0, 8)], NS,
                         reads=[("XB", 0)], writes=[("XST",)])

        ln_rr = {"n": 0}

        def rstd_ops(stt, sk, npart):
            P.pool(lambda e: e.tensor_tensor(out=stt[:, 15:16], in0=stt[:, 13:14], in1=EPSB[0:npart, :], op=ALU.add),
                   reads=[sk, ("EPSB",)], writes=[sk])
            P.pool(lambda e: e.tensor_tensor(out=stt[:, 14:15], in0=stt[:, 15:16], in1=MHALF[0:npart, :], op=ALU.pow),
                   reads=[sk, ("EPSB",)], writes=[sk])

        def ln_s1_stats(xv, stt, sk, npart):
            P.dve(lambda e: e.bn_aggr(stt[:, 12:14], stt[:, 0:12]), reads=[sk], writes=[sk])
            rstd_ops(stt, sk, npart)

        def ln_s1(xv, ps2, ps_keys, npart, xkeys, stt, sk):
            P.dve(lambda e: e.scalar_tensor_tensor(out=xv, in0=xv, scalar=ALPHA, in1=ps2, op0=ALU.mult, op1=ALU.add),
                  reads=list(ps_keys) + list(xkeys), writes=xkeys)
            for hf in range(2):
                P.dve(lambda e, hf=hf: e.bn_stats(stt[:, hf * 6:(hf + 1) * 6], xv[:, hf * 512:(hf + 1) * 512]),
                      reads=xkeys, writes=[sk])
            ln_s1_stats(xv, stt, sk, npart)

        def ln_s2(xv, npart, xkeys, stt, sk):
            P.dve(lambda e: e.scalar_tensor_tensor(out=xv, in0=xv, scalar=stt[:, 12:13], in1=LNT[0:npart, 0, :],
                                                   op0=ALU.subtract, op1=ALU.mult),
                  reads=[sk, ("LNT",)] + list(xkeys), writes=xkeys)

        def ln_s3(xv, npart, xkeys, stt, sk):
            P.dve(lambda e: e.scalar_tensor_tensor(out=xv, in0=xv, scalar=stt[:, 14:15], in1=LNT[0:npart, 1, :],
                                                   op0=ALU.mult, op1=ALU.add),
                  reads=[sk, ("LNT",)] + list(xkeys), writes=xkeys)

        def load_ln_tables(l, which):
            P.dma("sp", lambda e: e.dma_start(out=LNT[:, 0, :], in_=ln_g[which][l].partition_broadcast(128)),
                  writes=[("LNT",)])
            P.dma("sp", lambda e: e.dma_start(out=LNT[:, 1, :], in_=ln_b[which][l].partition_broadcast(128)),
                  writes=[("LNT",)])

        def ln_stage_tile(t, last_layer, final_ln, stage):
            xv = X[:, t, :]
            if final_ln and last_layer:
                P.dma("sp", lambda e: e.dma_start(out=yp[t * 128:(t + 1) * 128, :], in_=xv), reads=[("X", t)], final=True)
                return None
            sv, skeys = stage
            P.act(lambda e: e.copy(out=sv, in_=xv.rearrange("p (k c) -> p k c", c=128)), reads=[("X", t)], writes=skeys)
            if final_ln:
                P.dma("sp", lambda e: e.dma_start(out=xscr[t * 128:(t + 1) * 128, :], in_=xv), reads=[("X", t)],
                      writes=[("xscr", t)])
            return (t, sv, skeys)

        def ln_stage_sample(last_layer, final_ln):
            if final_ln and last_layer:
                P.dma("sp", lambda e: e.dma_start(out=ys, in_=XS[:]), reads=[("XS",)], final=True)
                return None
            P.act(lambda e: e.copy(out=XSB[:], in_=XS[:]), reads=[("XS",)], writes=[("XB", 0)])
            return ("s", None, None)

        def ln_transposes(item):
            if item is None:
                return
            t, sv, skeys = item
            if t == "s":
                transpose_to([XSB[:, k * 128:(k + 1) * 128] for k in range(8)], [(XST[:], 0, 8)], NS,
                             reads=[("XB", 0)], writes=[("XST",)])
            else:
                transpose_to([sv[:, k, :] for k in range(8)], [(XT[:, :, t * 128:(t + 1) * 128], 0, 8)], 128,
                             reads=skeys, writes=[("XT", t)])

        pj_rr = {"n": 0}

        pj_banks = {"b": [0, 1]}

        def pj_slot():
            bl = pj_banks["b"]
            s = bl[pj_rr["n"] % len(bl)]
            pj_rr["n"] += 1
            return s

        items = []

        def layer(l):
            last_layer = (l == DEPTH - 1)

            def sgu_consts():
                P.dma("sp", lambda e: e.dma_start(out=GAM[:, 0, :], in_=sgu_g[l].partition_broadcast(128)), writes=[("GAM",)])
                P.dma("sp", lambda e: e.dma_start(out=GAM[:, 1, :], in_=sgu_b[l].partition_broadcast(128)), writes=[("GAM",)])
                P.dma("sp", lambda e: e.dma_start(out=BSP[:], in_=b_sp[l].partition_broadcast(128)), writes=[("BSP",)])
                srcb = bass.AP(b_sp.tensor, b_sp[l].offset, [[0, 128], [128, 4], [1, 4]])
                P.dma("sp", lambda e: [e.dma_start(out=BSPS[:, :, b_ * 4:(b_ + 1) * 4], in_=srcb) for b_ in range(4)],
                      writes=[("BSPS",)], n_dma=4)
                P.dma("pool", lambda e: e.dma_start(out=SGL[:], in_=w_sp[l].rearrange("g t s -> t g s")), writes=[("SGL",)])
                transpose_to([SGL[:, g, :] for g in range(4)], [(SGW[:], 0, 4)], 128,
                             reads=[("SGL",)], writes=[("SGW",)])
                P.dve(lambda e: e.tensor_tensor(out=SGW[:], in0=SGW[:],
                                                in1=MASK[:, 128:256].unsqueeze(1).broadcast_to([128, 4, 128]), op=ALU.mult),
                      reads=[("SGW",), ("CB",)], writes=[("SGW",)])
                P.pool(lambda e: e.memset(SGWSF[:], 0.0), writes=[("SGWSF",)])
                for b in range(4):
                    def f(e, b=b):
                        return [e.dma_start(out=SGWSF[b * 4:(b + 1) * 4, g_, b * 4:(b + 1) * 4],
                                            in_=w_sp[l, g_, 0:4, 0:4].rearrange("t s -> s t"),
                                            allow_slow_non_contiguous=True) for g_ in range(4)]
                    P.dma("sp", f, reads=[("SGWSF",)], writes=[("SGWSF",)], n_dma=4)
                P.dve(lambda e: e.tensor_tensor(out=SGWS[:], in0=SGWSF[:],
                                                in1=CB[0:NS, CB_SGM:CB_SGM + NS].unsqueeze(1).broadcast_to([NS, 4, NS]),
                                                op=ALU.mult),
                      reads=[("SGWSF",), ("CB",)], writes=[("SGWS",)])

            def rope_ops(psv, nh, npart, cs1, cs2, rt1, rt2, ps_keys, rtkey, extra_reads=()):
                v = psv.rearrange("p (h d) -> p h d", d=64)
                x1 = v[:, :, 0:8].unsqueeze(2).broadcast_to([npart, nh, 2, 8])
                x2 = v[:, :, 8:16].unsqueeze(2).broadcast_to([npart, nh, 2, 8])
                c1 = cs1.rearrange("p (a d) -> p a d", d=8).unsqueeze(1).broadcast_to([npart, nh, 2, 8])
                c2 = cs2.rearrange("p (a d) -> p a d", d=8).unsqueeze(1).broadcast_to([npart, nh, 2, 8])
                o1 = rt1.rearrange("p (h a d) -> p h a d", a=2, d=8)
                o2 = rt2.rearrange("p (h a d) -> p h a d", a=2, d=8)
                P.dve(lambda e: e.tensor_tensor(out=o1, in0=x1, in1=c1, op=ALU.mult), reads=list(ps_keys) + [("CF",)] + list(extra_reads),
                      writes=[rtkey])
                P.dve(lambda e: e.tensor_tensor(out=o2, in0=x2, in1=c2, op=ALU.mult), reads=list(ps_keys) + [("CF",)] + list(extra_reads),
                      writes=[rtkey])
                return o1, o2

            for j in range(2):
                for g in range(3):
                    win, dil = GROUPS[g]
                    nb = NT // dil
                    c = 2 * g + j
                    kb = 0
                    nrows = min(win, T)
                    row0 = T - nrows

                    def load_ab(g=g, j=j):
                        s = take_slots(1)[0]
                        cq = 256 * g + 128 * j

                        def f(e):
                            W = wview(s, 384)
                            return [e.dma_start(out=W[:, :, i * 128:(i + 1) * 128],
                                                in_=wsrc(w_in[l], 768 * i + cq, 768 * i + cq + 128)) for i in range(3)]
                        P.dma("pool", f, writes=[("WR", s)], n_dma=3)
                        return s

                    def comp_ab(s, g=g, j=j, win=win, dil=dil, nb=nb, c=c, kb=kb, nrows=nrows, row0=row0):
                        W = wview(s, 384)
                        tr["banks"] = [7]
                        ktb = KT[kb]
                        vab = VAUG[kb]

                        def tok(r, b):
                            return slice(r + dil * 128 * b, r + dil * 128 * b + dil * 127 + 1, dil)

                        def attkeys(t):
                            return [("ATT", c, r, t // dil) for r in range(dil)]

                        def ktkeys(t):
                            return [("KT", kb, r, t // dil) for r in range(dil)]

                        ni = 1 if g == 0 else 4
                        srcp = cpast[g][l].rearrange("b (m s) c -> m b s c", s=dil)

                        def fpast(e):
                            return [e.dma_start(out=PKV[:, b_, 0:ni, kv, :],
                                                in_=srcp[:, b_, 0:ni, kv * 256 + j * 128: kv * 256 + j * 128 + 128])
                                    for kv in range(2) for b_ in range(4)]
                        P.dma("pool", fpast, writes=[("PKV",)], n_dma=8)
                        KAB = int(os.environ.get("KAB", "9"))
                        if KAB <= 1:
                            return
                        def qk_tile(t):
                            ps = pj_slot()
                            bk, c0 = ps, 0
                            out = bank(bk, c0, 256)
                            pk = psk(bk, c0, 256)
                            mm_group(out, [(XT[:, k, t * 128:(t + 1) * 128], W[:, k, 0:256]) for k in range(8)],
                                     reads=[("XT", t), ("WR", s)], writes=pk)
                            qb = t % 4
                            P.act(lambda e: e.copy(out=QKB[:, qb, :], in_=out), reads=pk, writes=[("QKB", qb)])
                            KB = os.environ.get("KB", "mrkt")
                            if "r" not in KB:
                                return qb
                            cs1 = CF[:, CF_CS1 + t * 16: CF_CS1 + t * 16 + 16]
                            cs2 = CF[:, CF_CS2 + t * 16: CF_CS2 + t * 16 + 16]
                            o1, o2 = rope_ops(out, 4, 128, cs1, cs2, RT[:, qb, 0, :], RT[:, qb, 1, :], pk, ("RT", qb), [("QKB", qb)])
                            dq = QKB[:, qb, :].rearrange("p (h d) -> p h d", d=64)[:, :, 0:16].rearrange("p h (a d) -> p h a d", d=8)
                            P.dve(lambda e: e.tensor_tensor(out=dq, in0=o1, in1=o2, op=ALU.add),
                                  reads=[("RT", qb), ("QKB", qb)], writes=[("QKB", qb)])
                            if t * 128 >= row0 and "k" in KB:
                                kfb = qb % 2
                                P.act(lambda e: e.copy(out=KF[:, kfb, :], in_=out[:, 128:256]), reads=pk, writes=[("KF", kfb)])
                                dk = KF[:, kfb, :].rearrange("p (h d) -> p h d", d=64)[:, :, 0:16].rearrange("p h (a d) -> p h a d", d=8)
                                P.dve(lambda e: e.tensor_tensor(out=dk, in0=o1[:, 2:4], in1=o2[:, 2:4], op=ALU.add),
                                       reads=[("RT", qb), ("KF", kfb)], writes=[("KF", kfb)])
                                r0 = t * 128 - row0
                                P.dma("sp", lambda e: e.dma_start(out=kvp[g][l, r0:r0 + 128, j * 128:(j + 1) * 128], in_=KF[:, kfb, :]),
                                      reads=[("KF", kfb)], final=True)
                            return qb

                        def qk_transposes(t, qb):
                            if "t" not in os.environ.get("KB", "mrkt"):
                                return
                            transpose_to([QKB[:, qb, 0:128], QKB[:, qb, 128:256]],
                                         [(ATT[:, c, t * 128:(t + 1) * 128], 0, 1), (ktb[:, t * 128:(t + 1) * 128], 1, 1)], 128,
                                         reads=[("QKB", qb)], writes=attkeys(t) + ktkeys(t))

                        def v_chunk(r, b):
                            ch = r * nb + b
                            ps = pj_slot()
                            bk, c0 = ps, 0
                            out = bank(bk, c0, 128)
                            pk = psk(bk, c0, 128)
                            tl = list(range(dil * b, dil * b + dil))
                            mm_group(out, [(XT[:, k, tok(r, b)], W[:, k, 256:384]) for k in range(8)],
                                     reads=[("XT", t_) for t_ in tl] + [("WR", s)], writes=pk)
                            P.act(lambda e: e.copy(out=vab[:, ch, :], in_=out), reads=pk, writes=[("VAUG", kb, ch)])
                            tok0 = r + dil * 128 * b
                            if tok0 >= row0:
                                vb = ch % 2
                                P.dve(lambda e: e.tensor_copy(out=VF[:, vb, :], in_=out), reads=pk, writes=[("VF", vb)])
                                dst = kvp[g][l].rearrange("(i s) c -> i s c", s=dil)
                                i0 = (tok0 - row0 - r) // dil
                                P.dma("sp", lambda e: e.dma_start(out=dst[i0:i0 + 128, r, 256 + j * 128: 256 + (j + 1) * 128],
                                                                  in_=VF[:, vb, :]),
                                      reads=[("VF", vb)], final=True)

                        pj_banks["b"] = [0, 1, 2, 3, 4, 5]
                        pendq = []
                        for t in range(NT):
                            qb = qk_tile(t)
                            pendq.append((t, qb))
                            if len(pendq) > 3:
                                qk_transposes(*pendq.pop(0))
                        ps = pj_slot()
                        bk, c0 = ps, 0
                        outs_ = bank(bk, c0, 256)[0:NS, :]
                        pk = psk(bk, c0, 256)
                        mm_group(outs_, [(XST[:, k, :], W[:, k, 0:256]) for k in range(8)],
                                 reads=[("XST",), ("WR", s)], writes=pk)
                        while pendq:
                            qk_transposes(*pendq.pop(0))
                        if ni == 1:
                            transpose_to([PKV[:, b, 0, 0, :] for b in range(4)], [(PKT[:, :, 0, :], 0, 4)], 128,
                                         reads=[("PKV",)], writes=[("PKT",)])
                        else:
                            for b2 in range(0, 4, 2):
                                transpose_to([PKV[:, b, i, 0, :] for b in (b2, b2 + 1) for i in range(4)],
                                             [(PKT[:, b2, :, :], 0, 4), (PKT[:, b2 + 1, :, :], 4, 4)], 128,
                                             reads=[("PKV",)], writes=[("PKT",)])

                        P.act(lambda e: e.copy(out=SQKB[:], in_=outs_), reads=pk, writes=[("SQKB",)])
                        o1, o2 = rope_ops(outs_, 4, NS, CF[0:NS, CF_SS1:CF_SS1 + 16], CF[0:NS, CF_SS2:CF_SS2 + 16],
                                          SRT[:, 0, :], SRT[:, 1, :], pk, ("SRT",))
                        dq = SQKB[:].rearrange("p (h d) -> p h d", d=64)[:, :, 0:16].rearrange("p h (a d) -> p h a d", d=8)
                        P.dve(lambda e: e.tensor_tensor(out=dq, in0=o1, in1=o2, op=ALU.add), reads=[("SRT",), ("SQKB",)],
                              writes=[("SQKB",)])
                        P.act(lambda e: e.copy(out=SKF[:], in_=outs_[:, 128:256]), reads=pk, writes=[("SKF",)])
                        dk = SKF[:].rearrange("p (h d) -> p h d", d=64)[:, :, 0:16].rearrange("p h (a d) -> p h a d", d=8)
                        P.dve(lambda e: e.tensor_tensor(out=dk, in0=o1[:, 2:4], in1=o2[:, 2:4], op=ALU.add),
                               reads=[("SRT",), ("SKF",)], writes=[("SKF",)])
                        P.dma("sp", lambda e: e.dma_start(out=kvs[g][l, :, j * 128:(j + 1) * 128], in_=SKF[:]),
                              reads=[("SKF",)], final=True)
                        transpose_to([SQKB[:, 0:128], SQKB[:, 128:256]], [(SQT[:], 0, 1), (SKT[:], 1, 1)], NS,
                                     reads=[("SQKB",)], writes=[("SQT",), ("SKT",)])
                        ps = pj_slot()
                        bk, c0 = ps, 0
                        outv = bank(bk, c0, 128)[0:NS, :]
                        pkv = psk(bk, c0, 128)
                        mm_group(outv, [(XST[:, k, :], W[:, k, 256:384]) for k in range(8)],
                                 reads=[("XST",), ("WR", s)], writes=pkv)
                        P.act(lambda e: e.copy(out=SVB[:], in_=outv), reads=pkv, writes=[("SVB",)])
                        P.dve(lambda e: e.tensor_copy(out=SVF[:], in_=outv), reads=pkv, writes=[("SVF",)])
                        P.dma("sp", lambda e: e.dma_start(out=kvs[g][l, :, 256 + j * 128: 256 + (j + 1) * 128], in_=SVF[:]),
                              reads=[("SVF",)], final=True)
                        if KAB <= 3:
                            return
                        for r in range(dil):
                            for b in range(nb):
                                v_chunk(r, b)

                        if KAB <= 4:
                            return
                        SB_ = 6
                        Sp = bank(SB_, 0, 32)
                        Sn = bank(SB_, 32, 32)[0:NS, :]
                        Dn = bank(SB_, 64, 32)
                        Ov = bank(SB_, 128, 32)
                        k0 = k1 = ("ps", SB_)

                        def f_sp(e):
                            ins = None
                            for hp in range(2):
                                rows = slice(hp * 64, hp * 64 + 64)
                                for b in range(4):
                                    if g == 0:
                                        ins = e.matmul(Sp[:, hp * 16 + b * 4: hp * 16 + b * 4 + 4], lhsT=PKT[rows, b, 0, :],
                                                       rhs=SQT[rows, b * 4:b * 4 + 4], start=True, stop=True)
                                    else:
                                        for i in range(4):
                                            col = hp * 16 + b * 4 + i
                                            ins = e.matmul(Sp[:, col:col + 1], lhsT=PKT[rows, b, i, :],
                                                           rhs=SQT[rows, b * 4 + i:b * 4 + i + 1], start=True, stop=True)
                                ins = e.matmul(Sn[:, hp * 16:hp * 16 + 16], lhsT=SKT[rows, :], rhs=SQT[rows, :],
                                               start=True, stop=True)
                            return ins
                        P.pe(f_sp, reads=[("PKT",), ("SQT",), ("SKT",)], writes=[k0])
                        P.act(lambda e: e.activation(out=SPP[:], in_=Sp, func=AF.Exp, scale=0.125), reads=[k0], writes=[("SPP",)])
                        P.act(lambda e: e.activation(out=SPN[:], in_=Sn, func=AF.Exp, scale=0.125), reads=[k0], writes=[("SPN",)])
                        if g == 0:
                            P.dve(lambda e: e.tensor_tensor(out=SPP[:], in0=SPP[:], in1=CB[:, CB_MP0:CB_MP0 + 32], op=ALU.mult),
                                  reads=[("SPP",), ("CB",)], writes=[("SPP",)])
                        mn = CB_MN0 if g == 0 else CB_MN1
                        P.dve(lambda e: e.tensor_tensor(out=SPN[:], in0=SPN[:], in1=CB[0:NS, mn:mn + 32], op=ALU.mult),
                              reads=[("SPN",), ("CB",)], writes=[("SPN",)])

                        def f_pv(e):
                            e.matmul(Dn, lhsT=ONES, rhs=SPP[:], start=True, stop=False)
                            ins = e.matmul(Dn, lhsT=ONES[0:NS, :], rhs=SPN[:], start=False, stop=True)
                            for hp in range(2):
                                for b in range(4):
                                    if g == 0:
                                        cs = slice(hp * 16 + b * 4, hp * 16 + b * 4 + 4)
                                        e.matmul(Ov[:, cs], lhsT=PKV[:, b, 0, 1, :], rhs=SPP[:, cs], start=True, stop=False)
                                        ins = e.matmul(Ov[:, cs], lhsT=SVB[:], rhs=SPN[:, cs], start=False, stop=True)
                                    else:
                                        for i in range(4):
                                            cs = slice(hp * 16 + b * 4 + i, hp * 16 + b * 4 + i + 1)
                                            e.matmul(Ov[:, cs], lhsT=PKV[:, b, i, 1, :], rhs=SPP[:, cs], start=True, stop=False)
                                            ins = e.matmul(Ov[:, cs], lhsT=SVB[:], rhs=SPN[:, cs], start=False, stop=True)
                            return ins
                        P.pe(f_pv, reads=[("SPP",), ("SPN",), ("PKV",), ("SVB",), ("CB",)], writes=[k0, k1])
                        if g == 0:
                            P.dve(lambda e: e.tensor_copy(out=SDEN[:], in_=Dn), reads=[k0], writes=[("SDEN",)])
                        else:
                            P.dve(lambda e: e.tensor_tensor(out=SDEN[:], in0=SDEN[:], in1=Dn, op=ALU.add),
                                  reads=[k0, ("SDEN",)], writes=[("SDEN",)])
                        P.act(lambda e: e.copy(out=SO[:, g, :], in_=Ov), reads=[k1], writes=[("SO", g)])

                        if KAB <= 5:
                            return
                        blocks = [(r, b) for r in range(dil) for b in range(nb)]
                        NB = len(blocks)
                        state = {}
                        NPT = 3

                        def stage1(n):
                            r, b = blocks[n]
                            ss = n % NPT
                            bk = 2 * (n % 2)
                            S = PSA[:, bk * 512: bk * 512 + 1024]
                            pk = [("ps", bk), ("ps", bk + 1)]
                            chunks = ([b - 1] if b > 0 else []) + [b]
                            lo = 0 if b > 0 else 128

                            def f(e):
                                ins = None
                                for hp in range(2):
                                    rows = slice(hp * 64, hp * 64 + 64)
                                    q = ATT[rows, c, tok(r, b)]
                                    for ci, bb in enumerate(chunks):
                                        cc = hp * 512 + lo + ci * 128
                                        ins = e.matmul(S[:, cc:cc + 128], lhsT=ktb[rows, tok(r, bb)], rhs=q, start=True, stop=True)
                                return ins
                            P.pe(f, reads=[("ATT", c, r, b)] + [("KT", kb, r, bb) for bb in chunks], writes=pk)
                            Sv = S.rearrange("p (h c) -> p h c", c=512)[:, :, lo:256]
                            Pv = PT[:, ss, :].rearrange("p (h c) -> p h c", c=256)[:, :, lo:256]
                            A2H = os.environ.get("A2H", "em")
                            if "e" in A2H:
                                P.act(lambda e: e.activation(out=Pv, in_=Sv, func=AF.Exp, scale=0.125), reads=pk, writes=[("PT", ss)])
                            else:
                                for hp_ in range(2):
                                    P.act(lambda e, hp_=hp_: e.activation(out=PT[:, ss, hp_ * 256 + lo:hp_ * 256 + 256],
                                                                          in_=S[:, hp_ * 512 + lo:hp_ * 512 + 256], func=AF.Exp, scale=0.125),
                                          reads=pk, writes=[("PT", ss)])
                            if "m" in A2H:
                                mk_ = MASK[:, lo:256].unsqueeze(1).broadcast_to([128, 2, 256 - lo])
                                P.dve(lambda e: e.tensor_tensor(out=Pv, in0=Pv, in1=mk_, op=ALU.mult),
                                      reads=[("PT", ss), ("CB",)], writes=[("PT", ss)])
                            else:
                                for hp_ in range(2):
                                    P.dve(lambda e, hp_=hp_: e.tensor_tensor(out=PT[:, ss, hp_ * 256 + lo:hp_ * 256 + 256],
                                                                             in0=PT[:, ss, hp_ * 256 + lo:hp_ * 256 + 256],
                                                                             in1=MASK[:, lo:256], op=ALU.mult),
                                          reads=[("PT", ss), ("CB",)], writes=[("PT", ss)])
                            state[n] = (ss, chunks, lo)

                        def stage2(n):
                            r, b = blocks[n]
                            ss, chunks, lo = state.pop(n)
                            ob = 4 + n % 2
                            O = bank(ob, 0, 256)
                            pk = [("ps", ob)]

                            def f(e):
                                ins = None
                                nchk = len(chunks)
                                for hp in range(2):
                                    rows = slice(hp * 64, hp * 64 + 64)
                                    tp = (0, 64) if hp else None
                                    for which in range(2):
                                        for ci, bb in enumerate(chunks):
                                            cc = hp * 256 + lo + ci * 128
                                            lhs = vab[:, r * nb + bb, hp * 64:(hp + 1) * 64] if which == 0 else ONES[:, 0:64]
                                            ins = e.matmul(O[rows, which * 128:(which + 1) * 128], lhsT=lhs, rhs=PT[:, ss, cc:cc + 128],
                                                           start=(ci == 0), stop=(ci == nchk - 1), tile_position=tp)
                                return ins
                            P.pe(f, reads=[("PT", ss), ("CB",)] + [("VAUG", kb, r * nb + bb) for bb in chunks], writes=pk)
                            P.act(lambda e: e.copy(out=ATT[:, c, tok(r, b)], in_=O[:, 0:128]), reads=pk, writes=[("ATT", c, r, b)])
                            dkeys = [("DEN", hp, t_) for hp in range(2) for t_ in range(dil * b, dil * b + dil)]
                            if g == 0:
                                P.dve(lambda e: e.tensor_copy(out=DEN[:, tok(r, b)], in_=O[:, 128:256]), reads=pk, writes=dkeys)
                            else:
                                P.dve(lambda e: e.tensor_tensor(out=DEN[:, tok(r, b)], in0=DEN[:, tok(r, b)], in1=O[:, 128:256],
                                                                op=ALU.add),
                                      reads=pk + dkeys, writes=dkeys)

                        LAG = 2
                        for n in range(NB + LAG):
                            if n < NB:
                                stage1(n)
                            if n >= LAG:
                                stage2(n - LAG)

                        if g == 2:
                            for q4 in range(4):
                                cs = slice(q4 * 512, (q4 + 1) * 512)
                                dnk = [("DEN", hp, t_) for hp in range(2) for t_ in range(q4 * 4, q4 * 4 + 4)]
                                P.act(lambda e, cs=cs: e.activation(out=DEN[:, cs], in_=DEN[:, cs], func=AF.Ln), reads=dnk, writes=dnk)
                                P.act(lambda e, cs=cs: e.activation(out=DEN[:, cs], in_=DEN[:, cs], func=AF.Exp, scale=-1.0),
                                      reads=dnk, writes=dnk)
                                for g2 in range(3):
                                    c2 = 2 * g2 + j
                                    d2 = GROUPS[g2][1]
                                    ak = [("ATT", c2, r_, t_ // d2) for r_ in range(d2) for t_ in range(q4 * 4, q4 * 4 + 4)]
                                    ak = sorted(set(ak))
                                    P.dve(lambda e, cs=cs, c2=c2: e.tensor_tensor(out=ATT[:, c2, cs], in0=ATT[:, c2, cs],
                                                                                  in1=DEN[:, cs], op=ALU.mult),
                                          reads=dnk + ak, writes=ak)
                            P.dve(lambda e: e.reciprocal(out=SDEN[:], in_=SDEN[:]), reads=[("SDEN",)], writes=[("SDEN",)])
                            for g2 in range(3):
                                for hp in range(2):
                                    rows = slice(hp * 64, hp * 64 + 64)
                                    P.dve(lambda e, g2=g2, rows=rows, hp=hp: e.tensor_tensor(
                                        out=ATTS[rows, 2 * g2 + j, :], in0=SO[rows, g2, hp * 16:hp * 16 + 16],
                                        in1=SDEN[rows, hp * 16:hp * 16 + 16], op=ALU.mult),
                                        reads=[("SO", g2), ("SDEN",)], writes=[("ATTS",)])
                    items.append((load_ab, comp_ab))

            def load_sgu():
                s = take_slots(1)[0]
                P.dma("pool", lambda e: e.dma_start(out=wview(s, 512), in_=wsrc(w_in[l], 2304, 2816)), writes=[("WR", s)])
                return s

            def comp_sgu(s):
                W = wview(s, 512)
                tr["banks"] = [6, 7]
                sgu_consts()

                pj_banks["b"] = [0, 1, 4, 5]

                def sgu_tile(t, npart, xt_tok, xt_keys, dst, dst_keys, wsg, bsp, wsg_key, sample=False):
                    ub = t % 3
                    ps = pj_slot()
                    bk, c0 = ps, 0
                    pk = psk(bk, c0, 256)
                    Ups = bank(bk, c0, 256)

                    def fu(e):
                        ins = None
                        for uc in range(2):
                            for k in range(8):
                                ins = e.matmul(Ups[:, uc * 128: uc * 128 + npart], lhsT=W[:, k, uc * 128:(uc + 1) * 128],
                                               rhs=xt_tok(k), start=(k == 0), stop=(k == 7))
                        return ins
                    P.pe(fu, reads=list(xt_keys) + [("WR", s)], writes=pk)
                    Uv = Ups.rearrange("p (u t) -> p u t", t=128)[:, :, 0:npart]
                    ugv = UG[:, ub, :].rearrange("p (u t) -> p u t", t=128)[:, :, 0:npart]
                    P.act(lambda e: e.activation(out=ugv, in_=Uv, func=AF.Gelu_apprx_tanh), reads=pk, writes=[("UG", ub)])
                    ps2 = pj_slot()
                    bk2, c02 = ps2, 0
                    pk2 = psk(bk2, c02, 256)
                    Gps = bank(bk2, c02, 256)[0:npart, :]
                    mm_group(Gps, [(xt_tok(k), W[:, k, 256:512]) for k in range(8)], reads=list(xt_keys) + [("WR", s)], writes=pk2)
                    gg = GG[0:npart, ub, :]
                    P.act(lambda e: e.activation(out=gg, in_=Gps, func=AF.Gelu_apprx_tanh), reads=pk2, writes=[("GG", ub)])
                    stt = STATG[0:npart, ub, :]
                    sk = ("STATG", ub)
                    P.dve(lambda e: e.bn_stats(stt[:, 0:6], gg), reads=[("GG", ub)], writes=[sk])
                    P.dve(lambda e: e.bn_aggr(stt[:, 12:14], stt[:, 0:6]), reads=[sk], writes=[sk])
                    rstd_ops(stt, sk, npart)

                    def s2():
                        gt = GTMP[0:npart, ub, :]
                        P.dve(lambda e: e.scalar_tensor_tensor(out=gt, in0=gg, scalar=stt[:, 12:13], in1=GAM[0:npart, 0, :],
                                                               op0=ALU.subtract, op1=ALU.mult),
                              reads=[sk, ("GG", ub), ("GAM",)], writes=[("GTMP", ub)])
                        gv = GV[0:npart, ub, :]
                        if sample:
                            P.dve(lambda e: e.scalar_tensor_tensor(out=GVF[:], in0=gt, scalar=stt[:, 14:15], in1=GAM[0:npart, 1, :],
                                                                   op0=ALU.mult, op1=ALU.add),
                                  reads=[sk, ("GTMP", ub), ("GAM",)], writes=[("GVF",)])
                            P.dma("sp", lambda e: e.dma_start(out=chunkv[l], in_=GVF[:]), reads=[("GVF",)], final=True)
                            P.act(lambda e: e.copy(out=gv, in_=GVF[:]), reads=[("GVF",)], writes=[("GV", ub)])
                        else:
                            P.dve(lambda e: e.scalar_tensor_tensor(out=gv, in0=gt, scalar=stt[:, 14:15], in1=GAM[0:npart, 1, :],
                                                                   op0=ALU.mult, op1=ALU.add),
                                  reads=[sk, ("GTMP", ub), ("GAM",)], writes=[("GV", ub)])
                    gv = GV[0:npart, ub, :]
                    return s2, (lambda: sgu_part2(t, npart, gv, ugv, ub, dst, dst_keys, wsg, bsp, wsg_key))

                def sgu_part2(t, npart, gv, ugv, ub, dst, dst_keys, wsg, bsp, wsg_key):
                    MBk = 2 + (t % 2)
                    mk = [("ps", MBk)]

                    def fM(e):
                        ins = None
                        for sg in range(4):
                            ins = e.matmul(bank(MBk, sg * 128, npart), lhsT=gv[:, (sg // 2) * 128:(sg // 2) * 128 + 128],
                                           rhs=wsg[:, sg, :], start=True, stop=True)
                        return ins
                    P.pe(fM, reads=[("GV", ub), wsg_key], writes=mk)
                    for sg in range(4):
                        Mps = bank(MBk, sg * 128, npart)
                        rows = slice((sg % 2) * 64, (sg % 2) * 64 + 64)
                        tb = sg % 2
                        tmp = STMP[rows, tb, 0:npart]
                        P.dve(lambda e, Mps=Mps, rows=rows, sg=sg, tmp=tmp: e.tensor_tensor(out=tmp, in0=Mps[rows, :],
                                                                                         in1=bsp[rows, sg, :], op=ALU.add),
                              reads=mk + [("BSP",), ("BSPS",)], writes=[("STMP", tb)])
                        P.dve(lambda e, rows=rows, sg=sg, tmp=tmp: e.tensor_tensor(out=dst(rows, sg // 2), in0=tmp,
                                                                                  in1=ugv[rows, sg // 2, :], op=ALU.mult),
                              reads=[("STMP", tb), ("UG", ub)], writes=dst_keys)

                st2, st3 = {}, {}
                for t in range(NT + 1 + 2):
                    if t < NT:
                        st2[t], st3[t] = sgu_tile(t, 128, lambda k, t=t: XT[:, k, t * 128:(t + 1) * 128], [("XT", t)],
                                                  lambda rows, uc, t=t: SGUT[rows, uc, t * 128:(t + 1) * 128], [("ATT", 6, t)],
                                                  SGW, BSP, ("SGW",))
                    elif t == NT:
                        st2[t], st3[t] = sgu_tile(NT, NS, lambda k: XST[:, k, :], [("XST",)],
                                                  lambda rows, uc: ATTS[rows, 6 + uc, :], [("ATTS",)], SGWS, BSPS, ("SGWS",),
                                                  sample=True)
                    if (t - 1) in st2:
                        st2.pop(t - 1)()
                    if (t - 2) in st3:
                        st3.pop(t - 2)()
            items.append((load_sgu, comp_sgu))

            def load_mix():
                ss = take_slots(2)
                for i, s in enumerate(ss):
                    P.dma("pool", lambda e, i=i, s=s: e.dma_start(out=wview(s, 512), in_=wsrc(w_mix[l], i * 512, (i + 1) * 512)),
                          writes=[("WR", s)])
                return ss

            def tokmajor_ln_phase(ss, lhs_fn, lhs_keys_fn, lhs_s_fn, lhs_s_keys, which, first_phase, final_ln):
                Ws = [wview(s, 512) for s in ss]
                load_ln_tables(l, which)
                tr["banks"] = [6, 7]
                pend = []
                ctx = {}
                for t in range(NT + 1 + 2):
                    if t <= NT:
                        sample = (t == NT)
                        npart = NS if sample else 128
                        pb = (t % 3) * 2
                        if first_phase and not sample:
                            src = xp if l == 0 else xscr
                            rk = [] if l == 0 else [("xscr", t)]
                            P.dma("sp", lambda e, t=t, src=src: e.dma_start(out=X[:, t, :], in_=src[t * 128:(t + 1) * 128, :]),
                                  reads=rk, writes=[("X", t)])
                        for hf in range(2):
                            out = bank(pb + hf)[0:npart, :]
                            if sample:
                                pairs = [(lhs_s_fn(k), Ws[hf][:, k, :]) for k in range(8)]
                                rd = list(lhs_s_keys)
                            else:
                                pairs = [(lhs_fn(k, t), Ws[hf][:, k, :]) for k in range(8)]
                                rd = list(lhs_keys_fn(t))
                            mm_group(out, pairs, reads=rd + [("WR", ss[hf])], writes=[("ps", pb + hf)])
                        ps2 = PSA[0:npart, pb * 512: pb * 512 + 1024]
                        pk2 = [("ps", pb), ("ps", pb + 1)]
                        xv = XS[:] if sample else X[:, t, :]
                        xk = [("XS",)] if sample else [("X", t)]
                        stt = STAT[0:npart, t % 4, :]
                        sk = ("STAT", t % 4)
                        ctx[t] = (xv, npart, xk, stt, sk, sample)
                        ln_s1(xv, ps2, pk2, npart, xk, stt, sk)
                    if 1 <= t <= NT + 1:
                        xv, npart, xk, stt, sk, sample = ctx[t - 1]
                        ln_s2(xv, npart, xk, stt, sk)
                    if 2 <= t <= NT + 2:
                        t2 = t - 2
                        xv, npart, xk, stt, sk, sample = ctx.pop(t2)
                        ln_s3(xv, npart, xk, stt, sk)
                        if sample:
                            pend.append(ln_stage_sample(last_layer, final_ln))
                        else:
                            sv = lhs_fn(slice(0, 8), t2)
                            pend.append(ln_stage_tile(t2, last_layer, final_ln, (sv, list(lhs_keys_fn(t2)))))
                    if len(pend) > 2:
                        ln_transposes(pend.pop(0))
                while pend:
                    ln_transposes(pend.pop(0))

            def comp_mix(ss):
                P.fence("dve", lambda e: e.memset(DUMMY[:, 0:1], 0.0), AB_PREF, ("X",))
                P.fence("dve", lambda e: e.memset(DUMMY[:, 1:2], 0.0), ("ATT",), ("ATTC",))
                tokmajor_ln_phase(ss, lambda k, t: ATTALL[:, k, t * 128:(t + 1) * 128],
                                  lambda t: [("ATTC", t)],
                                  lambda k: ATTS[:, k, :], [("ATTS",)], 0, True, False)
            items.append((load_mix, comp_mix))

            for ch in range(4):
                def load_m(ch=ch):
                    s = take_slots(1)[0]
                    P.dma("pool", lambda e: e.dma_start(out=wview(s, 512), in_=wsrc(w_xkv[l], ch * 512, (ch + 1) * 512)),
                          writes=[("WR", s)])
                    return s

                def comp_m(s, ch=ch):
                    W = wview(s, 512)
                    pj_banks["b"] = [2, 3]
                    if ch == 0:
                        P.fence("dve", lambda e: e.memset(DUMMY[:, 5:6], 0.0), S_3, S_M + ("MEMKT", "MEMV"))
                        for mt_ in range(2):
                            P.dma("pool", lambda e, mt_=mt_: e.dma_start(out=XB[:, mt_, :], in_=memp[mt_ * 128:(mt_ + 1) * 128, :]),
                                  writes=[("XB", mt_)])
                            transpose_to([XB[:, mt_, k * 128:(k + 1) * 128] for k in range(8)],
                                         [(MEMT[:, :, mt_ * 128:(mt_ + 1) * 128], 0, 8)], 128,
                                         reads=[("XB", mt_)], writes=[("MEMT",)])
                    for mt in range(2):
                        bk = mt
                        out = bank(bk)
                        mm_group(out, [(MEMT[:, k, mt * 128:(mt + 1) * 128], W[:, k, :]) for k in range(8)],
                                 reads=[("MEMT",), ("WR", s)], writes=psk(bk, 0, 512))
                        mb = (ch * 2 + mt) % 2
                        P.act(lambda e, out=out, mb=mb: e.copy(out=MKF[:, mb, :], in_=out), reads=psk(bk, 0, 512),
                              writes=[("MKF", mb)])
                        if ch >= 2:
                            P.dve(lambda e, out=out, mt=mt: e.tensor_copy(out=MEMV[:, mt, (ch - 2) * 512:(ch - 1) * 512], in_=out),
                                  reads=psk(bk, 0, 512), writes=[("MEMV",)])
                        P.dma("sp", lambda e, mb=mb, mt=mt: e.dma_start(
                            out=memkvp[l, mt * 128:(mt + 1) * 128, ch * 512:(ch + 1) * 512], in_=MKF[:, mb, :]),
                            reads=[("MKF", mb)], final=True)
                    if ch < 2:
                        for sub in range(4):
                            ps = pj_slot()
                            bk, c0 = ps, 0
                            out = bank(bk, c0, 256)
                            mm_group(out, [(W[:, k, sub * 128:(sub + 1) * 128], MEMT[:, k, :]) for k in range(8)],
                                     reads=[("MEMT",), ("WR", s)], writes=psk(bk, c0, 256))
                            P.act(lambda e, out=out, sub=sub: e.copy(out=MEMKT[:, ch * 4 + sub, :], in_=out),
                                  reads=psk(bk, c0, 256), writes=[("MEMKT",)])
                items.append((load_m, comp_m))

            for wc in range(2):
                def load_xq(wc=wc):
                    s = take_slots(1)[0]
                    P.dma("pool", lambda e: e.dma_start(out=wview(s, 512), in_=wsrc(w_xq[l], wc * 512, (wc + 1) * 512)),
                          writes=[("WR", s)])
                    return s

                def comp_xq(s, wc=wc):
                    W = wview(s, 512)
                    if wc == 0:
                        P.fence("dve", lambda e: e.memset(DUMMY[:, 1:2], 0.0), ("ATT", "ATTC"), ("QXT",))
                    n = 0
                    for tg in range(4):
                        for sub in range(4):
                            bk = 4 + (n % 2)
                            n += 1
                            out = bank(bk)
                            pk = psk(bk, 0, 512)
                            mm_group(out, [(W[:, k, sub * 128:(sub + 1) * 128], XT[:, k, tg * 512:(tg + 1) * 512]) for k in range(8)],
                                     reads=[("XT", t_) for t_ in range(tg * 4, tg * 4 + 4)] + [("WR", s)], writes=pk)
                            dst = QXT[:, wc * 4 + sub, tg * 512:(tg + 1) * 512]
                            wk = [("QXT", wc * 4 + sub, t_) for t_ in range(tg * 4, tg * 4 + 4)]
                            if n % 2:
                                P.act(lambda e, dst=dst, out=out: e.copy(out=dst, in_=out), reads=pk, writes=wk)
                            else:
                                P.dve(lambda e, dst=dst, out=out: e.tensor_copy(out=dst, in_=out), reads=pk, writes=wk)
                    out = bank(4, 0, 64)
                    pk = psk(4, 0, 64)

                    def fs(e):
                        ins = None
                        for sub in range(4):
                            for k in range(8):
                                ins = e.matmul(out[:, sub * NS:(sub + 1) * NS], lhsT=W[:, k, sub * 128:(sub + 1) * 128],
                                               rhs=XST[:, k, :], start=(k == 0), stop=(k == 7))
                        return ins
                    P.pe(fs, reads=[("XST",), ("WR", s)], writes=pk)
                    P.act(lambda e: e.copy(out=QXST[:, wc * 4:(wc + 1) * 4, :], in_=out.rearrange("p (c t) -> p c t", t=NS)),
                          reads=pk, writes=[("QXST", wc)])
                items.append((load_xq, comp_xq))

            def load_xo():
                ss = take_slots(2)
                for i, s in enumerate(ss):
                    P.dma("pool", lambda e, i=i, s=s: e.dma_start(out=wview(s, 512), in_=wsrc(w_xo[l], i * 512, (i + 1) * 512)),
                          writes=[("WR", s)])
                return ss

            def comp_xo(ss):
                P.fence("dve", lambda e: e.memset(DUMMY[:, 6:7], 0.0), S_M, S_X)
                xitems = [(tg, h) for tg in range(4) for h in range(4)]

                def xa(n):
                    tg, h = xitems[n]
                    tk = slice(tg * 512, (tg + 1) * 512)
                    sb_ = 0 if n % 2 == 0 else 5
                    S2 = PSA[:, sb_ * 512: sb_ * 512 + 1024]
                    kS = [("ps", sb_), ("ps", sb_ + 1)]

                    def fS(e):
                        ins = None
                        for mt in range(2):
                            for dc in range(2):
                                ins = e.matmul(bank(sb_ + mt), lhsT=MEMKT[:, 2 * h + dc, mt * 128:(mt + 1) * 128],
                                               rhs=QXT[:, 2 * h + dc, tk], start=(dc == 0), stop=(dc == 1))
                        return ins
                    P.pe(fS, reads=[("MEMKT",)] + [("QXT", 2 * h + dc_, t_) for dc_ in range(2) for t_ in range(tg * 4, tg * 4 + 4)],
                         writes=kS)
                    pb = n % 2
                    PmA = PM[:, pb, :]
                    P.act(lambda e: e.activation(out=PmA, in_=S2, func=AF.Exp, scale=1.0 / 16.0), reads=kS, writes=[("PM", pb)])

                def xb(n):
                    tg, h = xitems[n]
                    tk = slice(tg * 512, (tg + 1) * 512)
                    pb = n % 2
                    rb = n % 2
                    PmA = PM[:, pb, :]
                    Dps = bank(2)
                    kD = [("ps", 2)]

                    def fD(e):
                        e.matmul(Dps, lhsT=ONES, rhs=PmA[:, 0:512], start=True, stop=False)
                        return e.matmul(Dps, lhsT=ONES, rhs=PmA[:, 512:1024], start=False, stop=True)
                    P.pe(fD, reads=[("PM", pb), ("CB",)], writes=kD)
                    P.act(lambda e: e.activation(out=RDEN[:, rb, :], in_=Dps, func=AF.Ln), reads=kD, writes=[("RDEN", rb)])
                    P.act(lambda e: e.activation(out=RDEN[:, rb, :], in_=RDEN[:, rb, :], func=AF.Exp, scale=-1.0),
                          reads=[("RDEN", rb)], writes=[("RDEN", rb)])
                    for dc in range(2):
                        Ops = bank(3 + dc)
                        kO = [("ps", 3 + dc)]

                        def fO(e, dc=dc, Ops=Ops):
                            e.matmul(Ops, lhsT=MEMV[:, 0, (2 * h + dc) * 128:(2 * h + dc + 1) * 128], rhs=PmA[:, 0:512],
                                     start=True, stop=False)
                            return e.matmul(Ops, lhsT=MEMV[:, 1, (2 * h + dc) * 128:(2 * h + dc + 1) * 128],
                                            rhs=PmA[:, 512:1024], start=False, stop=True)
                        P.pe(fO, reads=[("PM", pb), ("MEMV",)], writes=kO)
                        P.dve(lambda e, dc=dc, Ops=Ops: e.tensor_tensor(out=QXT[:, 2 * h + dc, tk], in0=Ops, in1=RDEN[:, rb, :],
                                                                        op=ALU.mult),
                              reads=kO + [("RDEN", rb)], writes=[("QXT", 2 * h + dc, t_) for t_ in range(tg * 4, tg * 4 + 4)])

                xa(0)
                for n in range(len(xitems)):
                    if n + 1 < len(xitems):
                        xa(n + 1)
                    xb(n)
                P.fence("dve", lambda e: e.memset(DUMMY[:, 0:1], 0.0), ("PM",), ("CMK",))
                for b in range(4):
                    P.dma("pool", lambda e, b=b: e.dma_start(out=CMK[:], in_=cmem[l, b].rearrange("(m p) c -> p m c", p=128)[:, :, 0:1024]),
                          writes=[("CMK",)])
                    P.dma("pool", lambda e, b=b: e.dma_start(out=SCV[:], in_=cmem[l, b].rearrange("(m p) c -> p m c", p=128)[:, :, 1024:2048]),
                          writes=[("SCV",)])
                    for mt in range(2):
                        transpose_to([CMK[:, mt, ch * 128:(ch + 1) * 128] for ch in range(8)],
                                     [(MKTS[:, :, mt * 128:(mt + 1) * 128], 0, 8)], 128,
                                     reads=[("CMK",)], writes=[("MKTS",)])
                    Sx = bank(5, 0, 32)
                    Dx = bank(5, 128, 16)
                    Ox = bank(5, 256, 32)
                    kx = [("ps", 5), ("ps", 5), ("ps", 5)]

                    def fS(e, b=b):
                        ins = None
                        for mt in range(2):
                            for h in range(4):
                                for dc in range(2):
                                    ins = e.matmul(Sx[:, mt * 16 + h * 4: mt * 16 + h * 4 + 4],
                                                   lhsT=MKTS[:, 2 * h + dc, mt * 128:(mt + 1) * 128],
                                                   rhs=QXST[:, 2 * h + dc, b * 4:(b + 1) * 4], start=(dc == 0), stop=(dc == 1))
                        return ins
                    P.pe(fS, reads=[("MKTS",), ("QXST", 0), ("QXST", 1)], writes=[kx[0]])
                    P.act(lambda e: e.activation(out=SPM[:].rearrange("p a c -> p (a c)"), in_=Sx, func=AF.Exp, scale=1.0 / 16.0),
                          reads=[kx[0]], writes=[("SPM",)])

                    def fO(e, b=b, SCV=SCV):
                        e.matmul(Dx, lhsT=ONES, rhs=SPM[:, 0, :], start=True, stop=False)
                        ins = e.matmul(Dx, lhsT=ONES, rhs=SPM[:, 1, :], start=False, stop=True)
                        for h in range(4):
                            for dc in range(2):
                                ch = 2 * h + dc
                                for mt in range(2):
                                    ins = e.matmul(Ox[:, ch * 4:(ch + 1) * 4], lhsT=SCV[:, mt, ch * 128:(ch + 1) * 128],
                                                   rhs=SPM[:, mt, h * 4:(h + 1) * 4], start=(mt == 0), stop=(mt == 1))
                        return ins
                    P.pe(fO, reads=[("SPM",), ("SCV",), ("CB",)], writes=[kx[1], kx[2]])
                    P.dve(lambda e: e.reciprocal(out=SRD2[:], in_=Dx), reads=[kx[1]], writes=[("SRD2",)])
                    P.dve(lambda e, b=b: e.tensor_tensor(
                        out=QXST[:, :, b * 4:(b + 1) * 4].rearrange("p (h dc) i -> p h dc i", dc=2),
                        in0=Ox.rearrange("p (h dc i) -> p h dc i", dc=2, i=4),
                        in1=SRD2[:].rearrange("p (h i) -> p h i", i=4).unsqueeze(2).broadcast_to([128, 4, 2, 4]), op=ALU.mult),
                        reads=[kx[2], ("SRD2",)], writes=[("QXST", 0), ("QXST", 1)])
                tokmajor_ln_phase(ss, lambda k, t: QXT[:, k, t * 128:(t + 1) * 128],
                                  lambda t: [("QXT", cc, t) for cc in range(8)],
                                  lambda k: QXST[:, k, :], [("QXST", 0), ("QXST", 1)], 1, False, False)
            items.append((load_xo, comp_xo))

            for tg in range(4):
                for hc in range(8):
                    def load_up(hc=hc):
                        s = take_slots(1)[0]
                        P.dma("pool", lambda e: e.dma_start(out=wview(s, 512), in_=wsrc(w_up[l], hc * 512, (hc + 1) * 512)),
                              writes=[("WR", s)])
                        return s

                    def comp_up(s, tg=tg, hc=hc):
                        W = wview(s, 512)
                        tr["banks"] = [6, 7]
                        if hc >= 2 and mlp_pend:
                            ln_transposes(mlp_pend.pop(0))
                        if tg == 0 and hc == 0:
                            P.fence("dve", lambda e: e.memset(DUMMY[:, 2:3], 0.0), ("QXT",), ("HT",))
                            P.fence("dve", lambda e: e.memset(DUMMY[:, 7:8], 0.0), S_X, S_3)
                        for sub in range(4):
                            n = hc * 4 + sub
                            bk = 4 + (n % 2)
                            out = bank(bk)
                            pk = psk(bk, 0, 512)
                            mm_group(out, [(W[:, k, sub * 128:(sub + 1) * 128], XT[:, k, tg * 512:(tg + 1) * 512]) for k in range(8)],
                                     reads=[("XT", t_) for t_ in range(tg * 4, tg * 4 + 4)] + [("WR", s)], writes=pk)
                            rb = n % 4
                            P.act(lambda e, out=out, rb=rb: e.activation(out=RELU[:, rb, :], in_=out, func=AF.Relu), reads=pk,
                                  writes=[("RELU", rb)])
                            eng = P.dve
                            eng(lambda e, n=n, rb=rb: e.tensor_tensor(out=HT[:, n, :], in0=RELU[:, rb, :], in1=RELU[:, rb, :],
                                                                      op=ALU.mult),
                                reads=[("RELU", rb)], writes=[("HT", n)])
                            if sub == 1 and mlp_ln_pend:
                                mlp_ln_pend.pop(0)()
                        if tg == 0:
                            out = bank(5, 0, 64)
                            pk = psk(5, 0, 64)

                            def fs(e):
                                ins = None
                                for sub in range(4):
                                    for k in range(8):
                                        ins = e.matmul(out[:, sub * NS:(sub + 1) * NS], lhsT=W[:, k, sub * 128:(sub + 1) * 128],
                                                       rhs=XST[:, k, :], start=(k == 0), stop=(k == 7))
                                return ins
                            P.pe(fs, reads=[("XST",), ("WR", s)], writes=pk)
                            P.act(lambda e: e.activation(out=RELU[:, 0, 0:64], in_=out, func=AF.Relu), reads=pk,
                                  writes=[("RELU", 0)])
                            P.dve(lambda e: e.tensor_tensor(out=HTS[:, hc * 4:(hc + 1) * 4, :],
                                                            in0=RELU[:, 0, 0:64].rearrange("p (c t) -> p c t", t=NS),
                                                            in1=RELU[:, 0, 0:64].rearrange("p (c t) -> p c t", t=NS), op=ALU.mult),
                                  reads=[("RELU", 0)], writes=[("HTS",)])
                    items.append((load_up, comp_up))
                if tg == 0:
                    items.append((None, lambda s_: load_ln_tables(l, 2)))
                for hf in range(2):
                    for k4 in range(8):
                        def load_dn(hf=hf, k4=k4):
                            s = take_slots(1)[0]
                            P.dma("pool", lambda e: e.dma_start(out=wview(s, 512, 4), in_=wsrc(w_down[l], hf * 512, (hf + 1) * 512, k4 * 4, 4)),
                                  writes=[("WR", s)])
                            return s

                        def comp_dn(s, tg=tg, hf=hf, k4=k4):
                            W = wview(s, 512, 4)
                            with_s = (tg == 0)

                            def f(e):
                                ins = None
                                for kk in range(4):
                                    kc = k4 * 4 + kk
                                    for tl in range(4):
                                        ins = e.matmul(bank(tl), lhsT=HT[:, kc, tl * 128:(tl + 1) * 128], rhs=W[:, kk, :],
                                                       start=(kc == 0), stop=(kc == 31))
                                    if with_s:
                                        ins = e.matmul(bank(4)[0:NS, :], lhsT=HTS[:, kc, :], rhs=W[:, kk, :],
                                                       start=(kc == 0), stop=(kc == 31))
                                return ins
                            wr = [k for tl in range(4) for k in psk(tl, 0, 512)] + (psk(4, 0, 512) if with_s else [])
                            P.pe(f, reads=[("HT", k4 * 4 + kk) for kk in range(4)] + [("WR", s)] + ([("HTS",)] if with_s else []),
                                 writes=wr)
                            if k4 == 7:
                                tls = list(range(4 + (1 if with_s else 0)))
                                cx = {}
                                for tl in tls:
                                    sample = (tl == 4)
                                    t = tg * 4 + tl
                                    npart = NS if sample else 128
                                    xv = XS[:] if sample else X[:, t, :]
                                    xk = [("XS",)] if sample else [("X", t)]
                                    xh = xv[:, hf * 512:(hf + 1) * 512]
                                    psv = bank(tl)[0:npart, :]
                                    stk = ("STATM", tl)
                                    stt = STATM[0:npart, tl, :]
                                    cx[tl] = (xv, npart, xk, stt, stk, sample, t)
                                    P.dve(lambda e, xh=xh, psv=psv: e.scalar_tensor_tensor(out=xh, in0=xh, scalar=ALPHA, in1=psv,
                                                                                           op0=ALU.mult, op1=ALU.add),
                                          reads=psk(tl, 0, 512) + xk, writes=xk)
                                    P.dve(lambda e, xh=xh, stt=stt, hf=hf: e.bn_stats(stt[:, hf * 6:(hf + 1) * 6], xh), reads=xk,
                                          writes=[stk])
                                    if hf == 1:
                                        ln_s1_stats(xv, stt, stk, npart)
                                if hf == 1:
                                    for tl in tls:
                                        def fin(c_=cx[tl], tl=tl):
                                            xv, npart, xk, stt, stk, sample, t = c_
                                            ln_s2(xv, npart, xk, stt, stk)
                                            ln_s3(xv, npart, xk, stt, stk)
                                            if sample:
                                                mlp_pend.append(ln_stage_sample(last_layer, True))
                                            else:
                                                mlp_pend.append(ln_stage_tile(t, last_layer, True, (XBM[:, tl, :, :], [("XBM", tl)])))
                                        mlp_ln_pend.append(fin)
                                    if tg == 3:
                                        while mlp_ln_pend:
                                            mlp_ln_pend.pop(0)()
                        items.append((load_dn, comp_dn))
            if not last_layer:
                def comp_end(s_):
                    while mlp_pend:
                        ln_transposes(mlp_pend.pop(0))
                    P.fence("dve", lambda e: e.memset(DUMMY[:, 3:4], 0.0), ("X",), AB_PREF)
                    P.fence("dve", lambda e: e.memset(DUMMY[:, 4:5], 0.0), ("HT",), ("ATT",))
                items.append((None, comp_end))


        for l in range(DEPTH):
            layer(l)

        slots = [None] * len(items)
        LOOK = 2
        for i0 in range(min(LOOK, len(items))):
            if items[i0][0] is not None:
                slots[i0] = items[i0][0]()
        for i, (ld, comp) in enumerate(items):
            if stop_after is not None and i >= stop_after:
                break
            if i + LOOK < len(items) and items[i + LOOK][0] is not None:
                slots[i + LOOK] = items[i + LOOK][0]()
            comp(slots[i])
            if stop_after is not None and i + 1 >= stop_after:
                break
        P.emit()
    return nc


def _const_tables():
    cb = np.zeros((128, NCB), np.float32)
    p = np.arange(128)[:, None]
    f = np.arange(128)[None, :]
    cb[:, CB_ID:CB_ID + 128] = (p == f)
    cb[:, CB_MASK:CB_MASK + 128] = (p >= f)
    cb[:, CB_MASK + 128:CB_MASK + 256] = (p <= f)
    cb[:, CB_ONES:CB_ONES + 128] = 1.0
    col = np.arange(32)
    ci = col % 4
    cbb = (col % 16) // 4
    cb[:, CB_MP0:CB_MP0 + 32] = (np.arange(128)[:, None] >= ci[None, :])
    kk = np.arange(16)
    kb_, ki = kk // 4, kk % 4
    same_b = (kb_[:, None] == cbb[None, :])
    cb[0:16, CB_MN0:CB_MN0 + 32] = same_b & (ki[:, None] <= ci[None, :])
    cb[0:16, CB_MN1:CB_MN1 + 32] = same_b & (ki[:, None] == ci[None, :])
    cb[0:16, CB_SGM:CB_SGM + 16] = (kb_[:, None] == kb_[None, :]) & (ki[:, None] <= ki[None, :])

    cf = np.zeros((128, NCF), np.float32)
    half = 8
    inv = (np.float32(500000.0) ** (-(np.arange(half, dtype=np.float32) / np.float32(half)))).astype(np.float32)

    def cs(pos):
        ang = (pos.astype(np.float32)[:, None] * inv[None, :]).astype(np.float32)
        return np.cos(ang.astype(np.float64)).astype(np.float32), np.sin(ang.astype(np.float64)).astype(np.float32)
    for t in range(NT):
        c_, s_ = cs(np.arange(128) + 128 * t)
        cf[:, CF_CS1 + t * 16: CF_CS1 + t * 16 + 8] = c_
        cf[:, CF_CS1 + t * 16 + 8: CF_CS1 + t * 16 + 16] = s_
        cf[:, CF_CS2 + t * 16: CF_CS2 + t * 16 + 8] = -s_
        cf[:, CF_CS2 + t * 16 + 8: CF_CS2 + t * 16 + 16] = c_
    c_, s_ = cs(PAST + (np.arange(16) % 4))
    cf[0:16, CF_SS1:CF_SS1 + 8] = c_
    cf[0:16, CF_SS1 + 8:CF_SS1 + 16] = s_
    cf[0:16, CF_SS2:CF_SS2 + 8] = -s_
    cf[0:16, CF_SS2 + 8:CF_SS2 + 16] = c_
    return cb, cf


_CACHE = {}


def kernel(x_prompt, x_sample, cache_kv_w128, cache_kv_w512, cache_kv_w2048, cache_mem_kv, mem_prompt,
           w_in, sgu_ln_g, sgu_ln_b, w_spatial, b_spatial, w_mix_out, ln1_g, ln1_b,
           w_xq, w_xkv, w_xo, ln2_g, ln2_b, w_up, w_down, ln3_g, ln3_b, _stop_after=None):
    f = lambda a: np.ascontiguousarray(np.asarray(a, dtype=np.float32))
    key = ("nc", _stop_after)
    if key not in _CACHE:
        _CACHE[key] = build_program(_stop_after)
    nc = _CACHE[key]
    cb, cf = _const_tables()
    shared = {
        "w_in": f(w_in), "sgu_ln_g": f(sgu_ln_g), "sgu_ln_b": f(sgu_ln_b), "w_spatial": f(w_spatial),
        "b_spatial": f(b_spatial), "w_mix_out": f(w_mix_out), "ln1_g": f(ln1_g), "ln1_b": f(ln1_b),
        "w_xq": f(w_xq), "w_xkv": f(w_xkv), "w_xo": f(w_xo), "ln2_g": f(ln2_g), "ln2_b": f(ln2_b),
        "w_up": f(w_up), "w_down": f(w_down), "ln3_g": f(ln3_g), "ln3_b": f(ln3_b), "cb": cb, "cf": cf,
    }
    xpr, xsa = f(x_prompt), f(x_sample)
    c128, c512, c2048, cm, mp = f(cache_kv_w128), f(cache_kv_w512), f(cache_kv_w2048), f(cache_mem_kv), f(mem_prompt)
    in_maps = []
    for c in range(NCORES):
        bs = slice(4 * c, 4 * c + 4)
        m = dict(shared)
        m["xp"] = xpr[c]
        m["xs"] = np.ascontiguousarray(xsa[bs].reshape(NS, D))
        m["c128"] = np.ascontiguousarray(c128[:, bs].reshape(DEPTH, 4, 128, 512))
        m["c512"] = np.ascontiguousarray(c512[:, bs].reshape(DEPTH, 4, 512, 512))
        m["c2048"] = np.ascontiguousarray(c2048[:, bs].reshape(DEPTH, 4, 2048, 512))
        m["cmem"] = np.ascontiguousarray(cm[:, bs].reshape(DEPTH, 4, 256, 2048))
        m["memp"] = mp[c]
        in_maps.append(m)
    res = run_bass_kernel_spmd(nc, in_maps, core_ids=list(range(NCORES)))
    R = res.results

    def cat_p(name, shape_tail):
        return np.stack([np.asarray(R[c][name], np.float32).reshape((DEPTH,) + shape_tail) for c in range(NCORES)], axis=1)

    def cat_s(name, shape_tail):
        return np.concatenate([np.asarray(R[c][name], np.float32).reshape((DEPTH, 4, 4) + shape_tail) for c in range(NCORES)], axis=1)

    y_p = np.stack([np.asarray(R[c]["yp"], np.float32) for c in range(NCORES)], axis=0)
    y_s = np.concatenate([np.asarray(R[c]["ys"], np.float32).reshape(4, 4, D) for c in range(NCORES)], axis=0)
    return (y_p, y_s,
            cat_p("kv128p", (128, 2, 4, 64)), cat_p("kv512p", (512, 2, 4, 64)), cat_p("kv2048p", (2048, 2, 4, 64)),
            cat_p("memkvp", (256, 2, 4, 256)),
            cat_s("kv128s", (2, 4, 64)), cat_s("kv512s", (2, 4, 64)), cat_s("kv2048s", (2, 4, 64)),
            cat_s("chunkv", (256,)))
```

```python
import contextlib
import numpy as np
import concourse.bass as bass
import concourse.mybir as mybir
from concourse.bass_utils import run_bass_kernel_spmd

F32 = mybir.dt.float32
BF16 = mybir.dt.bfloat16
AF = mybir.ActivationFunctionType
ALU = mybir.AluOpType

D = 1024
T = 2048
NT = 16
DEPTH = 2
NS = 16
NCORES = 8
IN_COLS = 2816
DFF = 4096
ALPHA = float((2 * DEPTH) ** 0.25)
EPS = 1e-5
PAST = 8192
GROUPS = ((128, 1), (512, 4), (2048, 16))
NSLOT = 4
N_DMA_SEMS = 12
ENGS = ("pe", "act", "dve", "pool", "sp")

CB_ID, CB_MASK, CB_ONES, CB_MP0, CB_MN0, CB_MN1, CB_SGM, NCB = 0, 128, 384, 512, 544, 576, 608, 624
CF_CS1, CF_CS2, CF_SS1, CF_SS2, NCF = 0, 256, 512, 528, 544


class Op:
    __slots__ = ("eng", "fn", "reads", "writes", "dma", "idx", "deps", "sig", "semslot", "semcnt", "n_dma", "raw")


class Prog:
    def __init__(self, nc, same_engine_sync=True):
        self.nc = nc
        self.ops = []
        self.same_engine_sync = same_engine_sync
        self.last_writer = {}
        self.readers = {}
        self.final_ops = []
        self.fences = {}

    def op(self, eng, fn, reads=(), writes=(), dma=False, n_dma=1, final=False):
        o = Op()
        o.eng, o.fn, o.reads, o.writes, o.dma, o.n_dma = eng, fn, tuple(reads), tuple(writes), dma, n_dma
        o.sig = None
        o.semslot = None
        o.semcnt = None
        o.idx = len(self.ops)
        deps = set()
        for k in o.reads:
            w = self.last_writer.get(k)
            if w is not None:
                deps.add(w)
            f = self.fences.get(k[0])
            if f is not None:
                deps.add(f)
            if k[0] == "ps":
                for r in self.readers.get(k, ()):
                    if self.ops[r].eng != eng:
                        deps.add(r)
        for k in o.writes:
            w = self.last_writer.get(k)
            if w is not None:
                deps.add(w)
            for r in self.readers.get(k, ()):
                deps.add(r)
            f = self.fences.get(k[0])
            if f is not None:
                deps.add(f)
        deps.discard(o.idx)
        o.deps = sorted(deps)
        o.raw = set()
        for k in o.reads:
            w = self.last_writer.get(k)
            if w is not None:
                o.raw.add(w)
        for k in o.reads:
            self.readers.setdefault(k, []).append(o.idx)
        for k in o.writes:
            self.last_writer[k] = o.idx
            self.readers[k] = []
        self.ops.append(o)
        if final:
            self.final_ops.append(o.idx)
        return o

    def fence(self, eng, fn, old_prefixes, new_prefixes):
        deps = set()
        for k, w in self.last_writer.items():
            if k[0] in old_prefixes and w is not None:
                deps.add(w)
        for k, rs in self.readers.items():
            if k[0] in old_prefixes:
                deps.update(rs)
        for p in old_prefixes:
            f = self.fences.get(p)
            if f is not None:
                deps.add(f)
        o = Op()
        o.eng, o.fn, o.reads, o.writes, o.dma, o.n_dma = eng, fn, (), (), False, 1
        o.sig = None
        o.semslot = None
        o.semcnt = None
        o.idx = len(self.ops)
        o.deps = sorted(deps)
        o.raw = set(deps)
        self.ops.append(o)
        for p in new_prefixes:
            self.fences[p] = o.idx
        for k in [k for k in self.last_writer if k[0] in old_prefixes]:
            del self.last_writer[k]
        for k in [k for k in self.readers if k[0] in old_prefixes]:
            del self.readers[k]
        return o

    def pe(self, fn, reads=(), writes=()):
        return self.op("pe", fn, reads, writes)

    def act(self, fn, reads=(), writes=()):
        return self.op("act", fn, reads, writes)

    def dve(self, fn, reads=(), writes=()):
        return self.op("dve", fn, reads, writes)

    def pool(self, fn, reads=(), writes=()):
        return self.op("pool", fn, reads, writes)

    def dma(self, q, fn, reads=(), writes=(), final=False, n_dma=1):
        return self.op(q, fn, reads, writes, dma=True, n_dma=n_dma, final=final)

    def emit(self):
        nc = self.nc
        ops = self.ops
        needed = [False] * len(ops)
        for o in ops:
            for d in o.deps:
                p = ops[d]
                if p.dma or p.eng != o.eng or (self.same_engine_sync and p.eng != "pe" and d in o.raw):
                    needed[d] = True
        for i in self.final_ops:
            needed[i] = True
        with contextlib.ExitStack() as st:
            eng_sem = {e: st.enter_context(nc.semaphore("s_" + e)) for e in ENGS}
            dma_sems = {q: [st.enter_context(nc.semaphore("d_%s%d" % (q, i))) for i in range(N_DMA_SEMS)]
                        for q in ("sp", "act", "pool")}
            block = st.enter_context(nc.Block())
            cnt = {e: 0 for e in ENGS}
            dcnt = {q: [0] * N_DMA_SEMS for q in dma_sems}
            drr = {q: 0 for q in dma_sems}
            for o in ops:
                if o.dma:
                    q = o.eng
                    s = drr[q] % N_DMA_SEMS
                    drr[q] += 1
                    o.semslot = s
                    o.semcnt = dcnt[q][s]
                    dcnt[q][s] += 16 * o.n_dma
                    o.sig = (dma_sems[q][s], dcnt[q][s])
                elif needed[o.idx]:
                    cnt[o.eng] += 1
                    o.sig = (eng_sem[o.eng], cnt[o.eng])
            per_eng = {e: [o for o in ops if o.eng == e] for e in ENGS}
            final_ops = [ops[i] for i in self.final_ops]
            same = self.same_engine_sync

            def run(e, engobj):
                waited = {}

                def wait(sem, val):
                    key = id(sem)
                    if waited.get(key, 0) >= val:
                        return
                    waited[key] = val
                    engobj.wait_ge(sem, val)

                for o in per_eng[e]:
                    if o.dma and o.semcnt > 0:
                        wait(dma_sems[e][o.semslot], o.semcnt)
                    for d in o.deps:
                        p = ops[d]
                        if p.sig is None:
                            continue
                        if (not p.dma) and p.eng == e and (e == "pe" or not same or d not in o.raw):
                            continue
                        wait(*p.sig)
                    ins = o.fn(engobj)
                    if o.dma:
                        if isinstance(ins, (list, tuple)):
                            assert len(ins) == o.n_dma
                            for i_ in ins:
                                i_.then_inc(o.sig[0], 16)
                        else:
                            assert o.n_dma == 1
                            ins.then_inc(o.sig[0], 16)
                    elif o.sig is not None:
                        ins.then_inc(o.sig[0], 1)
                if e == "sp":
                    for o in final_ops:
                        wait(*o.sig)

            @block.tensor
            def _(eng):
                run("pe", eng)

            @block.scalar
            def _(eng):
                run("act", eng)

            @block.vector
            def _(eng):
                run("dve", eng)

            @block.gpsimd
            def _(eng):
                run("pool", eng)

            @block.sync
            def _(eng):
                run("sp", eng)


def psk(bank, c0, n):
    return [("ps", bank)]


def build_program(stop_after=None):
    nc = bass.Bass("TRN2", target_bir_lowering=False)

    def din(name, shape):
        return nc.dram_tensor(name, list(shape), F32, kind="ExternalInput").ap()

    def dout(name, shape):
        return nc.dram_tensor(name, list(shape), F32, kind="ExternalOutput").ap()

    xp = din("xp", [T, D])
    xs = din("xs", [NS, D])
    cpast = [din("c128", [DEPTH, 4, 128, 512]), din("c512", [DEPTH, 4, 512, 512]), din("c2048", [DEPTH, 4, 2048, 512])]
    cmem = din("cmem", [DEPTH, 4, 256, 2048])
    memp = din("memp", [256, D])
    w_in = din("w_in", [DEPTH, D, IN_COLS])
    sgu_g = din("sgu_ln_g", [DEPTH, 256])
    sgu_b = din("sgu_ln_b", [DEPTH, 256])
    w_sp = din("w_spatial", [DEPTH, 4, 128, 128])
    b_sp = din("b_spatial", [DEPTH, 4, 128])
    w_mix = din("w_mix_out", [DEPTH, D, D])
    ln_g = [din("ln1_g", [DEPTH, D]), din("ln2_g", [DEPTH, D]), din("ln3_g", [DEPTH, D])]
    ln_b = [din("ln1_b", [DEPTH, D]), din("ln2_b", [DEPTH, D]), din("ln3_b", [DEPTH, D])]
    w_xq = din("w_xq", [DEPTH, D, D])
    w_xkv = din("w_xkv", [DEPTH, D, 2 * D])
    w_xo = din("w_xo", [DEPTH, D, D])
    w_up = din("w_up", [DEPTH, D, DFF])
    w_down = din("w_down", [DEPTH, DFF, D])
    cb_d = din("cb", [128, NCB])
    cf_d = din("cf", [128, NCF])

    yp = dout("yp", [T, D])
    ys = dout("ys", [NS, D])
    kvp = [dout("kv128p", [DEPTH, 128, 512]), dout("kv512p", [DEPTH, 512, 512]), dout("kv2048p", [DEPTH, 2048, 512])]
    memkvp = dout("memkvp", [DEPTH, 256, 2 * D])
    kvs = [dout("kv128s", [DEPTH, NS, 512]), dout("kv512s", [DEPTH, NS, 512]), dout("kv2048s", [DEPTH, NS, 512])]
    chunkv = dout("chunkv", [DEPTH, NS, 256])
    xscr = nc.dram_tensor("xscr", [T, D], F32).ap()

    st = contextlib.ExitStack()

    def sb(name, shape, dt):
        return st.enter_context(nc.sbuf_tensor(name, list(shape), dt))

    with st:
        XT = sb("XT", [128, 8, T], BF16)
        XST = sb("XST", [128, 8, NS], BF16)
        XS = sb("XS", [NS, D], F32)
        Yt = sb("Y", [128, 16384], BF16)
        Zt = sb("Z", [128, 16384], F32)
        St = sb("S", [128, 6144], F32)
        WR = sb("WR", [128, NSLOT, 4096], BF16)
        LNT = sb("LNT", [128, 2, 1024], F32)
        CB = sb("CB", [128, NCB], BF16)
        CF = sb("CF", [128, NCF], F32)
        XB = sb("XB", [128, 2, 1024], BF16)
        STAT = sb("STAT", [128, 4, 16], F32)
        STATG = sb("STATG", [128, 4, 16], F32)
        STATM = sb("STATM", [128, 5, 16], F32)
        HTS = sb("HTS", [128, 32, NS], BF16)
        SDEN = sb("SDEN", [128, 32], F32)
        SO = sb("SO", [128, 3, 32], F32)
        ATTS = sb("ATTS", [128, 8, NS], BF16)
        QXST = sb("QXST", [128, 8, NS], BF16)
        SPM = sb("SPM", [128, 2, 16], BF16)
        SRD2 = sb("SRD2", [128, 16], F32)
        GVF = sb("GVF", [NS, 256], F32)
        DUMMY = sb("DUMMY", [128, 8], F32)
        EPSB = sb("EPSB", [128, 1], F32)
        MHALF = sb("MHALF", [128, 1], F32)
        PSA = st.enter_context(nc.psum_tensor("PSA", [128, 4096], F32))
        Zb = Zt[:].bitcast(BF16)
        Sb = St[:].bitcast(BF16)

        def carve(base_f32, base_b16, state, limit, shape, dt):
            npart = shape[0]
            nel = 1
            for d_ in shape[1:]:
                nel *= d_
            esz = 4 if dt == F32 else 2
            o = state["o"]
            state["o"] = o + ((nel * esz + 31) // 32) * 32
            assert state["o"] <= limit, (state["o"], limit)
            if dt == F32:
                v = base_f32[0:npart, o // 4: o // 4 + nel]
            else:
                v = base_b16[0:npart, o // 2: o // 2 + nel]
            fd = shape[1:]
            if len(fd) == 2:
                v = v.rearrange("p (a b) -> p a b", b=fd[1])
            elif len(fd) == 3:
                v = v.rearrange("p (a b c) -> p a b c", b=fd[1], c=fd[2])
            elif len(fd) == 4:
                v = v.rearrange("p (a b c d) -> p a b c d", b=fd[1], c=fd[2], d=fd[3])
            return v

        zstate = {"o": 7168 * 4}
        AB_NAMES = []

        def zal(name, shape, dt):
            AB_NAMES.append(name)
            return carve(Zt, Zb, zstate, 65536, shape, dt)

        hole1 = {"o": 4096}
        hole2 = {"o": 14336}

        def zal1(name, shape, dt):
            AB_NAMES.append(name)
            return carve(Zt, Zb, hole1, 8192, shape, dt)

        def zal2(name, shape, dt):
            AB_NAMES.append(name)
            return carve(Zt, Zb, hole2, 20480, shape, dt)

        PKV = zal("PKV", [128, 4, 4, 2, 128], BF16)
        PKT = zal("PKT", [128, 4, 4, 128], BF16)
        QKB = zal1("QKB", [128, 4, 256], BF16)
        RT = zal1("RT", [128, 4, 2, 64], F32)
        KF = zal2("KF", [128, 2, 128], F32)
        VF = zal2("VF", [128, 2, 128], F32)
        PT = zal("PT", [128, 3, 512], BF16)
        SGW = zal("SGW", [128, 4, 128], BF16)
        SGWS = zal("SGWS", [NS, 4, NS], BF16)
        SGWSF = zal("SGWSF", [NS, 4, NS], F32)
        SGL = zal2("SGL", [128, 4, 128], BF16)
        GAM = zal("GAM", [128, 2, 256], F32)
        BSP = zal("BSP", [128, 4, 128], F32)
        BSPS = zal("BSPS", [128, 4, NS], F32)
        UG = zal("UG", [128, 3, 256], F32)
        GG = zal("GG", [128, 3, 256], F32)
        GTMP = zal("GTMP", [128, 3, 256], F32)
        GV = zal("GV", [128, 3, 256], BF16)
        STMP = zal2("STMP", [128, 2, 128], F32)
        SQKB = zal("SQKB", [NS, 256], BF16)
        SRT = zal("SRT", [NS, 2, 64], F32)
        SKF = zal("SKF", [NS, 128], F32)
        SVF = zal("SVF", [NS, 128], F32)
        SVB = zal("SVB", [NS, 128], BF16)
        SQT = zal("SQT", [128, NS], BF16)
        SKT = zal("SKT", [128, NS], BF16)
        SPP = zal("SPP", [128, 32], BF16)
        SPN = zal("SPN", [NS, 32], BF16)

        def sal(slot, shape, dt):
            return carve(St, Sb, {"o": slot * 4096}, 24576, shape, dt)

        MEMT = sal(0, [128, 8, 256], BF16)
        PM = sal(0, [128, 2, 1024], BF16)
        CMK = sal(0, [128, 2, 1024], BF16)
        RELU = sal(0, [128, 4, 512], F32)
        MKF = sal(1, [128, 2, 512], F32)
        RDEN = sal(1, [128, 2, 512], F32)
        MEMKT = sal(2, [128, 8, 256], BF16)
        MEMV = sal(3, [128, 2, 1024], BF16)
        SCV = sal(4, [128, 2, 1024], BF16)
        MKTS = sal(5, [128, 8, 256], BF16)
        XBM = sal(2, [128, 4, 8, 128], BF16)
        S_M, S_X, S_3 = ("MEMT", "MKF"), ("PM", "CMK", "RDEN", "MEMKT", "MEMV"), ("RELU", "XBM")
        mlp_pend = []
        mlp_ln_pend = []
        XSB = XB[0:NS, 0, :]

        PSB = PSA[:].bitcast(BF16)
        Zb = Zt[:].bitcast(BF16)
        X = Zt[:].rearrange("p (t d) -> p t d", d=D)
        KT = [Zb[:, i * 2048:(i + 1) * 2048] for i in range(2)]
        VAUG = [Zb[:, 4096 + i * 3072: 4096 + i * 3072 + 2048].rearrange("p (c v) -> p c v", v=128) for i in range(2)]
        DEN = Zt[:, 5120:5120 + 2048]
        ATT = Yt[:, 0:12288].rearrange("p (c t) -> p c t", t=T)
        SGUT = Yt[:, 12288:16384].rearrange("p (c t) -> p c t", t=T)
        ATTALL = Yt[:].rearrange("p (c t) -> p c t", t=T)
        QXT = Yt[:].rearrange("p (c t) -> p c t", t=T)
        HT = Yt[:].rearrange("p (c t) -> p c t", t=512)
        IDENT = CB[:, CB_ID:CB_ID + 128]
        MASK = CB[:, CB_MASK:CB_MASK + 256]
        ONES = CB[:, CB_ONES:CB_ONES + 128]
        AB_PREF = tuple(["KT", "VAUG", "DEN"] + AB_NAMES)
        Y_AB = ("ATT",)

        P = Prog(nc)

        def bank(b, c0=0, n=512):
            return PSA[:, b * 512 + c0: b * 512 + c0 + n]

        def bankb(b, c0=0, n=1024):
            return PSB[:, b * 1024 + c0: b * 1024 + c0 + n]

        P.dma("pool", lambda e: e.dma_start(out=CB[:], in_=cb_d), writes=[("CB",)])
        P.dma("sp", lambda e: e.dma_start(out=CF[:], in_=cf_d), writes=[("CF",)])
        P.dma("sp", lambda e: e.dma_start(out=XS[:], in_=xs), writes=[("XS",)])
        P.pool(lambda e: e.memset(EPSB[:], EPS), writes=[("EPSB",)])
        P.pool(lambda e: e.memset(MHALF[:], -0.5), writes=[("EPSB",)])

        ring = {"n": 0}

        def take_slots(k):
            s = [(ring["n"] + i) % NSLOT for i in range(k)]
            ring["n"] += k
            return s

        def wview(slot, ncols, nk=8):
            return WR[:, slot, 0:nk * ncols].rearrange("p (k n) -> p k n", n=ncols)

        def wsrc(w2d, c0, c1, k0=0, nk=8):
            return w2d[k0 * 128:(k0 + nk) * 128, c0:c1].rearrange("(k p) n -> p k n", p=128)

        tr = {"n": 0, "banks": [6, 7]}

        def transpose_to(srcs, dsts, nrows, reads, writes):
            assert len(srcs) <= 8
            B = tr["banks"][tr["n"] % len(tr["banks"])]
            tr["n"] += 1
            idn = IDENT[0:nrows, 0:nrows]

            def f(e, srcs=srcs, B=B, idn=idn):
                ins = None
                for i, a_ in enumerate(srcs):
                    ins = e.transpose(out=bankb(B, i * nrows, nrows), in_=a_, identity=idn)
                return ins
            P.pe(f, reads=list(reads) + [("CB",)], writes=[("ps", B)])
            for (d, i0, n) in dsts:
                src = bankb(B, i0 * nrows, n * nrows)
                if n > 1:
                    src = src.rearrange("p (a r) -> p a r", r=nrows)
                P.act(lambda e, d=d, src=src: e.copy(out=d, in_=src), reads=[("ps", B)], writes=writes)

        def mm_group(out, pairs, reads, writes):
            n = len(pairs)

            def f(e, out=out, pairs=pairs, n=n):
                ins = None
                for i, (l, r) in enumerate(pairs):
                    ins = e.matmul(out, lhsT=l, rhs=r, start=(i == 0), stop=(i == n - 1))
                return ins
            P.pe(f, reads=reads, writes=writes)

        def make_xt_from_dram(src, t, buf):
            stg = XBM[:, t % 4, :, :]
            P.dma("pool", lambda e: e.dma_start(out=stg, in_=src[t * 128:(t + 1) * 128, :].rearrange("p (k c) -> p k c", c=128)),
                  writes=[("XBM", t % 4)])
            transpose_to([stg[:, k, :] for k in range(8)],
                         [(XT[:, :, t * 128:(t + 1) * 128], 0, 8)], 128,
                         reads=[("XBM", t % 4)], writes=[("XT", t)])

        import os
        KD = os.environ.get("KDBG", "xs")
        if "x" in KD:
            for t in range(int(os.environ.get("KNT", NT))):
                make_xt_from_dram(xp, t, t % 2)
        if "s" in KD:
            P.dma("pool", lambda e: e.dma_start(out=XSB[:], in_=xs), writes=[("XB", 0)])
            transpose_to([XSB[:, k * 128:(k + 1) * 128] for k in range(8)], [(XST[:], 0, 8)], NS,
                         reads=[("XB", 0)], writes=[("XST",)])

        ln_rr = {"n": 0}

        def rstd_ops(stt, sk, npart):
            P.pool(lambda e: e.tensor_tensor(out=stt[:, 15:16], in0=stt[:, 13:14], in1=EPSB[0:npart, :], op=ALU.add),
                   reads=[sk, ("EPSB",)], writes=[sk])
            P.pool(lambda e: e.tensor_tensor(out=stt[:, 14:15], in0=stt[:, 15:16], in1=MHALF[0:npart, :], op=ALU.pow),
                   reads=[sk, ("EPSB",)], writes=[sk])

        def ln_s1_stats(xv, stt, sk, npart):
            P.dve(lambda e: e.bn_aggr(stt[:, 12:14], stt[:, 0:12]), reads=[sk], writes=[sk])
            rstd_ops(stt, sk, npart)

        def ln_s1(xv, ps2, ps_keys, npart, xkeys, stt, sk):
            P.dve(lambda e: e.scalar_tensor_tensor(out=xv, in0=xv, scalar=ALPHA, in1=ps2, op0=ALU.mult, op1=ALU.add),
                  reads=list(ps_keys) + list(xkeys), writes=xkeys)
            for hf in range(2):
                P.dve(lambda e, hf=hf: e.bn_stats(stt[:, hf * 6:(hf + 1) * 6], xv[:, hf * 512:(hf + 1) * 512]),
                      reads=xkeys, writes=[sk])
            ln_s1_stats(xv, stt, sk, npart)

        def ln_s2(xv, npart, xkeys, stt, sk):
            P.dve(lambda e: e.scalar_tensor_tensor(out=xv, in0=xv, scalar=stt[:, 12:13], in1=LNT[0:npart, 0, :],
                                                   op0=ALU.subtract, op1=ALU.mult),
                  reads=[sk, ("LNT",)] + list(xkeys), writes=xkeys)

        def ln_s3(xv, npart, xkeys, stt, sk):
            P.dve(lambda e: e.scalar_tensor_tensor(out=xv, in0=xv, scalar=stt[:, 14:15], in1=LNT[0:npart, 1, :],
                                                   op0=ALU.mult, op1=ALU.add),
                  reads=[sk, ("LNT",)] + list(xkeys), writes=xkeys)

        def load_ln_tables(l, which):
            P.dma("sp", lambda e: e.dma_start(out=LNT[:, 0, :], in_=ln_g[which][l].partition_broadcast(128)),
                  writes=[("LNT",)])
            P.dma("sp", lambda e: e.dma_start(out=LNT[:, 1, :], in_=ln_b[which][l].partition_broadcast(128)),
                  writes=[("LNT",)])

        def ln_stage_tile(t, last_layer, final_ln, stage):
            xv = X[:, t, :]
            if final_ln and last_layer:
                P.dma("sp", lambda e: e.dma_start(out=yp[t * 128:(t + 1) * 128, :], in_=xv), reads=[("X", t)], final=True)
                return None
            sv, skeys = stage
            P.act(lambda e: e.copy(out=sv, in_=xv.rearrange("p (k c) -> p k c", c=128)), reads=[("X", t)], writes=skeys)
            if final_ln:
                P.dma("sp", lambda e: e.dma_start(out=xscr[t * 128:(t + 1) * 128, :], in_=xv), reads=[("X", t)],
                      writes=[("xscr", t)])
            return (t, sv, skeys)

        def ln_stage_sample(last_layer, final_ln):
            if final_ln and last_layer:
                P.dma("sp", lambda e: e.dma_start(out=ys, in_=XS[:]), reads=[("XS",)], final=True)
                return None
            P.act(lambda e: e.copy(out=XSB[:], in_=XS[:]), reads=[("XS",)], writes=[("XB", 0)])
            return ("s", None, None)

        def ln_transposes(item):
            if item is None:
                return
            t, sv, skeys = item
            if t == "s":
                transpose_to([XSB[:, k * 128:(k + 1) * 128] for k in range(8)], [(XST[:], 0, 8)], NS,
                             reads=[("XB", 0)], writes=[("XST",)])
            else:
                transpose_to([sv[:, k, :] for k in range(8)], [(XT[:, :, t * 128:(t + 1) * 128], 0, 8)], 128,
                             reads=skeys, writes=[("XT", t)])

        pj_rr = {"n": 0}

        pj_banks = {"b": [0, 1]}

        def pj_slot():
            bl = pj_banks["b"]
            s = bl[pj_rr["n"] % len(bl)]
            pj_rr["n"] += 1
            return s

        items = []

        def layer(l):
            last_layer = (l == DEPTH - 1)

            def sgu_consts():
                P.dma("sp", lambda e: e.dma_start(out=GAM[:, 0, :], in_=sgu_g[l].partition_broadcast(128)), writes=[("GAM",)])
                P.dma("sp", lambda e: e.dma_start(out=GAM[:, 1, :], in_=sgu_b[l].partition_broadcast(128)), writes=[("GAM",)])
                P.dma("sp", lambda e: e.dma_start(out=BSP[:], in_=b_sp[l].partition_broadcast(128)), writes=[("BSP",)])
                srcb = bass.AP(b_sp.tensor, b_sp[l].offset, [[0, 128], [128, 4], [1, 4]])
                P.dma("sp", lambda e: [e.dma_start(out=BSPS[:, :, b_ * 4:(b_ + 1) * 4], in_=srcb) for b_ in range(4)],
                      writes=[("BSPS",)], n_dma=4)
                P.dma("pool", lambda e: e.dma_start(out=SGL[:], in_=w_sp[l].rearrange("g t s -> t g s")), writes=[("SGL",)])
                transpose_to([SGL[:, g, :] for g in range(4)], [(SGW[:], 0, 4)], 128,
                             reads=[("SGL",)], writes=[("SGW",)])
                P.dve(lambda e: e.tensor_tensor(out=SGW[:], in0=SGW[:],
                                                in1=MASK[:, 128:256].unsqueeze(1).broadcast_to([128, 4, 128]), op=ALU.mult),
                      reads=[("SGW",), ("CB",)], writes=[("SGW",)])
                P.pool(lambda e: e.memset(SGWSF[:], 0.0), writes=[("SGWSF",)])
                for b in range(4):
                    def f(e, b=b):
                        return [e.dma_start(out=SGWSF[b * 4:(b + 1) * 4, g_, b * 4:(b + 1) * 4],
                                            in_=w_sp[l, g_, 0:4, 0:4].rearrange("t s -> s t"),
                                            allow_slow_non_contiguous=True) for g_ in range(4)]
                    P.dma("sp", f, reads=[("SGWSF",)], writes=[("SGWSF",)], n_dma=4)
                P.dve(lambda e: e.tensor_tensor(out=SGWS[:], in0=SGWSF[:],
                                                in1=CB[0:NS, CB_SGM:CB_SGM + NS].unsqueeze(1).broadcast_to([NS, 4, NS]),
                                                op=ALU.mult),
                      reads=[("SGWSF",), ("CB",)], writes=[("SGWS",)])

            def rope_ops(psv, nh, npart, cs1, cs2, rt1, rt2, ps_keys, rtkey, extra_reads=()):
                v = psv.rearrange("p (h d) -> p h d", d=64)
                x1 = v[:, :, 0:8].unsqueeze(2).broadcast_to([npart, nh, 2, 8])
                x2 = v[:, :, 8:16].unsqueeze(2).broadcast_to([npart, nh, 2, 8])
                c1 = cs1.rearrange("p (a d) -> p a d", d=8).unsqueeze(1).broadcast_to([npart, nh, 2, 8])
                c2 = cs2.rearrange("p (a d) -> p a d", d=8).unsqueeze(1).broadcast_to([npart, nh, 2, 8])
                o1 = rt1.rearrange("p (h a d) -> p h a d", a=2, d=8)
                o2 = rt2.rearrange("p (h a d) -> p h a d", a=2, d=8)
                P.dve(lambda e: e.tensor_tensor(out=o1, in0=x1, in1=c1, op=ALU.mult), reads=list(ps_keys) + [("CF",)] + list(extra_reads),
                      writes=[rtkey])
                P.dve(lambda e: e.tensor_tensor(out=o2, in0=x2, in1=c2, op=ALU.mult), reads=list(ps_keys) + [("CF",)] + list(extra_reads),
                      writes=[rtkey])
                return o1, o2

            for j in range(2):
                for g in range(3):
                    win, dil = GROUPS[g]
                    nb = NT // dil
                    c = 2 * g + j
                    kb = 0
                    nrows = min(win, T)
                    row0 = T - nrows

                    def load_ab(g=g, j=j):
                        s = take_slots(1)[0]
                        cq = 256 * g + 128 * j

                        def f(e):
                            W = wview(s, 384)
                            return [e.dma_start(out=W[:, :, i * 128:(i + 1) * 128],
                                                in_=wsrc(w_in[l], 768 * i + cq, 768 * i + cq + 128)) for i in range(3)]
                        P.dma("pool", f, writes=[("WR", s)], n_dma=3)
                        return s

                    def comp_ab(s, g=g, j=j, win=win, dil=dil, nb=nb, c=c, kb=kb, nrows=nrows, row0=row0):
                        W = wview(s, 384)
                        tr["banks"] = [7]
                        ktb = KT[kb]
                        vab = VAUG[kb]

                        def tok(r, b):
                            return slice(r + dil * 128 * b, r + dil * 128 * b + dil * 127 + 1, dil)

                        def attkeys(t):
                            return [("ATT", c, r, t // dil) for r in range(dil)]

                        def ktkeys(t):
                            return [("KT", kb, r, t // dil) for r in range(dil)]

                        ni = 1 if g == 0 else 4
                        srcp = cpast[g][l].rearrange("b (m s) c -> m b s c", s=dil)

                        def fpast(e):
                            return [e.dma_start(out=PKV[:, b_, 0:ni, kv, :],
                                                in_=srcp[:, b_, 0:ni, kv * 256 + j * 128: kv * 256 + j * 128 + 128])
                                    for kv in range(2) for b_ in range(4)]
                        P.dma("pool", fpast, writes=[("PKV",)], n_dma=8)
                        KAB = int(os.environ.get("KAB", "9"))
                        if KAB <= 1:
                            return
                        def qk_tile(t):
                            ps = pj_slot()
                            bk, c0 = ps, 0
                            out = bank(bk, c0, 256)
                            pk = psk(bk, c0, 256)
                            mm_group(out, [(XT[:, k, t * 128:(t + 1) * 128], W[:, k, 0:256]) for k in range(8)],
                                     reads=[("XT", t), ("WR", s)], writes=pk)
                            qb = t % 4
                            P.act(lambda e: e.copy(out=QKB[:, qb, :], in_=out), reads=pk, writes=[("QKB", qb)])
                            KB = os.environ.get("KB", "mrkt")
                            if "r" not in KB:
                                return qb
                            cs1 = CF[:, CF_CS1 + t * 16: CF_CS1 + t * 16 + 16]
                            cs2 = CF[:, CF_CS2 + t * 16: CF_CS2 + t * 16 + 16]
                            o1, o2 = rope_ops(out, 4, 128, cs1, cs2, RT[:, qb, 0, :], RT[:, qb, 1, :], pk, ("RT", qb), [("QKB", qb)])
                            dq = QKB[:, qb, :].rearrange("p (h d) -> p h d", d=64)[:, :, 0:16].rearrange("p h (a d) -> p h a d", d=8)
                            P.dve(lambda e: e.tensor_tensor(out=dq, in0=o1, in1=o2, op=ALU.add),
                                  reads=[("RT", qb), ("QKB", qb)], writes=[("QKB", qb)])
                            if t * 128 >= row0 and "k" in KB:
                                kfb = qb % 2
                                P.act(lambda e: e.copy(out=KF[:, kfb, :], in_=out[:, 128:256]), reads=pk, writes=[("KF", kfb)])
                                dk = KF[:, kfb, :].rearrange("p (h d) -> p h d", d=64)[:, :, 0:16].rearrange("p h (a d) -> p h a d", d=8)
                                P.dve(lambda e: e.tensor_tensor(out=dk, in0=o1[:, 2:4], in1=o2[:, 2:4], op=ALU.add),
                                       reads=[("RT", qb), ("KF", kfb)], writes=[("KF", kfb)])
                                r0 = t * 128 - row0
                                P.dma("sp", lambda e: e.dma_start(out=kvp[g][l, r0:r0 + 128, j * 128:(j + 1) * 128], in_=KF[:, kfb, :]),
                                      reads=[("KF", kfb)], final=True)
                            return qb

                        def qk_transposes(t, qb):
                            if "t" not in os.environ.get("KB", "mrkt"):
                                return
                            transpose_to([QKB[:, qb, 0:128], QKB[:, qb, 128:256]],
                                         [(ATT[:, c, t * 128:(t + 1) * 128], 0, 1), (ktb[:, t * 128:(t + 1) * 128], 1, 1)], 128,
                                         reads=[("QKB", qb)], writes=attkeys(t) + ktkeys(t))

                        def v_chunk(r, b):
                            ch = r * nb + b
                            ps = pj_slot()
                            bk, c0 = ps, 0
                            out = bank(bk, c0, 128)
                            pk = psk(bk, c0, 128)
                            tl = list(range(dil * b, dil * b + dil))
                            mm_group(out, [(XT[:, k, tok(r, b)], W[:, k, 256:384]) for k in range(8)],
                                     reads=[("XT", t_) for t_ in tl] + [("WR", s)], writes=pk)
                            P.act(lambda e: e.copy(out=vab[:, ch, :], in_=out), reads=pk, writes=[("VAUG", kb, ch)])
                            tok0 = r + dil * 128 * b
                            if tok0 >= row0:
                                vb = ch % 2
                                P.dve(lambda e: e.tensor_copy(out=VF[:, vb, :], in_=out), reads=pk, writes=[("VF", vb)])
                                dst = kvp[g][l].rearrange("(i s) c -> i s c", s=dil)
                                i0 = (tok0 - row0 - r) // dil
                                P.dma("sp", lambda e: e.dma_start(out=dst[i0:i0 + 128, r, 256 + j * 128: 256 + (j + 1) * 128],
                                                                  in_=VF[:, vb, :]),
                                      reads=[("VF", vb)], final=True)

                        pj_banks["b"] = [0, 1, 2, 3, 4, 5]
                        pendq = []
                        for t in range(NT):
                            qb = qk_tile(t)
                            pendq.append((t, qb))
                            if len(pendq) > 3:
                                qk_transposes(*pendq.pop(0))
                        ps = pj_slot()
                        bk, c0 = ps, 0
                        outs_ = bank(bk, c0, 256)[0:NS, :]
                        pk = psk(bk, c0, 256)
                        mm_group(outs_, [(XST[:, k, :], W[:, k, 0:256]) for k in range(8)],
                                 reads=[("XST",), ("WR", s)], writes=pk)
                        while pendq:
                            qk_transposes(*pendq.pop(0))
                        if ni == 1:
                            transpose_to([PKV[:, b, 0, 0, :] for b in range(4)], [(PKT[:, :, 0, :], 0, 4)], 128,
                                         reads=[("PKV",)], writes=[("PKT",)])
                        else:
                            for b2 in range(0, 4, 2):
                                transpose_to([PKV[:, b, i, 0, :] for b in (b2, b2 + 1) for i in range(4)],
                                             [(PKT[:, b2, :, :], 0, 4), (PKT[:, b2 + 1, :, :], 4, 4)], 128,
                                             reads=[("PKV",)], writes=[("PKT",)])

                        P.act(lambda e: e.copy(out=SQKB[:], in_=outs_), reads=pk, writes=[("SQKB",)])
                        o1, o2 = rope_ops(outs_, 4, NS, CF[0:NS, CF_SS1:CF_SS1 + 16], CF[0:NS, CF_SS2:CF_SS2 + 16],
                                          SRT[:, 0, :], SRT[:, 1, :], pk, ("SRT",))
                        dq = SQKB[:].rearrange("p (h d) -> p h d", d=64)[:, :, 0:16].rearrange("p h (a d) -> p h a d", d=8)
                        P.dve(lambda e: e.tensor_tensor(out=dq, in0=o1, in1=o2, op=ALU.add), reads=[("SRT",), ("SQKB",)],
                              writes=[("SQKB",)])
                        P.act(lambda e: e.copy(out=SKF[:], in_=outs_[:, 128:256]), reads=pk, writes=[("SKF",)])
                        dk = SKF[:].rearrange("p (h d) -> p h d", d=64)[:, :, 0:16].rearrange("p h (a d) -> p h a d", d=8)
                        P.dve(lambda e: e.tensor_tensor(out=dk, in0=o1[:, 2:4], in1=o2[:, 2:4], op=ALU.add),
                               reads=[("SRT",), ("SKF",)], writes=[("SKF",)])
                        P.dma("sp", lambda e: e.dma_start(out=kvs[g][l, :, j * 128:(j + 1) * 128], in_=SKF[:]),
                              reads=[("SKF",)], final=True)
                        ps = pj_slot()
                        bk, c0 = ps, 0
                        outv = bank(bk, c0, 128)[0:NS, :]
                        pkv = psk(bk, c0, 128)
                        mm_group(outv, [(XST[:, k, :], W[:, k, 256:384]) for k in range(8)],
                                 reads=[("XST",), ("WR", s)], writes=pkv)
                        P.act(lambda e: e.copy(out=SVB[:], in_=outv), reads=pkv, writes=[("SVB",)])
                        P.dve(lambda e: e.tensor_copy(out=SVF[:], in_=outv), reads=pkv, writes=[("SVF",)])
                        P.dma("sp", lambda e: e.dma_start(out=kvs[g][l, :, 256 + j * 128: 256 + (j + 1) * 128], in_=SVF[:]),
                              reads=[("SVF",)], final=True)
                        if KAB <= 3:
                            return
                        for r in range(dil):
                            for b in range(nb):
                                v_chunk(r, b)
                        transpose_to([SQKB[:, 0:128], SQKB[:, 128:256]], [(SQT[:], 0, 1), (SKT[:], 1, 1)], NS,
                                     reads=[("SQKB",)], writes=[("SQT",), ("SKT",)])

                        if KAB <= 5:
                            return
                        blocks = [(r, b) for r in range(dil) for b in range(nb)]
                        NB = len(blocks)
                        state = {}
                        NPT = 3

                        def stage1(n):
                            r, b = blocks[n]
                            ss = n % NPT
                            bk = 2 * (n % 2)
                            S = PSA[:, bk * 512: bk * 512 + 1024]
                            pk = [("ps", bk), ("ps", bk + 1)]
                            chunks = ([b - 1] if b > 0 else []) + [b]
                            lo = 0 if b > 0 else 128

                            def f(e):
                                ins = None
                                for hp in range(2):
                                    rows = slice(hp * 64, hp * 64 + 64)
                                    q = ATT[rows, c, tok(r, b)]
                                    for ci, bb in enumerate(chunks):
                                        cc = hp * 512 + lo + ci * 128
                                        ins = e.matmul(S[:, cc:cc + 128], lhsT=ktb[rows, tok(r, bb)], rhs=q, start=True, stop=True)
                                return ins
                            P.pe(f, reads=[("ATT", c, r, b)] + [("KT", kb, r, bb) for bb in chunks], writes=pk)
                            Sv = S.rearrange("p (h c) -> p h c", c=512)[:, :, lo:256]
                            Pv = PT[:, ss, :].rearrange("p (h c) -> p h c", c=256)[:, :, lo:256]
                            A2H = os.environ.get("A2H", "em")
                            if "e" in A2H:
                                P.act(lambda e: e.activation(out=Pv, in_=Sv, func=AF.Exp, scale=0.125), reads=pk, writes=[("PT", ss)])
                            else:
                                for hp_ in range(2):
                                    P.act(lambda e, hp_=hp_: e.activation(out=PT[:, ss, hp_ * 256 + lo:hp_ * 256 + 256],
                                                                          in_=S[:, hp_ * 512 + lo:hp_ * 512 + 256], func=AF.Exp, scale=0.125),
                                          reads=pk, writes=[("PT", ss)])
                            if "m" in A2H:
                                mk_ = MASK[:, lo:256].unsqueeze(1).broadcast_to([128, 2, 256 - lo])
                                P.dve(lambda e: e.tensor_tensor(out=Pv, in0=Pv, in1=mk_, op=ALU.mult),
                                      reads=[("PT", ss), ("CB",)], writes=[("PT", ss)])
                            else:
                                for hp_ in range(2):
                                    P.dve(lambda e, hp_=hp_: e.tensor_tensor(out=PT[:, ss, hp_ * 256 + lo:hp_ * 256 + 256],
                                                                             in0=PT[:, ss, hp_ * 256 + lo:hp_ * 256 + 256],
                                                                             in1=MASK[:, lo:256], op=ALU.mult),
                                          reads=[("PT", ss), ("CB",)], writes=[("PT", ss)])
                            state[n] = (ss, chunks, lo)

                        def stage2(n):
                            r, b = blocks[n]
                            ss, chunks, lo = state.pop(n)
                            ob = 4 + n % 2
                            O = bank(ob, 0, 256)
                            pk = [("ps", ob)]

                            def f(e):
                                ins = None
                                nchk = len(chunks)
                                for hp in range(2):
                                    rows = slice(hp * 64, hp * 64 + 64)
                                    tp = (0, 64) if hp else None
                                    for which in range(2):
                                        for ci, bb in enumerate(chunks):
                                            cc = hp * 256 + lo + ci * 128
                                            lhs = vab[:, r * nb + bb, hp * 64:(hp + 1) * 64] if which == 0 else ONES[:, 0:64]
                                            ins = e.matmul(O[rows, which * 128:(which + 1) * 128], lhsT=lhs, rhs=PT[:, ss, cc:cc + 128],
                                                           start=(ci == 0), stop=(ci == nchk - 1), tile_position=tp)
                                return ins
                            P.pe(f, reads=[("PT", ss), ("CB",)] + [("VAUG", kb, r * nb + bb) for bb in chunks], writes=pk)
                            P.act(lambda e: e.copy(out=ATT[:, c, tok(r, b)], in_=O[:, 0:128]), reads=pk, writes=[("ATT", c, r, b)])
                            dkeys = [("DEN", hp, t_) for hp in range(2) for t_ in range(dil * b, dil * b + dil)]
                            if g == 0:
                                P.dve(lambda e: e.tensor_copy(out=DEN[:, tok(r, b)], in_=O[:, 128:256]), reads=pk, writes=dkeys)
                            else:
                                P.dve(lambda e: e.tensor_tensor(out=DEN[:, tok(r, b)], in0=DEN[:, tok(r, b)], in1=O[:, 128:256],
                                                                op=ALU.add),
                                      reads=pk + dkeys, writes=dkeys)

                        LAG = 2
                        for n in range(NB + LAG):
                            if n < NB:
                                stage1(n)
                            if n >= LAG:
                                stage2(n - LAG)

                        if KAB <= 4:
                            return
                        SB_ = 6
                        Sp = bank(SB_, 0, 32)
                        Sn = bank(SB_, 32, 32)[0:NS, :]
                        Dn = bank(SB_, 64, 32)
                        Ov = bank(SB_, 128, 32)
                        k0 = k1 = ("ps", SB_)

                        def f_sp(e):
                            ins = None
                            for hp in range(2):
                                rows = slice(hp * 64, hp * 64 + 64)
                                for b in range(4):
                                    if g == 0:
                                        ins = e.matmul(Sp[:, hp * 16 + b * 4: hp * 16 + b * 4 + 4], lhsT=PKT[rows, b, 0, :],
                                                       rhs=SQT[rows, b * 4:b * 4 + 4], start=True, stop=True)
                                    else:
                                        for i in range(4):
                                            col = hp * 16 + b * 4 + i
                                            ins = e.matmul(Sp[:, col:col + 1], lhsT=PKT[rows, b, i, :],
                                                           rhs=SQT[rows, b * 4 + i:b * 4 + i + 1], start=True, stop=True)
                                ins = e.matmul(Sn[:, hp * 16:hp * 16 + 16], lhsT=SKT[rows, :], rhs=SQT[rows, :],
                                               start=True, stop=True)
                            return ins
                        P.pe(f_sp, reads=[("PKT",), ("SQT",), ("SKT",)], writes=[k0])
                        P.act(lambda e: e.activation(out=SPP[:], in_=Sp, func=AF.Exp, scale=0.125), reads=[k0], writes=[("SPP",)])
                        P.act(lambda e: e.activation(out=SPN[:], in_=Sn, func=AF.Exp, scale=0.125), reads=[k0], writes=[("SPN",)])
                        if g == 0:
                            P.dve(lambda e: e.tensor_tensor(out=SPP[:], in0=SPP[:], in1=CB[:, CB_MP0:CB_MP0 + 32], op=ALU.mult),
                                  reads=[("SPP",), ("CB",)], writes=[("SPP",)])
                        mn = CB_MN0 if g == 0 else CB_MN1
                        P.dve(lambda e: e.tensor_tensor(out=SPN[:], in0=SPN[:], in1=CB[0:NS, mn:mn + 32], op=ALU.mult),
                              reads=[("SPN",), ("CB",)], writes=[("SPN",)])

                        def f_pv(e):
                            e.matmul(Dn, lhsT=ONES, rhs=SPP[:], start=True, stop=False)
                            ins = e.matmul(Dn, lhsT=ONES[0:NS, :], rhs=SPN[:], start=False, stop=True)
                            for hp in range(2):
                                for b in range(4):
                                    if g == 0:
                                        cs = slice(hp * 16 + b * 4, hp * 16 + b * 4 + 4)
                                        e.matmul(Ov[:, cs], lhsT=PKV[:, b, 0, 1, :], rhs=SPP[:, cs], start=True, stop=False)
                                        ins = e.matmul(Ov[:, cs], lhsT=SVB[:], rhs=SPN[:, cs], start=False, stop=True)
                                    else:
                                        for i in range(4):
                                            cs = slice(hp * 16 + b * 4 + i, hp * 16 + b * 4 + i + 1)
                                            e.matmul(Ov[:, cs], lhsT=PKV[:, b, i, 1, :], rhs=SPP[:, cs], start=True, stop=False)
                                            ins = e.matmul(Ov[:, cs], lhsT=SVB[:], rhs=SPN[:, cs], start=False, stop=True)
                            return ins
                        P.pe(f_pv, reads=[("SPP",), ("SPN",), ("PKV",), ("SVB",), ("CB",)], writes=[k0, k1])
                        if g == 0:
                            P.dve(lambda e: e.tensor_copy(out=SDEN[:], in_=Dn), reads=[k0], writes=[("SDEN",)])
                        else:
                            P.dve(lambda e: e.tensor_tensor(out=SDEN[:], in0=SDEN[:], in1=Dn, op=ALU.add),
                                  reads=[k0, ("SDEN",)], writes=[("SDEN",)])
                        P.act(lambda e: e.copy(out=SO[:, g, :], in_=Ov), reads=[k1], writes=[("SO", g)])

                        if g == 2:
                            for q4 in range(4):
                                cs = slice(q4 * 512, (q4 + 1) * 512)
                                dnk = [("DEN", hp, t_) for hp in range(2) for t_ in range(q4 * 4, q4 * 4 + 4)]
                                P.act(lambda e, cs=cs: e.activation(out=DEN[:, cs], in_=DEN[:, cs], func=AF.Ln), reads=dnk, writes=dnk)
                                P.act(lambda e, cs=cs: e.activation(out=DEN[:, cs], in_=DEN[:, cs], func=AF.Exp, scale=-1.0),
                                      reads=dnk, writes=dnk)
                                for g2 in range(3):
                                    c2 = 2 * g2 + j
                                    d2 = GROUPS[g2][1]
                                    ak = [("ATT", c2, r_, t_ // d2) for r_ in range(d2) for t_ in range(q4 * 4, q4 * 4 + 4)]
                                    ak = sorted(set(ak))
                                    P.dve(lambda e, cs=cs, c2=c2: e.tensor_tensor(out=ATT[:, c2, cs], in0=ATT[:, c2, cs],
                                                                                  in1=DEN[:, cs], op=ALU.mult),
                                          reads=dnk + ak, writes=ak)
                            P.dve(lambda e: e.reciprocal(out=SDEN[:], in_=SDEN[:]), reads=[("SDEN",)], writes=[("SDEN",)])
                            for g2 in range(3):
                                for hp in range(2):
                                    rows = slice(hp * 64, hp * 64 + 64)
                                    P.dve(lambda e, g2=g2, rows=rows, hp=hp: e.tensor_tensor(
                                        out=ATTS[rows, 2 * g2 + j, :], in0=SO[rows, g2, hp * 16:hp * 16 + 16],
                                        in1=SDEN[rows, hp * 16:hp * 16 + 16], op=ALU.mult),
                                        reads=[("SO", g2), ("SDEN",)], writes=[("ATTS",)])
                    items.append((load_ab, comp_ab))

            def load_sgu():
                s = take_slots(1)[0]
                P.dma("pool", lambda e: e.dma_start(out=wview(s, 512), in_=wsrc(w_in[l], 2304, 2816)), writes=[("WR", s)])
                return s

            def comp_sgu(s):
                W = wview(s, 512)
                tr["banks"] = [6, 7]
                sgu_consts()

                pj_banks["b"] = [0, 1, 4, 5]

                def sgu_tile(t, npart, xt_tok, xt_keys, dst, dst_keys, wsg, bsp, wsg_key, sample=False):
                    ub = t % 3
                    ps = pj_slot()
                    bk, c0 = ps, 0
                    pk = psk(bk, c0, 256)
                    Ups = bank(bk, c0, 256)

                    def fu(e):
                        ins = None
                        for uc in range(2):
                            for k in range(8):
                                ins = e.matmul(Ups[:, uc * 128: uc * 128 + npart], lhsT=W[:, k, uc * 128:(uc + 1) * 128],
                                               rhs=xt_tok(k), start=(k == 0), stop=(k == 7))
                        return ins
                    P.pe(fu, reads=list(xt_keys) + [("WR", s)], writes=pk)
                    Uv = Ups.rearrange("p (u t) -> p u t", t=128)[:, :, 0:npart]
                    ugv = UG[:, ub, :].rearrange("p (u t) -> p u t", t=128)[:, :, 0:npart]
                    P.act(lambda e: e.activation(out=ugv, in_=Uv, func=AF.Gelu_apprx_tanh), reads=pk, writes=[("UG", ub)])
                    ps2 = pj_slot()
                    bk2, c02 = ps2, 0
                    pk2 = psk(bk2, c02, 256)
                    Gps = bank(bk2, c02, 256)[0:npart, :]
                    mm_group(Gps, [(xt_tok(k), W[:, k, 256:512]) for k in range(8)], reads=list(xt_keys) + [("WR", s)], writes=pk2)
                    gg = GG[0:npart, ub, :]
                    P.act(lambda e: e.activation(out=gg, in_=Gps, func=AF.Gelu_apprx_tanh), reads=pk2, writes=[("GG", ub)])
                    stt = STATG[0:npart, ub, :]
                    sk = ("STATG", ub)
                    P.dve(lambda e: e.bn_stats(stt[:, 0:6], gg), reads=[("GG", ub)], writes=[sk])
                    P.dve(lambda e: e.bn_aggr(stt[:, 12:14], stt[:, 0:6]), reads=[sk], writes=[sk])
                    rstd_ops(stt, sk, npart)

                    def s2():
                        gt = GTMP[0:npart, ub, :]
                        P.dve(lambda e: e.scalar_tensor_tensor(out=gt, in0=gg, scalar=stt[:, 12:13], in1=GAM[0:npart, 0, :],
                                                               op0=ALU.subtract, op1=ALU.mult),
                              reads=[sk, ("GG", ub), ("GAM",)], writes=[("GTMP", ub)])
                        gv = GV[0:npart, ub, :]
                        if sample:
                            P.dve(lambda e: e.scalar_tensor_tensor(out=GVF[:], in0=gt, scalar=stt[:, 14:15], in1=GAM[0:npart, 1, :],
                                                                   op0=ALU.mult, op1=ALU.add),
                                  reads=[sk, ("GTMP", ub), ("GAM",)], writes=[("GVF",)])
                            P.dma("sp", lambda e: e.dma_start(out=chunkv[l], in_=GVF[:]), reads=[("GVF",)], final=True)
                            P.act(lambda e: e.copy(out=gv, in_=GVF[:]), reads=[("GVF",)], writes=[("GV", ub)])
                        else:
                            P.dve(lambda e: e.scalar_tensor_tensor(out=gv, in0=gt, scalar=stt[:, 14:15], in1=GAM[0:npart, 1, :],
                                                                   op0=ALU.mult, op1=ALU.add),
                                  reads=[sk, ("GTMP", ub), ("GAM",)], writes=[("GV", ub)])
                    gv = GV[0:npart, ub, :]
                    return s2, (lambda: sgu_part2(t, npart, gv, ugv, ub, dst, dst_keys, wsg, bsp, wsg_key))

                def sgu_part2(t, npart, gv, ugv, ub, dst, dst_keys, wsg, bsp, wsg_key):
                    MBk = 2 + (t % 2)
                    mk = [("ps", MBk)]

                    def fM(e):
                        ins = None
                        for sg in range(4):
                            ins = e.matmul(bank(MBk, sg * 128, npart), lhsT=gv[:, (sg // 2) * 128:(sg // 2) * 128 + 128],
                                           rhs=wsg[:, sg, :], start=True, stop=True)
                        return ins
                    P.pe(fM, reads=[("GV", ub), wsg_key], writes=mk)
                    for sg in range(4):
                        Mps = bank(MBk, sg * 128, npart)
                        rows = slice((sg % 2) * 64, (sg % 2) * 64 + 64)
                        tb = sg % 2
                        tmp = STMP[rows, tb, 0:npart]
                        P.dve(lambda e, Mps=Mps, rows=rows, sg=sg, tmp=tmp: e.tensor_tensor(out=tmp, in0=Mps[rows, :],
                                                                                         in1=bsp[rows, sg, :], op=ALU.add),
                              reads=mk + [("BSP",), ("BSPS",)], writes=[("STMP", tb)])
                        P.dve(lambda e, rows=rows, sg=sg, tmp=tmp: e.tensor_tensor(out=dst(rows, sg // 2), in0=tmp,
                                                                                  in1=ugv[rows, sg // 2, :], op=ALU.mult),
                              reads=[("STMP", tb), ("UG", ub)], writes=dst_keys)

                st2, st3 = {}, {}
                for t in range(NT + 1 + 2):
                    if t < NT:
                        st2[t], st3[t] = sgu_tile(t, 128, lambda k, t=t: XT[:, k, t * 128:(t + 1) * 128], [("XT", t)],
                                                  lambda rows, uc, t=t: SGUT[rows, uc, t * 128:(t + 1) * 128], [("ATT", 6, t)],
                                                  SGW, BSP, ("SGW",))
                    elif t == NT:
                        st2[t], st3[t] = sgu_tile(NT, NS, lambda k: XST[:, k, :], [("XST",)],
                                                  lambda rows, uc: ATTS[rows, 6 + uc, :], [("ATTS",)], SGWS, BSPS, ("SGWS",),
                                                  sample=True)
                    if (t - 1) in st2:
                        st2.pop(t - 1)()
                    if (t - 2) in st3:
                        st3.pop(t - 2)()
            items.append((load_sgu, comp_sgu))

            def load_mix():
                ss = take_slots(2)
                for i, s in enumerate(ss):
                    P.dma("pool", lambda e, i=i, s=s: e.dma_start(out=wview(s, 512), in_=wsrc(w_mix[l], i * 512, (i + 1) * 512)),
                          writes=[("WR", s)])
                return ss

            def tokmajor_ln_phase(ss, lhs_fn, lhs_keys_fn, lhs_s_fn, lhs_s_keys, which, first_phase, final_ln):
                Ws = [wview(s, 512) for s in ss]
                load_ln_tables(l, which)
                tr["banks"] = [6, 7]
                pend = []
                ctx = {}
                for t in range(NT + 1 + 2):
                    if t <= NT:
                        sample = (t == NT)
                        npart = NS if sample else 128
                        pb = (t % 3) * 2
                        if first_phase and not sample:
                            src = xp if l == 0 else xscr
                            rk = [] if l == 0 else [("xscr", t)]
                            P.dma("sp", lambda e, t=t, src=src: e.dma_start(out=X[:, t, :], in_=src[t * 128:(t + 1) * 128, :]),
                                  reads=rk, writes=[("X", t)])
                        for hf in range(2):
                            out = bank(pb + hf)[0:npart, :]
                            if sample:
                                pairs = [(lhs_s_fn(k), Ws[hf][:, k, :]) for k in range(8)]
                                rd = list(lhs_s_keys)
                            else:
                                pairs = [(lhs_fn(k, t), Ws[hf][:, k, :]) for k in range(8)]
                                rd = list(lhs_keys_fn(t))
                            mm_group(out, pairs, reads=rd + [("WR", ss[hf])], writes=[("ps", pb + hf)])
                        ps2 = PSA[0:npart, pb * 512: pb * 512 + 1024]
                        pk2 = [("ps", pb), ("ps", pb + 1)]
                        xv = XS[:] if sample else X[:, t, :]
                        xk = [("XS",)] if sample else [("X", t)]
                        stt = STAT[0:npart, t % 4, :]
                        sk = ("STAT", t % 4)
                        ctx[t] = (xv, npart, xk, stt, sk, sample)
                        ln_s1(xv, ps2, pk2, npart, xk, stt, sk)
                    if 1 <= t <= NT + 1:
                        xv, npart, xk, stt, sk, sample = ctx[t - 1]
                        ln_s2(xv, npart, xk, stt, sk)
                    if 2 <= t <= NT + 2:
                        t2 = t - 2
                        xv, npart, xk, stt, sk, sample = ctx.pop(t2)
                        ln_s3(xv, npart, xk, stt, sk)
                        if sample:
                            pend.append(ln_stage_sample(last_layer, final_ln))
                        else:
                            sv = lhs_fn(slice(0, 8), t2)
                            pend.append(ln_stage_tile(t2, last_layer, final_ln, (sv, list(lhs_keys_fn(t2)))))
                    if len(pend) > 2:
                        ln_transposes(pend.pop(0))
                while pend:
                    ln_transposes(pend.pop(0))

            def comp_mix(ss):
                P.fence("dve", lambda e: e.memset(DUMMY[:, 0:1], 0.0), AB_PREF, ("X",))
                P.fence("dve", lambda e: e.memset(DUMMY[:, 1:2], 0.0), ("ATT",), ("ATTC",))
                tokmajor_ln_phase(ss, lambda k, t: ATTALL[:, k, t * 128:(t + 1) * 128],
                                  lambda t: [("ATTC", t)],
                                  lambda k: ATTS[:, k, :], [("ATTS",)], 0, True, False)
            items.append((load_mix, comp_mix))

            for ch in range(4):
                def load_m(ch=ch):
                    s = take_slots(1)[0]
                    P.dma("pool", lambda e: e.dma_start(out=wview(s, 512), in_=wsrc(w_xkv[l], ch * 512, (ch + 1) * 512)),
                          writes=[("WR", s)])
                    return s

                def comp_m(s, ch=ch):
                    W = wview(s, 512)
                    pj_banks["b"] = [2, 3]
                    if ch == 0:
                        P.fence("dve", lambda e: e.memset(DUMMY[:, 5:6], 0.0), S_3, S_M + ("MEMKT", "MEMV"))
                        for mt_ in range(2):
                            P.dma("pool", lambda e, mt_=mt_: e.dma_start(out=XB[:, mt_, :], in_=memp[mt_ * 128:(mt_ + 1) * 128, :]),
                                  writes=[("XB", mt_)])
                            transpose_to([XB[:, mt_, k * 128:(k + 1) * 128] for k in range(8)],
                                         [(MEMT[:, :, mt_ * 128:(mt_ + 1) * 128], 0, 8)], 128,
                                         reads=[("XB", mt_)], writes=[("MEMT",)])
                    for mt in range(2):
                        bk = mt
                        out = bank(bk)
                        mm_group(out, [(MEMT[:, k, mt * 128:(mt + 1) * 128], W[:, k, :]) for k in range(8)],
                                 reads=[("MEMT",), ("WR", s)], writes=psk(bk, 0, 512))
                        mb = (ch * 2 + mt) % 2
                        P.act(lambda e, out=out, mb=mb: e.copy(out=MKF[:, mb, :], in_=out), reads=psk(bk, 0, 512),
                              writes=[("MKF", mb)])
                        if ch >= 2:
                            P.dve(lambda e, out=out, mt=mt: e.tensor_copy(out=MEMV[:, mt, (ch - 2) * 512:(ch - 1) * 512], in_=out),
                                  reads=psk(bk, 0, 512), writes=[("MEMV",)])
                        P.dma("sp", lambda e, mb=mb, mt=mt: e.dma_start(
                            out=memkvp[l, mt * 128:(mt + 1) * 128, ch * 512:(ch + 1) * 512], in_=MKF[:, mb, :]),
                            reads=[("MKF", mb)], final=True)
                    if ch < 2:
                        for sub in range(4):
                            ps = pj_slot()
                            bk, c0 = ps, 0
                            out = bank(bk, c0, 256)
                            mm_group(out, [(W[:, k, sub * 128:(sub + 1) * 128], MEMT[:, k, :]) for k in range(8)],
                                     reads=[("MEMT",), ("WR", s)], writes=psk(bk, c0, 256))
                            P.act(lambda e, out=out, sub=sub: e.copy(out=MEMKT[:, ch * 4 + sub, :], in_=out),
                                  reads=psk(bk, c0, 256), writes=[("MEMKT",)])
                items.append((load_m, comp_m))

            for wc in range(2):
                def load_xq(wc=wc):
                    s = take_slots(1)[0]
                    P.dma("pool", lambda e: e.dma_start(out=wview(s, 512), in_=wsrc(w_xq[l], wc * 512, (wc + 1) * 512)),
                          writes=[("WR", s)])
                    return s

                def comp_xq(s, wc=wc):
                    W = wview(s, 512)
                    if wc == 0:
                        P.fence("dve", lambda e: e.memset(DUMMY[:, 1:2], 0.0), ("ATT", "ATTC"), ("QXT",))
                    n = 0
                    for tg in range(4):
                        for sub in range(4):
                            bk = 4 + (n % 2)
                            n += 1
                            out = bank(bk)
                            pk = psk(bk, 0, 512)
                            mm_group(out, [(W[:, k, sub * 128:(sub + 1) * 128], XT[:, k, tg * 512:(tg + 1) * 512]) for k in range(8)],
                                     reads=[("XT", t_) for t_ in range(tg * 4, tg * 4 + 4)] + [("WR", s)], writes=pk)
                            dst = QXT[:, wc * 4 + sub, tg * 512:(tg + 1) * 512]
                            wk = [("QXT", wc * 4 + sub, t_) for t_ in range(tg * 4, tg * 4 + 4)]
                            if n % 2:
                                P.act(lambda e, dst=dst, out=out: e.copy(out=dst, in_=out), reads=pk, writes=wk)
                            else:
                                P.dve(lambda e, dst=dst, out=out: e.tensor_copy(out=dst, in_=out), reads=pk, writes=wk)
                    out = bank(4, 0, 64)
                    pk = psk(4, 0, 64)

                    def fs(e):
                        ins = None
                        for sub in range(4):
                            for k in range(8):
                                ins = e.matmul(out[:, sub * NS:(sub + 1) * NS], lhsT=W[:, k, sub * 128:(sub + 1) * 128],
                                               rhs=XST[:, k, :], start=(k == 0), stop=(k == 7))
                        return ins
                    P.pe(fs, reads=[("XST",), ("WR", s)], writes=pk)
                    P.act(lambda e: e.copy(out=QXST[:, wc * 4:(wc + 1) * 4, :], in_=out.rearrange("p (c t) -> p c t", t=NS)),
                          reads=pk, writes=[("QXST", wc)])
                items.append((load_xq, comp_xq))

            def load_xo():
                ss = take_slots(2)
                for i, s in enumerate(ss):
                    P.dma("pool", lambda e, i=i, s=s: e.dma_start(out=wview(s, 512), in_=wsrc(w_xo[l], i * 512, (i + 1) * 512)),
                          writes=[("WR", s)])
                return ss

            def comp_xo(ss):
                P.fence("dve", lambda e: e.memset(DUMMY[:, 6:7], 0.0), S_M, S_X)
                xitems = [(tg, h) for tg in range(4) for h in range(4)]

                def xa(n):
                    tg, h = xitems[n]
                    tk = slice(tg * 512, (tg + 1) * 512)
                    sb_ = 0 if n % 2 == 0 else 5
                    S2 = PSA[:, sb_ * 512: sb_ * 512 + 1024]
                    kS = [("ps", sb_), ("ps", sb_ + 1)]

                    def fS(e):
                        ins = None
                        for mt in range(2):
                            for dc in range(2):
                                ins = e.matmul(bank(sb_ + mt), lhsT=MEMKT[:, 2 * h + dc, mt * 128:(mt + 1) * 128],
                                               rhs=QXT[:, 2 * h + dc, tk], start=(dc == 0), stop=(dc == 1))
                        return ins
                    P.pe(fS, reads=[("MEMKT",)] + [("QXT", 2 * h + dc_, t_) for dc_ in range(2) for t_ in range(tg * 4, tg * 4 + 4)],
                         writes=kS)
                    pb = n % 2
                    PmA = PM[:, pb, :]
                    P.act(lambda e: e.activation(out=PmA, in_=S2, func=AF.Exp, scale=1.0 / 16.0), reads=kS, writes=[("PM", pb)])

                def xb(n):
                    tg, h = xitems[n]
                    tk = slice(tg * 512, (tg + 1) * 512)
                    pb = n % 2
                    rb = n % 2
                    PmA = PM[:, pb, :]
                    Dps = bank(2)
                    kD = [("ps", 2)]

                    def fD(e):
                        e.matmul(Dps, lhsT=ONES, rhs=PmA[:, 0:512], start=True, stop=False)
                        return e.matmul(Dps, lhsT=ONES, rhs=PmA[:, 512:1024], start=False, stop=True)
                    P.pe(fD, reads=[("PM", pb), ("CB",)], writes=kD)
                    P.act(lambda e: e.activation(out=RDEN[:, rb, :], in_=Dps, func=AF.Ln), reads=kD, writes=[("RDEN", rb)])
                    P.act(lambda e: e.activation(out=RDEN[:, rb, :], in_=RDEN[:, rb, :], func=AF.Exp, scale=-1.0),
                          reads=[("RDEN", rb)], writes=[("RDEN", rb)])
                    for dc in range(2):
                        Ops = bank(3 + dc)
                        kO = [("ps", 3 + dc)]

                        def fO(e, dc=dc, Ops=Ops):
                            e.matmul(Ops, lhsT=MEMV[:, 0, (2 * h + dc) * 128:(2 * h + dc + 1) * 128], rhs=PmA[:, 0:512],
                                     start=True, stop=False)
                            return e.matmul(Ops, lhsT=MEMV[:, 1, (2 * h + dc) * 128:(2 * h + dc + 1) * 128],
                                            rhs=PmA[:, 512:1024], start=False, stop=True)
                        P.pe(fO, reads=[("PM", pb), ("MEMV",)], writes=kO)
                        P.dve(lambda e, dc=dc, Ops=Ops: e.tensor_tensor(out=QXT[:, 2 * h + dc, tk], in0=Ops, in1=RDEN[:, rb, :],
                                                                        op=ALU.mult),
                              reads=kO + [("RDEN", rb)], writes=[("QXT", 2 * h + dc, t_) for t_ in range(tg * 4, tg * 4 + 4)])

                xa(0)
                for n in range(len(xitems)):
                    if n + 1 < len(xitems):
                        xa(n + 1)
                    xb(n)
                P.fence("dve", lambda e: e.memset(DUMMY[:, 0:1], 0.0), ("PM",), ("CMK",))
                for b in range(4):
                    P.dma("pool", lambda e, b=b: e.dma_start(out=CMK[:], in_=cmem[l, b].rearrange("(m p) c -> p m c", p=128)[:, :, 0:1024]),
                          writes=[("CMK",)])
                    P.dma("pool", lambda e, b=b: e.dma_start(out=SCV[:], in_=cmem[l, b].rearrange("(m p) c -> p m c", p=128)[:, :, 1024:2048]),
                          writes=[("SCV",)])
                    for mt in range(2):
                        transpose_to([CMK[:, mt, ch * 128:(ch + 1) * 128] for ch in range(8)],
                                     [(MKTS[:, :, mt * 128:(mt + 1) * 128], 0, 8)], 128,
                                     reads=[("CMK",)], writes=[("MKTS",)])
                    Sx = bank(5, 0, 32)
                    Dx = bank(5, 128, 16)
                    Ox = bank(5, 256, 32)
                    kx = [("ps", 5), ("ps", 5), ("ps", 5)]

                    def fS(e, b=b):
                        ins = None
                        for mt in range(2):
                            for h in range(4):
                                for dc in range(2):
                                    ins = e.matmul(Sx[:, mt * 16 + h * 4: mt * 16 + h * 4 + 4],
                                                   lhsT=MKTS[:, 2 * h + dc, mt * 128:(mt + 1) * 128],
                                                   rhs=QXST[:, 2 * h + dc, b * 4:(b + 1) * 4], start=(dc == 0), stop=(dc == 1))
                        return ins
                    P.pe(fS, reads=[("MKTS",), ("QXST", 0), ("QXST", 1)], writes=[kx[0]])
                    P.act(lambda e: e.activation(out=SPM[:].rearrange("p a c -> p (a c)"), in_=Sx, func=AF.Exp, scale=1.0 / 16.0),
                          reads=[kx[0]], writes=[("SPM",)])

                    def fO(e, b=b, SCV=SCV):
                        e.matmul(Dx, lhsT=ONES, rhs=SPM[:, 0, :], start=True, stop=False)
                        ins = e.matmul(Dx, lhsT=ONES, rhs=SPM[:, 1, :], start=False, stop=True)
                        for h in range(4):
                            for dc in range(2):
                                ch = 2 * h + dc
                                for mt in range(2):
                                    ins = e.matmul(Ox[:, ch * 4:(ch + 1) * 4], lhsT=SCV[:, mt, ch * 128:(ch + 1) * 128],
                                                   rhs=SPM[:, mt, h * 4:(h + 1) * 4], start=(mt == 0), stop=(mt == 1))
                        return ins
                    P.pe(fO, reads=[("SPM",), ("SCV",), ("CB",)], writes=[kx[1], kx[2]])
                    P.dve(lambda e: e.reciprocal(out=SRD2[:], in_=Dx), reads=[kx[1]], writes=[("SRD2",)])
                    P.dve(lambda e, b=b: e.tensor_tensor(
                        out=QXST[:, :, b * 4:(b + 1) * 4].rearrange("p (h dc) i -> p h dc i", dc=2),
                        in0=Ox.rearrange("p (h dc i) -> p h dc i", dc=2, i=4),
                        in1=SRD2[:].rearrange("p (h i) -> p h i", i=4).unsqueeze(2).broadcast_to([128, 4, 2, 4]), op=ALU.mult),
                        reads=[kx[2], ("SRD2",)], writes=[("QXST", 0), ("QXST", 1)])
                tokmajor_ln_phase(ss, lambda k, t: QXT[:, k, t * 128:(t + 1) * 128],
                                  lambda t: [("QXT", cc, t) for cc in range(8)],
                                  lambda k: QXST[:, k, :], [("QXST", 0), ("QXST", 1)], 1, False, False)
            items.append((load_xo, comp_xo))

            for tg in range(4):
                for hc in range(8):
                    def load_up(hc=hc):
                        s = take_slots(1)[0]
                        P.dma("pool", lambda e: e.dma_start(out=wview(s, 512), in_=wsrc(w_up[l], hc * 512, (hc + 1) * 512)),
                              writes=[("WR", s)])
                        return s

                    def comp_up(s, tg=tg, hc=hc):
                        W = wview(s, 512)
                        tr["banks"] = [6, 7]
                        if hc >= 2 and mlp_pend:
                            ln_transposes(mlp_pend.pop(0))
                        if tg == 0 and hc == 0:
                            P.fence("dve", lambda e: e.memset(DUMMY[:, 2:3], 0.0), ("QXT",), ("HT",))
                            P.fence("dve", lambda e: e.memset(DUMMY[:, 7:8], 0.0), S_X, S_3)
                        for sub in range(4):
                            n = hc * 4 + sub
                            bk = 4 + (n % 2)
                            out = bank(bk)
                            pk = psk(bk, 0, 512)
                            mm_group(out, [(W[:, k, sub * 128:(sub + 1) * 128], XT[:, k, tg * 512:(tg + 1) * 512]) for k in range(8)],
                                     reads=[("XT", t_) for t_ in range(tg * 4, tg * 4 + 4)] + [("WR", s)], writes=pk)
                            rb = n % 4
                            P.act(lambda e, out=out, rb=rb: e.activation(out=RELU[:, rb, :], in_=out, func=AF.Relu), reads=pk,
                                  writes=[("RELU", rb)])
                            eng = P.dve
                            eng(lambda e, n=n, rb=rb: e.tensor_tensor(out=HT[:, n, :], in0=RELU[:, rb, :], in1=RELU[:, rb, :],
                                                                      op=ALU.mult),
                                reads=[("RELU", rb)], writes=[("HT", n)])
                            if sub == 1 and mlp_ln_pend:
                                mlp_ln_pend.pop(0)()
                        if tg == 0:
                            out = bank(5, 0, 64)
                            pk = psk(5, 0, 64)

                            def fs(e):
                                ins = None
                                for sub in range(4):
                                    for k in range(8):
                                        ins = e.matmul(out[:, sub * NS:(sub + 1) * NS], lhsT=W[:, k, sub * 128:(sub + 1) * 128],
                                                       rhs=XST[:, k, :], start=(k == 0), stop=(k == 7))
                                return ins
                            P.pe(fs, reads=[("XST",), ("WR", s)], writes=pk)
                            P.act(lambda e: e.activation(out=RELU[:, 0, 0:64], in_=out, func=AF.Relu), reads=pk,
                                  writes=[("RELU", 0)])
                            P.dve(lambda e: e.tensor_tensor(out=HTS[:, hc * 4:(hc + 1) * 4, :],
                                                            in0=RELU[:, 0, 0:64].rearrange("p (c t) -> p c t", t=NS),
                                                            in1=RELU[:, 0, 0:64].rearrange("p (c t) -> p c t", t=NS), op=ALU.mult),
                                  reads=[("RELU", 0)], writes=[("HTS",)])
                    items.append((load_up, comp_up))
                if tg == 0:
                    items.append((None, lambda s_: load_ln_tables(l, 2)))
                for hf in range(2):
                    for k4 in range(8):
                        def load_dn(hf=hf, k4=k4):
                            s = take_slots(1)[0]
                            P.dma("pool", lambda e: e.dma_start(out=wview(s, 512, 4), in_=wsrc(w_down[l], hf * 512, (hf + 1) * 512, k4 * 4, 4)),
                                  writes=[("WR", s)])
                            return s

                        def comp_dn(s, tg=tg, hf=hf, k4=k4):
                            W = wview(s, 512, 4)
                            with_s = (tg == 0)

                            def f(e):
                                ins = None
                                for kk in range(4):
                                    kc = k4 * 4 + kk
                                    for tl in range(4):
                                        ins = e.matmul(bank(tl), lhsT=HT[:, kc, tl * 128:(tl + 1) * 128], rhs=W[:, kk, :],
                                                       start=(kc == 0), stop=(kc == 31))
                                    if with_s:
                                        ins = e.matmul(bank(4)[0:NS, :], lhsT=HTS[:, kc, :], rhs=W[:, kk, :],
                                                       start=(kc == 0), stop=(kc == 31))
                                return ins
                            wr = [k for tl in range(4) for k in psk(tl, 0, 512)] + (psk(4, 0, 512) if with_s else [])
                            P.pe(f, reads=[("HT", k4 * 4 + kk) for kk in range(4)] + [("WR", s)] + ([("HTS",)] if with_s else []),
                                 writes=wr)
                            if k4 == 7:
                                tls = list(range(4 + (1 if with_s else 0)))
                                cx = {}
                                for tl in tls:
                                    sample = (tl == 4)
                                    t = tg * 4 + tl
                                    npart = NS if sample else 128
                                    xv = XS[:] if sample else X[:, t, :]
                                    xk = [("XS",)] if sample else [("X", t)]
                                    xh = xv[:, hf * 512:(hf + 1) * 512]
                                    psv = bank(tl)[0:npart, :]
                                    stk = ("STATM", tl)
                                    stt = STATM[0:npart, tl, :]
                                    cx[tl] = (xv, npart, xk, stt, stk, sample, t)
                                    P.dve(lambda e, xh=xh, psv=psv: e.scalar_tensor_tensor(out=xh, in0=xh, scalar=ALPHA, in1=psv,
                                                                                           op0=ALU.mult, op1=ALU.add),
                                          reads=psk(tl, 0, 512) + xk, writes=xk)
                                    P.dve(lambda e, xh=xh, stt=stt, hf=hf: e.bn_stats(stt[:, hf * 6:(hf + 1) * 6], xh), reads=xk,
                                          writes=[stk])
                                    if hf == 1:
                                        ln_s1_stats(xv, stt, stk, npart)
                                if hf == 1:
                                    for tl in tls:
                                        def fin(c_=cx[tl], tl=tl):
                                            xv, npart, xk, stt, stk, sample, t = c_
                                            ln_s2(xv, npart, xk, stt, stk)
                                            ln_s3(xv, npart, xk, stt, stk)
                                            if sample:
                                                mlp_pend.append(ln_stage_sample(last_layer, True))
                                            else:
                                                mlp_pend.append(ln_stage_tile(t, last_layer, True, (XBM[:, tl, :, :], [("XBM", tl)])))
                                        mlp_ln_pend.append(fin)
                                    if tg == 3:
                                        while mlp_ln_pend:
                                            mlp_ln_pend.pop(0)()
                        items.append((load_dn, comp_dn))
            if not last_layer:
                def comp_end(s_):
                    while mlp_pend:
                        ln_transposes(mlp_pend.pop(0))
                    P.fence("dve", lambda e: e.memset(DUMMY[:, 3:4], 0.0), ("X",), AB_PREF)
                    P.fence("dve", lambda e: e.memset(DUMMY[:, 4:5], 0.0), ("HT",), ("ATT",))
                items.append((None, comp_end))


        for l in range(DEPTH):
            layer(l)

        slots = [None] * len(items)
        LOOK = 2
        for i0 in range(min(LOOK, len(items))):
            if items[i0][0] is not None:
                slots[i0] = items[i0][0]()
        for i, (ld, comp) in enumerate(items):
            if stop_after is not None and i >= stop_after:
                break
            if i + LOOK < len(items) and items[i + LOOK][0] is not None:
                slots[i + LOOK] = items[i + LOOK][0]()
            comp(slots[i])
            if stop_after is not None and i + 1 >= stop_after:
                break
        P.emit()
    return nc


def _const_tables():
    cb = np.zeros((128, NCB), np.float32)
    p = np.arange(128)[:, None]
    f = np.arange(128)[None, :]
    cb[:, CB_ID:CB_ID + 128] = (p == f)
    cb[:, CB_MASK:CB_MASK + 128] = (p >= f)
    cb[:, CB_MASK + 128:CB_MASK + 256] = (p <= f)
    cb[:, CB_ONES:CB_ONES + 128] = 1.0
    col = np.arange(32)
    ci = col % 4
    cbb = (col % 16) // 4
    cb[:, CB_MP0:CB_MP0 + 32] = (np.arange(128)[:, None] >= ci[None, :])
    kk = np.arange(16)
    kb_, ki = kk // 4, kk % 4
    same_b = (kb_[:, None] == cbb[None, :])
    cb[0:16, CB_MN0:CB_MN0 + 32] = same_b & (ki[:, None] <= ci[None, :])
    cb[0:16, CB_MN1:CB_MN1 + 32] = same_b & (ki[:, None] == ci[None, :])
    cb[0:16, CB_SGM:CB_SGM + 16] = (kb_[:, None] == kb_[None, :]) & (ki[:, None] <= ki[None, :])

    cf = np.zeros((128, NCF), np.float32)
    half = 8
    inv = (np.float32(500000.0) ** (-(np.arange(half, dtype=np.float32) / np.float32(half)))).astype(np.float32)

    def cs(pos):
        ang = (pos.astype(np.float32)[:, None] * inv[None, :]).astype(np.float32)
        return np.cos(ang.astype(np.float64)).astype(np.float32), np.sin(ang.astype(np.float64)).astype(np.float32)
    for t in range(NT):
        c_, s_ = cs(np.arange(128) + 128 * t)
        cf[:, CF_CS1 + t * 16: CF_CS1 + t * 16 + 8] = c_
        cf[:, CF_CS1 + t * 16 + 8: CF_CS1 + t * 16 + 16] = s_
        cf[:, CF_CS2 + t * 16: CF_CS2 + t * 16 + 8] = -s_
        cf[:, CF_CS2 + t * 16 + 8: CF_CS2 + t * 16 + 16] = c_
    c_, s_ = cs(PAST + (np.arange(16) % 4))
    cf[0:16, CF_SS1:CF_SS1 + 8] = c_
    cf[0:16, CF_SS1 + 8:CF_SS1 + 16] = s_
    cf[0:16, CF_SS2:CF_SS2 + 8] = -s_
    cf[0:16, CF_SS2 + 8:CF_SS2 + 16] = c_
    return cb, cf


_CACHE = {}


def kernel(x_prompt, x_sample, cache_kv_w128, cache_kv_w512, cache_kv_w2048, cache_mem_kv, mem_prompt,
           w_in, sgu_ln_g, sgu_ln_b, w_spatial, b_spatial, w_mix_out, ln1_g, ln1_b,
           w_xq, w_xkv, w_xo, ln2_g, ln2_b, w_up, w_down, ln3_g, ln3_b, _stop_after=None):
    f = lambda a: np.ascontiguousarray(np.asarray(a, dtype=np.float32))
    key = ("nc", _stop_after)
    if key not in _CACHE:
        _CACHE[key] = build_program(_stop_after)
    nc = _CACHE[key]
    cb, cf = _const_tables()
    shared = {
        "w_in": f(w_in), "sgu_ln_g": f(sgu_ln_g), "sgu_ln_b": f(sgu_ln_b), "w_spatial": f(w_spatial),
        "b_spatial": f(b_spatial), "w_mix_out": f(w_mix_out), "ln1_g": f(ln1_g), "ln1_b": f(ln1_b),
        "w_xq": f(w_xq), "w_xkv": f(w_xkv), "w_xo": f(w_xo), "ln2_g": f(ln2_g), "ln2_b": f(ln2_b),
        "w_up": f(w_up), "w_down": f(w_down), "ln3_g": f(ln3_g), "ln3_b": f(ln3_b), "cb": cb, "cf": cf,
    }
    xpr, xsa = f(x_prompt), f(x_sample)
    c128, c512, c2048, cm, mp = f(cache_kv_w128), f(cache_kv_w512), f(cache_kv_w2048), f(cache_mem_kv), f(mem_prompt)
    in_maps = []
    for c in range(NCORES):
        bs = slice(4 * c, 4 * c + 4)
        m = dict(shared)
        m["xp"] = xpr[c]
        m["xs"] = np.ascontiguousarray(xsa[bs].reshape(NS, D))
        m["c128"] = np.ascontiguousarray(c128[:, bs].reshape(DEPTH, 4, 128, 512))
        m["c512"] = np.ascontiguousarray(c512[:, bs].reshape(DEPTH, 4, 512, 512))
        m["c2048"] = np.ascontiguousarray(c2048[:, bs].reshape(DEPTH, 4, 2048, 512))
        m["cmem"] = np.ascontiguousarray(cm[:, bs].reshape(DEPTH, 4, 256, 2048))
        m["memp"] = mp[c]
        in_maps.append(m)
    res = run_bass_kernel_spmd(nc, in_maps, core_ids=list(range(NCORES)))
    R = res.results

    def cat_p(name, shape_tail):
        return np.stack([np.asarray(R[c][name], np.float32).reshape((DEPTH,) + shape_tail) for c in range(NCORES)], axis=1)

    def cat_s(name, shape_tail):
        return np.concatenate([np.asarray(R[c][name], np.float32).reshape((DEPTH, 4, 4) + shape_tail) for c in range(NCORES)], axis=1)

    y_p = np.stack([np.asarray(R[c]["yp"], np.float32) for c in range(NCORES)], axis=0)
    y_s = np.concatenate([np.asarray(R[c]["ys"], np.float32).reshape(4, 4, D) for c in range(NCORES)], axis=0)
    return (y_p, y_s,
            cat_p("kv128p", (128, 2, 4, 64)), cat_p("kv512p", (512, 2, 4, 64)), cat_p("kv2048p", (2048, 2, 4, 64)),
            cat_p("memkvp", (256, 2, 4, 256)),
            cat_s("kv128s", (2, 4, 64)), cat_s("kv512s", (2, 4, 64)), cat_s("kv2048s", (2, 4, 64)),
            cat_s("chunkv", (256,)))
```

```python
import contextlib
import numpy as np
import concourse.bass as bass
import concourse.mybir as mybir
from concourse.bass_utils import run_bass_kernel_spmd

F32 = mybir.dt.float32
BF16 = mybir.dt.bfloat16
AF = mybir.ActivationFunctionType
ALU = mybir.AluOpType

D = 1024
T = 2048
NT = 16
DEPTH = 2
NS = 16
NCORES = 8
IN_COLS = 2816
DFF = 4096
ALPHA = float((2 * DEPTH) ** 0.25)
EPS = 1e-5
PAST = 8192
GROUPS = ((128, 1), (512, 4), (2048, 16))
NSLOT = 4
N_DMA_SEMS = 12
ENGS = ("pe", "act", "dve", "pool", "sp")

CB_ID, CB_MASK, CB_ONES, CB_MP0, CB_MN0, CB_MN1, CB_SGM, NCB = 0, 128, 384, 512, 544, 576, 608, 624
CF_CS1, CF_CS2, CF_SS1, CF_SS2, NCF = 0, 256, 512, 528, 544


class Op:
    __slots__ = ("eng", "fn", "reads", "writes", "dma", "idx", "deps", "sig", "semslot", "semcnt", "n_dma", "raw")


class Prog:
    def __init__(self, nc, same_engine_sync=True):
        self.nc = nc
        self.ops = []
        self.same_engine_sync = same_engine_sync
        self.last_writer = {}
        self.readers = {}
        self.final_ops = []
        self.fences = {}

    def op(self, eng, fn, reads=(), writes=(), dma=False, n_dma=1, final=False):
        o = Op()
        o.eng, o.fn, o.reads, o.writes, o.dma, o.n_dma = eng, fn, tuple(reads), tuple(writes), dma, n_dma
        o.sig = None
        o.semslot = None
        o.semcnt = None
        o.idx = len(self.ops)
        deps = set()
        for k in o.reads:
            w = self.last_writer.get(k)
            if w is not None:
                deps.add(w)
            f = self.fences.get(k[0])
            if f is not None:
                deps.add(f)
            if k[0] == "ps":
                for r in self.readers.get(k, ()):
                    if self.ops[r].eng != eng:
                        deps.add(r)
        for k in o.writes:
            w = self.last_writer.get(k)
            if w is not None:
                deps.add(w)
            for r in self.readers.get(k, ()):
                deps.add(r)
            f = self.fences.get(k[0])
            if f is not None:
                deps.add(f)
        deps.discard(o.idx)
        o.deps = sorted(deps)
        o.raw = set()
        for k in o.reads:
            w = self.last_writer.get(k)
            if w is not None:
                o.raw.add(w)
        for k in o.reads:
            self.readers.setdefault(k, []).append(o.idx)
        for k in o.writes:
            self.last_writer[k] = o.idx
            self.readers[k] = []
        self.ops.append(o)
        if final:
            self.final_ops.append(o.idx)
        return o

    def fence(self, eng, fn, old_prefixes, new_prefixes):
        deps = set()
        for k, w in self.last_writer.items():
            if k[0] in old_prefixes and w is not None:
                deps.add(w)
        for k, rs in self.readers.items():
            if k[0] in old_prefixes:
                deps.update(rs)
        for p in old_prefixes:
            f = self.fences.get(p)
            if f is not None:
                deps.add(f)
        o = Op()
        o.eng, o.fn, o.reads, o.writes, o.dma, o.n_dma = eng, fn, (), (), False, 1
        o.sig = None
        o.semslot = None
        o.semcnt = None
        o.idx = len(self.ops)
        o.deps = sorted(deps)
        o.raw = set(deps)
        self.ops.append(o)
        for p in new_prefixes:
            self.fences[p] = o.idx
        for k in [k for k in self.last_writer if k[0] in old_prefixes]:
            del self.last_writer[k]
        for k in [k for k in self.readers if k[0] in old_prefixes]:
            del self.readers[k]
        return o

    def pe(self, fn, reads=(), writes=()):
        return self.op("pe", fn, reads, writes)

    def act(self, fn, reads=(), writes=()):
        return self.op("act", fn, reads, writes)

    def dve(self, fn, reads=(), writes=()):
        return self.op("dve", fn, reads, writes)

    def pool(self, fn, reads=(), writes=()):
        return self.op("pool", fn, reads, writes)

    def dma(self, q, fn, reads=(), writes=(), final=False, n_dma=1):
        return self.op(q, fn, reads, writes, dma=True, n_dma=n_dma, final=final)

    def emit(self):
        nc = self.nc
        ops = self.ops
        needed = [False] * len(ops)
        for o in ops:
            for d in o.deps:
                p = ops[d]
                if p.dma or p.eng != o.eng or (self.same_engine_sync and p.eng != "pe" and d in o.raw):
                    needed[d] = True
        for i in self.final_ops:
            needed[i] = True
        with contextlib.ExitStack() as st:
            eng_sem = {e: st.enter_context(nc.semaphore("s_" + e)) for e in ENGS}
            dma_sems = {q: [st.enter_context(nc.semaphore("d_%s%d" % (q, i))) for i in range(N_DMA_SEMS)]
                        for q in ("sp", "act", "pool")}
            block = st.enter_context(nc.Block())
            cnt = {e: 0 for e in ENGS}
            dcnt = {q: [0] * N_DMA_SEMS for q in dma_sems}
            drr = {q: 0 for q in dma_sems}
            for o in ops:
                if o.dma:
                    q = o.eng
                    s = drr[q] % N_DMA_SEMS
                    drr[q] += 1
                    o.semslot = s
                    o.semcnt = dcnt[q][s]
                    dcnt[q][s] += 16 * o.n_dma
                    o.sig = (dma_sems[q][s], dcnt[q][s])
                elif needed[o.idx]:
                    cnt[o.eng] += 1
                    o.sig = (eng_sem[o.eng], cnt[o.eng])
            per_eng = {e: [o for o in ops if o.eng == e] for e in ENGS}
            final_ops = [ops[i] for i in self.final_ops]
            same = self.same_engine_sync

            def run(e, engobj):
                waited = {}

                def wait(sem, val):
                    key = id(sem)
                    if waited.get(key, 0) >= val:
                        return
                    waited[key] = val
                    engobj.wait_ge(sem, val)

                for o in per_eng[e]:
                    if o.dma and o.semcnt > 0:
                        wait(dma_sems[e][o.semslot], o.semcnt)
                    for d in o.deps:
                        p = ops[d]
                        if p.sig is None:
                            continue
                        if (not p.dma) and p.eng == e and (e == "pe" or not same or d not in o.raw):
                            continue
                        wait(*p.sig)
                    ins = o.fn(engobj)
                    if o.dma:
                        if isinstance(ins, (list, tuple)):
                            assert len(ins) == o.n_dma
                            for i_ in ins:
                                i_.then_inc(o.sig[0], 16)
                        else:
                            assert o.n_dma == 1
                            ins.then_inc(o.sig[0], 16)
                    elif o.sig is not None:
                        ins.then_inc(o.sig[0], 1)
                if e == "sp":
                    for o in final_ops:
                        wait(*o.sig)

            @block.tensor
            def _(eng):
                run("pe", eng)

            @block.scalar
            def _(eng):
                run("act", eng)

            @block.vector
            def _(eng):
                run("dve", eng)

            @block.gpsimd
            def _(eng):
                run("pool", eng)

            @block.sync
            def _(eng):
                run("sp", eng)


def psk(bank, c0, n):
    return [("ps", bank)]


def build_program(stop_after=None):
    nc = bass.Bass("TRN2", target_bir_lowering=False)

    def din(name, shape):
        return nc.dram_tensor(name, list(shape), F32, kind="ExternalInput").ap()

    def dout(name, shape):
        return nc.dram_tensor(name, list(shape), F32, kind="ExternalOutput").ap()

    xp = din("xp", [T, D])
    xs = din("xs", [NS, D])
    cpast = [din("c128", [DEPTH, 4, 128, 512]), din("c512", [DEPTH, 4, 512, 512]), din("c2048", [DEPTH, 4, 2048, 512])]
    cmem = din("cmem", [DEPTH, 4, 256, 2048])
    memp = din("memp", [256, D])
    w_in = din("w_in", [DEPTH, D, IN_COLS])
    sgu_g = din("sgu_ln_g", [DEPTH, 256])
    sgu_b = din("sgu_ln_b", [DEPTH, 256])
    w_sp = din("w_spatial", [DEPTH, 4, 128, 128])
    b_sp = din("b_spatial", [DEPTH, 4, 128])
    w_mix = din("w_mix_out", [DEPTH, D, D])
    ln_g = [din("ln1_g", [DEPTH, D]), din("ln2_g", [DEPTH, D]), din("ln3_g", [DEPTH, D])]
    ln_b = [din("ln1_b", [DEPTH, D]), din("ln2_b", [DEPTH, D]), din("ln3_b", [DEPTH, D])]
    w_xq = din("w_xq", [DEPTH, D, D])
    w_xkv = din("w_xkv", [DEPTH, D, 2 * D])
    w_xo = din("w_xo", [DEPTH, D, D])
    w_up = din("w_up", [DEPTH, D, DFF])
    w_down = din("w_down", [DEPTH, DFF, D])
    cb_d = din("cb", [128, NCB])
    cf_d = din("cf", [128, NCF])

    yp = dout("yp", [T, D])
    ys = dout("ys", [NS, D])
    kvp = [dout("kv128p", [DEPTH, 128, 512]), dout("kv512p", [DEPTH, 512, 512]), dout("kv2048p", [DEPTH, 2048, 512])]
    memkvp = dout("memkvp", [DEPTH, 256, 2 * D])
    kvs = [dout("kv128s", [DEPTH, NS, 512]), dout("kv512s", [DEPTH, NS, 512]), dout("kv2048s", [DEPTH, NS, 512])]
    chunkv = dout("chunkv", [DEPTH, NS, 256])
    xscr = nc.dram_tensor("xscr", [T, D], F32).ap()

    st = contextlib.ExitStack()

    def sb(name, shape, dt):
        return st.enter_context(nc.sbuf_tensor(name, list(shape), dt))

    with st:
        XT = sb("XT", [128, 8, T], BF16)
        XST = sb("XST", [128, 8, NS], BF16)
        XS = sb("XS", [NS, D], F32)
        Yt = sb("Y", [128, 16384], BF16)
        Zt = sb("Z", [128, 16384], F32)
        St = sb("S", [128, 6144], F32)
        WR = sb("WR", [128, NSLOT, 4096], BF16)
        LNT = sb("LNT", [128, 2, 1024], F32)
        CB = sb("CB", [128, NCB], BF16)
        CF = sb("CF", [128, NCF], F32)
        XB = sb("XB", [128, 2, 1024], BF16)
        STAT = sb("STAT", [128, 4, 16], F32)
        STATG = sb("STATG", [128, 4, 16], F32)
        STATM = sb("STATM", [128, 5, 16], F32)
        HTS = sb("HTS", [128, 32, NS], BF16)
        SDEN = sb("SDEN", [128, 32], F32)
        SO = sb("SO", [128, 3, 32], F32)
        ATTS = sb("ATTS", [128, 8, NS], BF16)
        QXST = sb("QXST", [128, 8, NS], BF16)
        SPM = sb("SPM", [128, 2, 16], BF16)
        SRD2 = sb("SRD2", [128, 16], F32)
        GVF = sb("GVF", [NS, 256], F32)
        DUMMY = sb("DUMMY", [128, 8], F32)
        EPSB = sb("EPSB", [128, 1], F32)
        MHALF = sb("MHALF", [128, 1], F32)
        PSA = st.enter_context(nc.psum_tensor("PSA", [128, 4096], F32))
        Zb = Zt[:].bitcast(BF16)
        Sb = St[:].bitcast(BF16)

        def carve(base_f32, base_b16, state, limit, shape, dt):
            npart = shape[0]
            nel = 1
            for d_ in shape[1:]:
                nel *= d_
            esz = 4 if dt == F32 else 2
            o = state["o"]
            state["o"] = o + ((nel * esz + 31) // 32) * 32
            assert state["o"] <= limit, (state["o"], limit)
            if dt == F32:
                v = base_f32[0:npart, o // 4: o // 4 + nel]
            else:
                v = base_b16[0:npart, o // 2: o // 2 + nel]
            fd = shape[1:]
            if len(fd) == 2:
                v = v.rearrange("p (a b) -> p a b", b=fd[1])
            elif len(fd) == 3:
                v = v.rearrange("p (a b c) -> p a b c", b=fd[1], c=fd[2])
            elif len(fd) == 4:
                v = v.rearrange("p (a b c d) -> p a b c d", b=fd[1], c=fd[2], d=fd[3])
            return v

        zstate = {"o": 7168 * 4}
        AB_NAMES = []

        def zal(name, shape, dt):
            AB_NAMES.append(name)
            return carve(Zt, Zb, zstate, 65536, shape, dt)

        hole1 = {"o": 4096}
        hole2 = {"o": 14336}

        def zal1(name, shape, dt):
            AB_NAMES.append(name)
            return carve(Zt, Zb, hole1, 8192, shape, dt)

        def zal2(name, shape, dt):
            AB_NAMES.append(name)
            return carve(Zt, Zb, hole2, 20480, shape, dt)

        PKV = zal("PKV", [128, 4, 4, 2, 128], BF16)
        PKT = zal("PKT", [128, 4, 4, 128], BF16)
        QKB = zal1("QKB", [128, 4, 256], BF16)
        RT = zal1("RT", [128, 4, 2, 64], F32)
        KF = zal2("KF", [128, 2, 128], F32)
        VF = zal2("VF", [128, 2, 128], F32)
        PT = zal("PT", [128, 3, 512], BF16)
        SGW = zal("SGW", [128, 4, 128], BF16)
        SGWS = zal("SGWS", [NS, 4, NS], BF16)
        SGWSF = zal("SGWSF", [NS, 4, NS], F32)
        SGL = zal2("SGL", [128, 4, 128], BF16)
        GAM = zal("GAM", [128, 2, 256], F32)
        BSP = zal("BSP", [128, 4, 128], F32)
        BSPS = zal("BSPS", [128, 4, NS], F32)
        UG = zal("UG", [128, 3, 256], F32)
        GG = zal("GG", [128, 3, 256], F32)
        GTMP = zal("GTMP", [128, 3, 256], F32)
        GV = zal("GV", [128, 3, 256], BF16)
        STMP = zal2("STMP", [128, 2, 128], F32)
        SQKB = zal("SQKB", [NS, 256], BF16)
        SRT = zal("SRT", [NS, 2, 64], F32)
        SKF = zal("SKF", [NS, 128], F32)
        SVF = zal("SVF", [NS, 128], F32)
        SVB = zal("SVB", [NS, 128], BF16)
        SQT = zal("SQT", [128, NS], BF16)
        SKT = zal("SKT", [128, NS], BF16)
        SPP = zal("SPP", [128, 32], BF16)
        SPN = zal("SPN", [NS, 32], BF16)

        def sal(slot, shape, dt):
            return carve(St, Sb, {"o": slot * 4096}, 24576, shape, dt)

        MEMT = sal(0, [128, 8, 256], BF16)
        PM = sal(0, [128, 2, 1024], BF16)
        CMK = sal(0, [128, 2, 1024], BF16)
        RELU = sal(0, [128, 4, 512], F32)
        MKF = sal(1, [128, 2, 512], F32)
        RDEN = sal(1, [128, 2, 512], F32)
        MEMKT = sal(2, [128, 8, 256], BF16)
        MEMV = sal(3, [128, 2, 1024], BF16)
        SCV = sal(4, [128, 2, 1024], BF16)
        MKTS = sal(5, [128, 8, 256], BF16)
        XBM = sal(2, [128, 4, 8, 128], BF16)
        S_M, S_X, S_3 = ("MEMT", "MKF"), ("PM", "CMK", "RDEN", "MEMKT", "MEMV"), ("RELU", "XBM")
        mlp_pend = []
        mlp_ln_pend = []
        XSB = XB[0:NS, 0, :]

        PSB = PSA[:].bitcast(BF16)
        Zb = Zt[:].bitcast(BF16)
        X = Zt[:].rearrange("p (t d) -> p t d", d=D)
        KT = [Zb[:, i * 2048:(i + 1) * 2048] for i in range(2)]
        VAUG = [Zb[:, 4096 + i * 3072: 4096 + i * 3072 + 2048].rearrange("p (c v) -> p c v", v=128) for i in range(2)]
        DEN = Zt[:, 5120:5120 + 2048]
        ATT = Yt[:, 0:12288].rearrange("p (c t) -> p c t", t=T)
        SGUT = Yt[:, 12288:16384].rearrange("p (c t) -> p c t", t=T)
        ATTALL = Yt[:].rearrange("p (c t) -> p c t", t=T)
        QXT = Yt[:].rearrange("p (c t) -> p c t", t=T)
        HT = Yt[:].rearrange("p (c t) -> p c t", t=512)
        IDENT = CB[:, CB_ID:CB_ID + 128]
        MASK = CB[:, CB_MASK:CB_MASK + 256]
        ONES = CB[:, CB_ONES:CB_ONES + 128]
        AB_PREF = tuple(["KT", "VAUG", "DEN"] + AB_NAMES)
        Y_AB = ("ATT",)

        P = Prog(nc)

        def bank(b, c0=0, n=512):
            return PSA[:, b * 512 + c0: b * 512 + c0 + n]

        def bankb(b, c0=0, n=1024):
            return PSB[:, b * 1024 + c0: b * 1024 + c0 + n]

        P.dma("pool", lambda e: e.dma_start(out=CB[:], in_=cb_d), writes=[("CB",)])
        P.dma("sp", lambda e: e.dma_start(out=CF[:], in_=cf_d), writes=[("CF",)])
        P.dma("sp", lambda e: e.dma_start(out=XS[:], in_=xs), writes=[("XS",)])
        P.pool(lambda e: e.memset(EPSB[:], EPS), writes=[("EPSB",)])
        P.pool(lambda e: e.memset(MHALF[:], -0.5), writes=[("EPSB",)])

        ring = {"n": 0}

        def take_slots(k):
            s = [(ring["n"] + i) % NSLOT for i in range(k)]
            ring["n"] += k
            return s

        def wview(slot, ncols, nk=8):
            return WR[:, slot, 0:nk * ncols].rearrange("p (k n) -> p k n", n=ncols)

        def wsrc(w2d, c0, c1, k0=0, nk=8):
            return w2d[k0 * 128:(k0 + nk) * 128, c0:c1].rearrange("(k p) n -> p k n", p=128)

        tr = {"n": 0, "banks": [6, 7]}

        def transpose_to(srcs, dsts, nrows, reads, writes):
            assert len(srcs) <= 8
            B = tr["banks"][tr["n"] % len(tr["banks"])]
            tr["n"] += 1
            idn = IDENT[0:nrows, 0:nrows]

            def f(e, srcs=srcs, B=B, idn=idn):
                ins = None
                for i, a_ in enumerate(srcs):
                    ins = e.transpose(out=bankb(B, i * nrows, nrows), in_=a_, identity=idn)
                return ins
            P.pe(f, reads=list(reads) + [("CB",)], writes=[("ps", B)])
            for (d, i0, n) in dsts:
                src = bankb(B, i0 * nrows, n * nrows)
                if n > 1:
                    src = src.rearrange("p (a r) -> p a r", r=nrows)
                P.act(lambda e, d=d, src=src: e.copy(out=d, in_=src), reads=[("ps", B)], writes=writes)

        def mm_group(out, pairs, reads, writes):
            n = len(pairs)

            def f(e, out=out, pairs=pairs, n=n):
                ins = None
                for i, (l, r) in enumerate(pairs):
                    ins = e.matmul(out, lhsT=l, rhs=r, start=(i == 0), stop=(i == n - 1))
                return ins
            P.pe(f, reads=reads, writes=writes)

        def make_xt_from_dram(src, t, buf):
            stg = XBM[:, t % 4, :, :]
            P.dma("pool", lambda e: e.dma_start(out=stg, in_=src[t * 128:(t + 1) * 128, :].rearrange("p (k c) -> p k c", c=128)),
                  writes=[("XBM", t % 4)])
            transpose_to([stg[:, k, :] for k in range(8)],
                         [(XT[:, :, t * 128:(t + 1) * 128], 0, 8)], 128,
                         reads=[("XBM", t % 4)], writes=[("XT", t)])

        import os
        KD = os.environ.get("KDBG", "xs")
        if "x" in KD:
            for t in range(int(os.environ.get("KNT", NT))):
                make_xt_from_dram(xp, t, t % 2)
        if "s" in KD:
            P.dma("pool", lambda e: e.dma_start(out=XSB[:], in_=xs), writes=[("XB", 0)])
            transpose_to([XSB[:, k * 128:(k + 1) * 128] for k in range(8)], [(XST[:], 0, 8)], NS,
                         reads=[("XB", 0)], writes=[("XST",)])

        ln_rr = {"n": 0}

        def rstd_ops(stt, sk, npart):
            P.pool(lambda e: e.tensor_tensor(out=stt[:, 15:16], in0=stt[:, 13:14], in1=EPSB[0:npart, :], op=ALU.add),
                   reads=[sk, ("EPSB",)], writes=[sk])
            P.pool(lambda e: e.tensor_tensor(out=stt[:, 14:15], in0=stt[:, 15:16], in1=MHALF[0:npart, :], op=ALU.pow),
                   reads=[sk, ("EPSB",)], writes=[sk])

        def ln_s1_stats(xv, stt, sk, npart):
            P.dve(lambda e: e.bn_aggr(stt[:, 12:14], stt[:, 0:12]), reads=[sk], writes=[sk])
            rstd_ops(stt, sk, npart)

        def ln_s1(xv, ps2, ps_keys, npart, xkeys, stt, sk):
            P.dve(lambda e: e.scalar_tensor_tensor(out=xv, in0=xv, scalar=ALPHA, in1=ps2, op0=ALU.mult, op1=ALU.add),
                  reads=list(ps_keys) + list(xkeys), writes=xkeys)
            for hf in range(2):
                P.dve(lambda e, hf=hf: e.bn_stats(stt[:, hf * 6:(hf + 1) * 6], xv[:, hf * 512:(hf + 1) * 512]),
                      reads=xkeys, writes=[sk])
            ln_s1_stats(xv, stt, sk, npart)

        def ln_s2(xv, npart, xkeys, stt, sk):
            P.dve(lambda e: e.scalar_tensor_tensor(out=xv, in0=xv, scalar=stt[:, 12:13], in1=LNT[0:npart, 0, :],
                                                   op0=ALU.subtract, op1=ALU.mult),
                  reads=[sk, ("LNT",)] + list(xkeys), writes=xkeys)

        def ln_s3(xv, npart, xkeys, stt, sk):
            P.dve(lambda e: e.scalar_tensor_tensor(out=xv, in0=xv, scalar=stt[:, 14:15], in1=LNT[0:npart, 1, :],
                                                   op0=ALU.mult, op1=ALU.add),
                  reads=[sk, ("LNT",)] + list(xkeys), writes=xkeys)

        def load_ln_tables(l, which):
            P.dma("sp", lambda e: e.dma_start(out=LNT[:, 0, :], in_=ln_g[which][l].partition_broadcast(128)),
                  writes=[("LNT",)])
            P.dma("sp", lambda e: e.dma_start(out=LNT[:, 1, :], in_=ln_b[which][l].partition_broadcast(128)),
                  writes=[("LNT",)])

        def ln_stage_tile(t, last_layer, final_ln, stage):
            xv = X[:, t, :]
            if final_ln and last_layer:
                P.dma("sp", lambda e: e.dma_start(out=yp[t * 128:(t + 1) * 128, :], in_=xv), reads=[("X", t)], final=True)
                return None
            sv, skeys = stage
            P.act(lambda e: e.copy(out=sv, in_=xv.rearrange("p (k c) -> p k c", c=128)), reads=[("X", t)], writes=skeys)
            if final_ln:
                P.dma("sp", lambda e: e.dma_start(out=xscr[t * 128:(t + 1) * 128, :], in_=xv), reads=[("X", t)],
                      writes=[("xscr", t)])
            return (t, sv, skeys)

        def ln_stage_sample(last_layer, final_ln):
            if final_ln and last_layer:
                P.dma("sp", lambda e: e.dma_start(out=ys, in_=XS[:]), reads=[("XS",)], final=True)
                return None
            P.act(lambda e: e.copy(out=XSB[:], in_=XS[:]), reads=[("XS",)], writes=[("XB", 0)])
            return ("s", None, None)

        def ln_transposes(item):
            if item is None:
                return
            t, sv, skeys = item
            if t == "s":
                transpose_to([XSB[:, k * 128:(k + 1) * 128] for k in range(8)], [(XST[:], 0, 8)], NS,
                             reads=[("XB", 0)], writes=[("XST",)])
            else:
                transpose_to([sv[:, k, :] for k in range(8)], [(XT[:, :, t * 128:(t + 1) * 128], 0, 8)], 128,
                             reads=skeys, writes=[("XT", t)])

        pj_rr = {"n": 0}

        pj_banks = {"b": [0, 1]}

        def pj_slot():
            bl = pj_banks["b"]
            s = bl[pj_rr["n"] % len(bl)]
            pj_rr["n"] += 1
            return s

        items = []

        def layer(l):
            last_layer = (l == DEPTH - 1)

            def sgu_consts():
                P.dma("sp", lambda e: e.dma_start(out=GAM[:, 0, :], in_=sgu_g[l].partition_broadcast(128)), writes=[("GAM",)])
                P.dma("sp", lambda e: e.dma_start(out=GAM[:, 1, :], in_=sgu_b[l].partition_broadcast(128)), writes=[("GAM",)])
                P.dma("sp", lambda e: e.dma_start(out=BSP[:], in_=b_sp[l].partition_broadcast(128)), writes=[("BSP",)])
                srcb = bass.AP(b_sp.tensor, b_sp[l].offset, [[0, 128], [128, 4], [1, 4]])
                P.dma("sp", lambda e: [e.dma_start(out=BSPS[:, :, b_ * 4:(b_ + 1) * 4], in_=srcb) for b_ in range(4)],
                      writes=[("BSPS",)], n_dma=4)
                P.dma("pool", lambda e: e.dma_start(out=SGL[:], in_=w_sp[l].rearrange("g t s -> t g s")), writes=[("SGL",)])
                transpose_to([SGL[:, g, :] for g in range(4)], [(SGW[:], 0, 4)], 128,
                             reads=[("SGL",)], writes=[("SGW",)])
                P.dve(lambda e: e.tensor_tensor(out=SGW[:], in0=SGW[:],
                                                in1=MASK[:, 128:256].unsqueeze(1).broadcast_to([128, 4, 128]), op=ALU.mult),
                      reads=[("SGW",), ("CB",)], writes=[("SGW",)])
                P.pool(lambda e: e.memset(SGWSF[:], 0.0), writes=[("SGWSF",)])
                for b in range(4):
                    def f(e, b=b):
                        return [e.dma_start(out=SGWSF[b * 4:(b + 1) * 4, g_, b * 4:(b + 1) * 4],
                                            in_=w_sp[l, g_, 0:4, 0:4].rearrange("t s -> s t"),
                                            allow_slow_non_contiguous=True) for g_ in range(4)]
                    P.dma("sp", f, reads=[("SGWSF",)], writes=[("SGWSF",)], n_dma=4)
                P.dve(lambda e: e.tensor_tensor(out=SGWS[:], in0=SGWSF[:],
                                                in1=CB[0:NS, CB_SGM:CB_SGM + NS].unsqueeze(1).broadcast_to([NS, 4, NS]),
                                                op=ALU.mult),
                      reads=[("SGWSF",), ("CB",)], writes=[("SGWS",)])

            def rope_ops(psv, nh, npart, cs1, cs2, rt1, rt2, ps_keys, rtkey, extra_reads=()):
                v = psv.rearrange("p (h d) -> p h d", d=64)
                x1 = v[:, :, 0:8].unsqueeze(2).broadcast_to([npart, nh, 2, 8])
                x2 = v[:, :, 8:16].unsqueeze(2).broadcast_to([npart, nh, 2, 8])
                c1 = cs1.rearrange("p (a d) -> p a d", d=8).unsqueeze(1).broadcast_to([npart, nh, 2, 8])
                c2 = cs2.rearrange("p (a d) -> p a d", d=8).unsqueeze(1).broadcast_to([npart, nh, 2, 8])
                o1 = rt1.rearrange("p (h a d) -> p h a d", a=2, d=8)
                o2 = rt2.rearrange("p (h a d) -> p h a d", a=2, d=8)
                P.dve(lambda e: e.tensor_tensor(out=o1, in0=x1, in1=c1, op=ALU.mult), reads=list(ps_keys) + [("CF",)] + list(extra_reads),
                      writes=[rtkey])
                P.dve(lambda e: e.tensor_tensor(out=o2, in0=x2, in1=c2, op=ALU.mult), reads=list(ps_keys) + [("CF",)] + list(extra_reads),
                      writes=[rtkey])
                return o1, o2

            for j in range(2):
                for g in range(3):
                    win, dil = GROUPS[g]
                    nb = NT // dil
                    c = 2 * g + j
                    kb = 0
                    nrows = min(win, T)
                    row0 = T - nrows

                    def load_ab(g=g, j=j):
                        s = take_slots(1)[0]
                        cq = 256 * g + 128 * j

                        def f(e):
                            W = wview(s, 384)
                            return [e.dma_start(out=W[:, :, i * 128:(i + 1) * 128],
                                                in_=wsrc(w_in[l], 768 * i + cq, 768 * i + cq + 128)) for i in range(3)]
                        P.dma("pool", f, writes=[("WR", s)], n_dma=3)
                        return s

                    def comp_ab(s, g=g, j=j, win=win, dil=dil, nb=nb, c=c, kb=kb, nrows=nrows, row0=row0):
                        W = wview(s, 384)
                        tr["banks"] = [7]
                        ktb = KT[kb]
                        vab = VAUG[kb]

                        def tok(r, b):
                            return slice(r + dil * 128 * b, r + dil * 128 * b + dil * 127 + 1, dil)

                        def attkeys(t):
                            return [("ATT", c, r, t // dil) for r in range(dil)]

                        def ktkeys(t):
                            return [("KT", kb, r, t // dil) for r in range(dil)]

                        ni = 1 if g == 0 else 4
                        srcp = cpast[g][l].rearrange("b (m s) c -> m b s c", s=dil)

                        def fpast(e):
                            return [e.dma_start(out=PKV[:, b_, 0:ni, kv, :],
                                                in_=srcp[:, b_, 0:ni, kv * 256 + j * 128: kv * 256 + j * 128 + 128])
                                    for kv in range(2) for b_ in range(4)]
                        P.dma("pool", fpast, writes=[("PKV",)], n_dma=8)
                        KAB = int(os.environ.get("KAB", "9"))
                        if KAB <= 1:
                            return
                        def qk_tile(t):
                            ps = pj_slot()
                            bk, c0 = ps, 0
                            out = bank(bk, c0, 256)
                            pk = psk(bk, c0, 256)
                            mm_group(out, [(XT[:, k, t * 128:(t + 1) * 128], W[:, k, 0:256]) for k in range(8)],
                                     reads=[("XT", t), ("WR", s)], writes=pk)
                            qb = t % 4
                            P.act(lambda e: e.copy(out=QKB[:, qb, :], in_=out), reads=pk, writes=[("QKB", qb)])
                            KB = os.environ.get("KB", "mrkt")
                            if "r" not in KB:
                                return qb
                            cs1 = CF[:, CF_CS1 + t * 16: CF_CS1 + t * 16 + 16]
                            cs2 = CF[:, CF_CS2 + t * 16: CF_CS2 + t * 16 + 16]
                            o1, o2 = rope_ops(out, 4, 128, cs1, cs2, RT[:, qb, 0, :], RT[:, qb, 1, :], pk, ("RT", qb), [("QKB", qb)])
                            dq = QKB[:, qb, :].rearrange("p (h d) -> p h d", d=64)[:, :, 0:16].rearrange("p h (a d) -> p h a d", d=8)
                            P.dve(lambda e: e.tensor_tensor(out=dq, in0=o1, in1=o2, op=ALU.add),
                                  reads=[("RT", qb), ("QKB", qb)], writes=[("QKB", qb)])
                            if t * 128 >= row0 and "k" in KB:
                                kfb = qb % 2
                                P.act(lambda e: e.copy(out=KF[:, kfb, :], in_=out[:, 128:256]), reads=pk, writes=[("KF", kfb)])
                                dk = KF[:, kfb, :].rearrange("p (h d) -> p h d", d=64)[:, :, 0:16].rearrange("p h (a d) -> p h a d", d=8)
                                P.dve(lambda e: e.tensor_tensor(out=dk, in0=o1[:, 2:4], in1=o2[:, 2:4], op=ALU.add),
                                       reads=[("RT", qb), ("KF", kfb)], writes=[("KF", kfb)])
                                r0 = t * 128 - row0
                                P.dma("sp", lambda e: e.dma_start(out=kvp[g][l, r0:r0 + 128, j * 128:(j + 1) * 128], in_=KF[:, kfb, :]),
                                      reads=[("KF", kfb)], final=True)
                            return qb

                        def qk_transposes(t, qb):
                            if "t" not in os.environ.get("KB", "mrkt"):
                                return
                            transpose_to([QKB[:, qb, 0:128], QKB[:, qb, 128:256]],
                                         [(ATT[:, c, t * 128:(t + 1) * 128], 0, 1), (ktb[:, t * 128:(t + 1) * 128], 1, 1)], 128,
                                         reads=[("QKB", qb)], writes=attkeys(t) + ktkeys(t))

                        def v_chunk(r, b):
                            ch = r * nb + b
                            ps = pj_slot()
                            bk, c0 = ps, 0
                            out = bank(bk, c0, 128)
                            pk = psk(bk, c0, 128)
                            tl = list(range(dil * b, dil * b + dil))
                            mm_group(out, [(XT[:, k, tok(r, b)], W[:, k, 256:384]) for k in range(8)],
                                     reads=[("XT", t_) for t_ in tl] + [("WR", s)], writes=pk)
                            P.act(lambda e: e.copy(out=vab[:, ch, :], in_=out), reads=pk, writes=[("VAUG", kb, ch)])
                            tok0 = r + dil * 128 * b
                            if tok0 >= row0:
                                vb = ch % 2
                                P.dve(lambda e: e.tensor_copy(out=VF[:, vb, :], in_=out), reads=pk, writes=[("VF", vb)])
                                dst = kvp[g][l].rearrange("(i s) c -> i s c", s=dil)
                                i0 = (tok0 - row0 - r) // dil
                                P.dma("sp", lambda e: e.dma_start(out=dst[i0:i0 + 128, r, 256 + j * 128: 256 + (j + 1) * 128],
                                                                  in_=VF[:, vb, :]),
                                      reads=[("VF", vb)], final=True)

                        pj_banks["b"] = [0, 1, 2, 3, 4, 5]
                        pendq = []
                        for t in range(NT):
                            qb = qk_tile(t)
                            pendq.append((t, qb))
                            if len(pendq) > 3:
                                qk_transposes(*pendq.pop(0))
                        ps = pj_slot()
                        bk, c0 = ps, 0
                        outs_ = bank(bk, c0, 256)[0:NS, :]
                        pk = psk(bk, c0, 256)
                        mm_group(outs_, [(XST[:, k, :], W[:, k, 0:256]) for k in range(8)],
                                 reads=[("XST",), ("WR", s)], writes=pk)
                        while pendq:
                            qk_transposes(*pendq.pop(0))
                        if ni == 1:
                            transpose_to([PKV[:, b, 0, 0, :] for b in range(4)], [(PKT[:, :, 0, :], 0, 4)], 128,
                                         reads=[("PKV",)], writes=[("PKT",)])
                        else:
                            for b2 in range(0, 4, 2):
                                transpose_to([PKV[:, b, i, 0, :] for b in (b2, b2 + 1) for i in range(4)],
                                             [(PKT[:, b2, :, :], 0, 4), (PKT[:, b2 + 1, :, :], 4, 4)], 128,
                                             reads=[("PKV",)], writes=[("PKT",)])

                        P.act(lambda e: e.copy(out=SQKB[:], in_=outs_), reads=pk, writes=[("SQKB",)])
                        o1, o2 = rope_ops(outs_, 4, NS, CF[0:NS, CF_SS1:CF_SS1 + 16], CF[0:NS, CF_SS2:CF_SS2 + 16],
                                          SRT[:, 0, :], SRT[:, 1, :], pk, ("SRT",))
                        dq = SQKB[:].rearrange("p (h d) -> p h d", d=64)[:, :, 0:16].rearrange("p h (a d) -> p h a d", d=8)
                        P.dve(lambda e: e.tensor_tensor(out=dq, in0=o1, in1=o2, op=ALU.add), reads=[("SRT",), ("SQKB",)],
                              writes=[("SQKB",)])
                        P.act(lambda e: e.copy(out=SKF[:], in_=outs_[:, 128:256]), reads=pk, writes=[("SKF",)])
                        dk = SKF[:].rearrange("p (h d) -> p h d", d=64)[:, :, 0:16].rearrange("p h (a d) -> p h a d", d=8)
                        P.dve(lambda e: e.tensor_tensor(out=dk, in0=o1[:, 2:4], in1=o2[:, 2:4], op=ALU.add),
                               reads=[("SRT",), ("SKF",)], writes=[("SKF",)])
                        P.dma("sp", lambda e: e.dma_start(out=kvs[g][l, :, j * 128:(j + 1) * 128], in_=SKF[:]),
                              reads=[("SKF",)], final=True)
                        ps = pj_slot()
                        bk, c0 = ps, 0
                        outv = bank(bk, c0, 128)[0:NS, :]
                        pkv = psk(bk, c0, 128)
                        mm_group(outv, [(XST[:, k, :], W[:, k, 256:384]) for k in range(8)],
                                 reads=[("XST",), ("WR", s)], writes=pkv)
                        P.act(lambda e: e.copy(out=SVB[:], in_=outv), reads=pkv, writes=[("SVB",)])
                        P.dve(lambda e: e.tensor_copy(out=SVF[:], in_=outv), reads=pkv, writes=[("SVF",)])
                        P.dma("sp", lambda e: e.dma_start(out=kvs[g][l, :, 256 + j * 128: 256 + (j + 1) * 128], in_=SVF[:]),
                              reads=[("SVF",)], final=True)
                        if KAB <= 3:
                            return
                        for r in range(dil):
                            for b in range(nb):
                                v_chunk(r, b)
                        transpose_to([SQKB[:, 0:128], SQKB[:, 128:256]], [(SQT[:], 0, 1), (SKT[:], 1, 1)], NS,
                                     reads=[("SQKB",)], writes=[("SQT",), ("SKT",)])

                        if KAB <= 5:
                            return
                        blocks = [(r, b) for r in range(dil) for b in range(nb)]
                        NB = len(blocks)
                        state = {}
                        NPT = 3

                        def stage1(n):
                            r, b = blocks[n]
                            ss = n % NPT
                            bk = 2 * (n % 2)
                            S = PSA[:, bk * 512: bk * 512 + 1024]
                            pk = [("ps", bk), ("ps", bk + 1)]
                            chunks = ([b - 1] if b > 0 else []) + [b]
                            lo = 0 if b > 0 else 128

                            def f(e):
                                ins = None
                                for hp in range(2):
                                    rows = slice(hp * 64, hp * 64 + 64)
                                    q = ATT[rows, c, tok(r, b)]
                                    for ci, bb in enumerate(chunks):
                                        cc = hp * 512 + lo + ci * 128
                                        ins = e.matmul(S[:, cc:cc + 128], lhsT=ktb[rows, tok(r, bb)], rhs=q, start=True, stop=True)
                                return ins
                            P.pe(f, reads=[("ATT", c, r, b)] + [("KT", kb, r, bb) for bb in chunks], writes=pk)
                            Sv = S.rearrange("p (h c) -> p h c", c=512)[:, :, lo:256]
                            Pv = PT[:, ss, :].rearrange("p (h c) -> p h c", c=256)[:, :, lo:256]
                            A2H = os.environ.get("A2H", "em")
                            if "e" in A2H:
                                P.act(lambda e: e.activation(out=Pv, in_=Sv, func=AF.Exp, scale=0.125), reads=pk, writes=[("PT", ss)])
                            else:
                                for hp_ in range(2):
                                    P.act(lambda e, hp_=hp_: e.activation(out=PT[:, ss, hp_ * 256 + lo:hp_ * 256 + 256],
                                                                          in_=S[:, hp_ * 512 + lo:hp_ * 512 + 256], func=AF.Exp, scale=0.125),
                                          reads=pk, writes=[("PT", ss)])
                            if "m" in A2H:
                                mk_ = MASK[:, lo:256].unsqueeze(1).broadcast_to([128, 2, 256 - lo])
                                P.dve(lambda e: e.tensor_tensor(out=Pv, in0=Pv, in1=mk_, op=ALU.mult),
                                      reads=[("PT", ss), ("CB",)], writes=[("PT", ss)])
                            else:
                                for hp_ in range(2):
                                    P.dve(lambda e, hp_=hp_: e.tensor_tensor(out=PT[:, ss, hp_ * 256 + lo:hp_ * 256 + 256],
                                                                             in0=PT[:, ss, hp_ * 256 + lo:hp_ * 256 + 256],
                                                                             in1=MASK[:, lo:256], op=ALU.mult),
                                          reads=[("PT", ss), ("CB",)], writes=[("PT", ss)])
                            state[n] = (ss, chunks, lo)

                        def stage2(n):
                            r, b = blocks[n]
                            ss, chunks, lo = state.pop(n)
                            ob = 4 + n % 2
                            O = bank(ob, 0, 256)
                            pk = [("ps", ob)]

                            def f(e):
                                ins = None
                                nchk = len(chunks)
                                for hp in range(2):
                                    rows = slice(hp * 64, hp * 64 + 64)
                                    tp = (0, 64) if hp else None
                                    for which in range(2):
                                        for ci, bb in enumerate(chunks):
                                            cc = hp * 256 + lo + ci * 128
                                            lhs = vab[:, r * nb + bb, hp * 64:(hp + 1) * 64] if which == 0 else ONES[:, 0:64]
                                            ins = e.matmul(O[rows, which * 128:(which + 1) * 128], lhsT=lhs, rhs=PT[:, ss, cc:cc + 128],
                                                           start=(ci == 0), stop=(ci == nchk - 1), tile_position=tp)
                                return ins
                            P.pe(f, reads=[("PT", ss), ("CB",)] + [("VAUG", kb, r * nb + bb) for bb in chunks], writes=pk)
                            P.act(lambda e: e.copy(out=ATT[:, c, tok(r, b)], in_=O[:, 0:128]), reads=pk, writes=[("ATT", c, r, b)])
                            dkeys = [("DEN", hp, t_) for hp in range(2) for t_ in range(dil * b, dil * b + dil)]
                            if g == 0:
                                P.dve(lambda e: e.tensor_copy(out=DEN[:, tok(r, b)], in_=O[:, 128:256]), reads=pk, writes=dkeys)
                            else:
                                P.dve(lambda e: e.tensor_tensor(out=DEN[:, tok(r, b)], in0=DEN[:, tok(r, b)], in1=O[:, 128:256],
                                                                op=ALU.add),
                                      reads=pk + dkeys, writes=dkeys)

                        LAG = 2
                        for n in range(NB + LAG):
                            if n < NB:
                                stage1(n)
                            if n >= LAG:
                                stage2(n - LAG)

                        if KAB <= 4:
                            return
                        SB_ = 6
                        Sp = bank(SB_, 0, 32)
                        Sn = bank(SB_, 32, 32)[0:NS, :]
                        Dn = bank(SB_, 64, 32)
                        Ov = bank(SB_, 128, 32)
                        k0 = k1 = ("ps", SB_)

                        def f_sp(e):
                            ins = None
                            for hp in range(2):
                                rows = slice(hp * 64, hp * 64 + 64)
                                for b in range(4):
                                    if g == 0:
                                        ins = e.matmul(Sp[:, hp * 16 + b * 4: hp * 16 + b * 4 + 4], lhsT=PKT[rows, b, 0, :],
                                                       rhs=SQT[rows, b * 4:b * 4 + 4], start=True, stop=True)
                                    else:
                                        for i in range(4):
                                            col = hp * 16 + b * 4 + i
                                            ins = e.matmul(Sp[:, col:col + 1], lhsT=PKT[rows, b, i, :],
                                                           rhs=SQT[rows, b * 4 + i:b * 4 + i + 1], start=True, stop=True)
                                ins = e.matmul(Sn[:, hp * 16:hp * 16 + 16], lhsT=SKT[rows, :], rhs=SQT[rows, :],
                                               start=True, stop=True)
                            return ins
                        P.pe(f_sp, reads=[("PKT",), ("SQT",), ("SKT",)], writes=[k0])
                        P.act(lambda e: e.activation(out=SPP[:], in_=Sp, func=AF.Exp, scale=0.125), reads=[k0], writes=[("SPP",)])
                        P.act(lambda e: e.activation(out=SPN[:], in_=Sn, func=AF.Exp, scale=0.125), reads=[k0], writes=[("SPN",)])
                        if g == 0:
                            P.dve(lambda e: e.tensor_tensor(out=SPP[:], in0=SPP[:], in1=CB[:, CB_MP0:CB_MP0 + 32], op=ALU.mult),
                                  reads=[("SPP",), ("CB",)], writes=[("SPP",)])
                        mn = CB_MN0 if g == 0 else CB_MN1
                        P.dve(lambda e: e.tensor_tensor(out=SPN[:], in0=SPN[:], in1=CB[0:NS, mn:mn + 32], op=ALU.mult),
                              reads=[("SPN",), ("CB",)], writes=[("SPN",)])

                        def f_pv(e):
                            e.matmul(Dn, lhsT=ONES, rhs=SPP[:], start=True, stop=False)
                            ins = e.matmul(Dn, lhsT=ONES[0:NS, :], rhs=SPN[:], start=False, stop=True)
                            for hp in range(2):
                                for b in range(4):
                                    if g == 0:
                                        cs = slice(hp * 16 + b * 4, hp * 16 + b * 4 + 4)
                                        e.matmul(Ov[:, cs], lhsT=PKV[:, b, 0, 1, :], rhs=SPP[:, cs], start=True, stop=False)
                                        ins = e.matmul(Ov[:, cs], lhsT=SVB[:], rhs=SPN[:, cs], start=False, stop=True)
                                    else:
                                        for i in range(4):
                                            cs = slice(hp * 16 + b * 4 + i, hp * 16 + b * 4 + i + 1)
                                            e.matmul(Ov[:, cs], lhsT=PKV[:, b, i, 1, :], rhs=SPP[:, cs], start=True, stop=False)
                                            ins = e.matmul(Ov[:, cs], lhsT=SVB[:], rhs=SPN[:, cs], start=False, stop=True)
                            return ins
                        P.pe(f_pv, reads=[("SPP",), ("SPN",), ("PKV",), ("SVB",), ("CB",)], writes=[k0, k1])
                        if g == 0:
                            P.dve(lambda e: e.tensor_copy(out=SDEN[:], in_=Dn), reads=[k0], writes=[("SDEN",)])
                        else:
                            P.dve(lambda e: e.tensor_tensor(out=SDEN[:], in0=SDEN[:], in1=Dn, op=ALU.add),
                                  reads=[k0, ("SDEN",)], writes=[("SDEN",)])
                        P.act(lambda e: e.copy(out=SO[:, g, :], in_=Ov), reads=[k1], writes=[("SO", g)])

                        if g == 2:
                            for q4 in range(4):
                                cs = slice(q4 * 512, (q4 + 1) * 512)
                                dnk = [("DEN", hp, t_) for hp in range(2) for t_ in range(q4 * 4, q4 * 4 + 4)]
                                P.act(lambda e, cs=cs: e.activation(out=DEN[:, cs], in_=DEN[:, cs], func=AF.Ln), reads=dnk, writes=dnk)
                                P.act(lambda e, cs=cs: e.activation(out=DEN[:, cs], in_=DEN[:, cs], func=AF.Exp, scale=-1.0),
                                      reads=dnk, writes=dnk)
                                for g2 in range(3):
                                    c2 = 2 * g2 + j
                                    d2 = GROUPS[g2][1]
                                    ak = [("ATT", c2, r_, t_ // d2) for r_ in range(d2) for t_ in range(q4 * 4, q4 * 4 + 4)]
                                    ak = sorted(set(ak))
                                    P.dve(lambda e, cs=cs, c2=c2: e.tensor_tensor(out=ATT[:, c2, cs], in0=ATT[:, c2, cs],
                                                                                  in1=DEN[:, cs], op=ALU.mult),
                                          reads=dnk + ak, writes=ak)
                            P.dve(lambda e: e.reciprocal(out=SDEN[:], in_=SDEN[:]), reads=[("SDEN",)], writes=[("SDEN",)])
                            for g2 in range(3):
                                for hp in range(2):
                                    rows = slice(hp * 64, hp * 64 + 64)
                                    P.dve(lambda e, g2=g2, rows=rows, hp=hp: e.tensor_tensor(
                                        out=ATTS[rows, 2 * g2 + j, :], in0=SO[rows, g2, hp * 16:hp * 16 + 16],
                                        in1=SDEN[rows, hp * 16:hp * 16 + 16], op=ALU.mult),
                                        reads=[("SO", g2), ("SDEN",)], writes=[("ATTS",)])
                    items.append((load_ab, comp_ab))

            def load_sgu():
                s = take_slots(1)[0]
                P.dma("pool", lambda e: e.dma_start(out=wview(s, 512), in_=wsrc(w_in[l], 2304, 2816)), writes=[("WR", s)])
                return s

            def comp_sgu(s):
                W = wview(s, 512)
                tr["banks"] = [6, 7]
                sgu_consts()

                pj_banks["b"] = [0, 1, 4, 5]

                def sgu_tile(t, npart, xt_tok, xt_keys, dst, dst_keys, wsg, bsp, wsg_key, sample=False):
                    ub = t % 3
                    ps = pj_slot()
                    bk, c0 = ps, 0
                    pk = psk(bk, c0, 256)
                    Ups = bank(bk, c0, 256)

                    def fu(e):
                        ins = None
                        for uc in range(2):
                            for k in range(8):
                                ins = e.matmul(Ups[:, uc * 128: uc * 128 + npart], lhsT=W[:, k, uc * 128:(uc + 1) * 128],
                                               rhs=xt_tok(k), start=(k == 0), stop=(k == 7))
                        return ins
                    P.pe(fu, reads=list(xt_keys) + [("WR", s)], writes=pk)
                    Uv = Ups.rearrange("p (u t) -> p u t", t=128)[:, :, 0:npart]
                    ugv = UG[:, ub, :].rearrange("p (u t) -> p u t", t=128)[:, :, 0:npart]
                    P.act(lambda e: e.activation(out=ugv, in_=Uv, func=AF.Gelu_apprx_tanh), reads=pk, writes=[("UG", ub)])
                    ps2 = pj_slot()
                    bk2, c02 = ps2, 0
                    pk2 = psk(bk2, c02, 256)
                    Gps = bank(bk2, c02, 256)[0:npart, :]
                    mm_group(Gps, [(xt_tok(k), W[:, k, 256:512]) for k in range(8)], reads=list(xt_keys) + [("WR", s)], writes=pk2)
                    gg = GG[0:npart, ub, :]
                    P.act(lambda e: e.activation(out=gg, in_=Gps, func=AF.Gelu_apprx_tanh), reads=pk2, writes=[("GG", ub)])
                    stt = STATG[0:npart, ub, :]
                    sk = ("STATG", ub)
                    P.dve(lambda e: e.bn_stats(stt[:, 0:6], gg), reads=[("GG", ub)], writes=[sk])
                    P.dve(lambda e: e.bn_aggr(stt[:, 12:14], stt[:, 0:6]), reads=[sk], writes=[sk])
                    rstd_ops(stt, sk, npart)

                    def s2():
                        gt = GTMP[0:npart, ub, :]
                        P.dve(lambda e: e.scalar_tensor_tensor(out=gt, in0=gg, scalar=stt[:, 12:13], in1=GAM[0:npart, 0, :],
                                                               op0=ALU.subtract, op1=ALU.mult),
                              reads=[sk, ("GG", ub), ("GAM",)], writes=[("GTMP", ub)])
                        gv = GV[0:npart, ub, :]
                        if sample:
                            P.dve(lambda e: e.scalar_tensor_tensor(out=GVF[:], in0=gt, scalar=stt[:, 14:15], in1=GAM[0:npart, 1, :],
                                                                   op0=ALU.mult, op1=ALU.add),
                                  reads=[sk, ("GTMP", ub), ("GAM",)], writes=[("GVF",)])
                            P.dma("sp", lambda e: e.dma_start(out=chunkv[l], in_=GVF[:]), reads=[("GVF",)], final=True)
                            P.act(lambda e: e.copy(out=gv, in_=GVF[:]), reads=[("GVF",)], writes=[("GV", ub)])
                        else:
                            P.dve(lambda e: e.scalar_tensor_tensor(out=gv, in0=gt, scalar=stt[:, 14:15], in1=GAM[0:npart, 1, :],
                                                                   op0=ALU.mult, op1=ALU.add),
                                  reads=[sk, ("GTMP", ub), ("GAM",)], writes=[("GV", ub)])
                    gv = GV[0:npart, ub, :]
                    return s2, (lambda: sgu_part2(t, npart, gv, ugv, ub, dst, dst_keys, wsg, bsp, wsg_key))

                def sgu_part2(t, npart, gv, ugv, ub, dst, dst_keys, wsg, bsp, wsg_key):
                    MBk = 2 + (t % 2)
                    mk = [("ps", MBk)]

                    def fM(e):
                        ins = None
                        for sg in range(4):
                            ins = e.matmul(bank(MBk, sg * 128, npart), lhsT=gv[:, (sg // 2) * 128:(sg // 2) * 128 + 128],
                                           rhs=wsg[:, sg, :], start=True, stop=True)
                        return ins
                    P.pe(fM, reads=[("GV", ub), wsg_key], writes=mk)
                    for sg in range(4):
                        Mps = bank(MBk, sg * 128, npart)
                        rows = slice((sg % 2) * 64, (sg % 2) * 64 + 64)
                        tb = sg % 2
                        tmp = STMP[rows, tb, 0:npart]
                        P.dve(lambda e, Mps=Mps, rows=rows, sg=sg, tmp=tmp: e.tensor_tensor(out=tmp, in0=Mps[rows, :],
                                                                                         in1=bsp[rows, sg, :], op=ALU.add),
                              reads=mk + [("BSP",), ("BSPS",)], writes=[("STMP", tb)])
                        P.dve(lambda e, rows=rows, sg=sg, tmp=tmp: e.tensor_tensor(out=dst(rows, sg // 2), in0=tmp,
                                                                                  in1=ugv[rows, sg // 2, :], op=ALU.mult),
                              reads=[("STMP", tb), ("UG", ub)], writes=dst_keys)

                st2, st3 = {}, {}
                for t in range(NT + 1 + 2):
                    if t < NT:
                        st2[t], st3[t] = sgu_tile(t, 128, lambda k, t=t: XT[:, k, t * 128:(t + 1) * 128], [("XT", t)],
                                                  lambda rows, uc, t=t: SGUT[rows, uc, t * 128:(t + 1) * 128], [("SGUT", t)],
                                                  SGW, BSP, ("SGW",))
                    elif t == NT:
                        st2[t], st3[t] = sgu_tile(NT, NS, lambda k: XST[:, k, :], [("XST",)],
                                                  lambda rows, uc: ATTS[rows, 6 + uc, :], [("ATTS",)], SGWS, BSPS, ("SGWS",),
                                                  sample=True)
                    if (t - 1) in st2:
                        st2.pop(t - 1)()
                    if (t - 2) in st3:
                        st3.pop(t - 2)()
            items.append((load_sgu, comp_sgu))

            def load_mix():
                ss = take_slots(2)
                for i, s in enumerate(ss):
                    P.dma("pool", lambda e, i=i, s=s: e.dma_start(out=wview(s, 512), in_=wsrc(w_mix[l], i * 512, (i + 1) * 512)),
                          writes=[("WR", s)])
                return ss

            def tokmajor_ln_phase(ss, lhs_fn, lhs_keys_fn, lhs_s_fn, lhs_s_keys, which, first_phase, final_ln):
                Ws = [wview(s, 512) for s in ss]
                load_ln_tables(l, which)
                tr["banks"] = [6, 7]
                pend = []
                ctx = {}
                for t in range(NT + 1 + 2):
                    if t <= NT:
                        sample = (t == NT)
                        npart = NS if sample else 128
                        pb = (t % 3) * 2
                        if first_phase and not sample:
                            src = xp if l == 0 else xscr
                            rk = [] if l == 0 else [("xscr", t)]
                            P.dma("sp", lambda e, t=t, src=src: e.dma_start(out=X[:, t, :], in_=src[t * 128:(t + 1) * 128, :]),
                                  reads=rk, writes=[("X", t)])
                        for hf in range(2):
                            out = bank(pb + hf)[0:npart, :]
                            if sample:
                                pairs = [(lhs_s_fn(k), Ws[hf][:, k, :]) for k in range(8)]
                                rd = list(lhs_s_keys)
                            else:
                                pairs = [(lhs_fn(k, t), Ws[hf][:, k, :]) for k in range(8)]
                                rd = list(lhs_keys_fn(t))
                            mm_group(out, pairs, reads=rd + [("WR", ss[hf])], writes=[("ps", pb + hf)])
                        ps2 = PSA[0:npart, pb * 512: pb * 512 + 1024]
                        pk2 = [("ps", pb), ("ps", pb + 1)]
                        xv = XS[:] if sample else X[:, t, :]
                        xk = [("XS",)] if sample else [("X", t)]
                        stt = STAT[0:npart, t % 4, :]
                        sk = ("STAT", t % 4)
                        ctx[t] = (xv, npart, xk, stt, sk, sample)
                        ln_s1(xv, ps2, pk2, npart, xk, stt, sk)
                    if 1 <= t <= NT + 1:
                        xv, npart, xk, stt, sk, sample = ctx[t - 1]
                        ln_s2(xv, npart, xk, stt, sk)
                    if 2 <= t <= NT + 2:
                        t2 = t - 2
                        xv, npart, xk, stt, sk, sample = ctx.pop(t2)
                        ln_s3(xv, npart, xk, stt, sk)
                        if sample:
                            pend.append(ln_stage_sample(last_layer, final_ln))
                        else:
                            sv = lhs_fn(slice(0, 8), t2)
                            pend.append(ln_stage_tile(t2, last_layer, final_ln, (sv, list(lhs_keys_fn(t2)))))
                    if len(pend) > 2:
                        ln_transposes(pend.pop(0))
                while pend:
                    ln_transposes(pend.pop(0))

            def comp_mix(ss):
                P.fence("dve", lambda e: e.memset(DUMMY[:, 0:1], 0.0), AB_PREF, ("X",))
                P.fence("dve", lambda e: e.memset(DUMMY[:, 1:2], 0.0), ("ATT",), ("ATTC",))
                tokmajor_ln_phase(ss, lambda k, t: ATTALL[:, k, t * 128:(t + 1) * 128],
                                  lambda t: [("ATTC", t), ("SGUT", t)],
                                  lambda k: ATTS[:, k, :], [("ATTS",)], 0, True, False)
            items.append((load_mix, comp_mix))

            for ch in range(4):
                def load_m(ch=ch):
                    s = take_slots(1)[0]
                    P.dma("pool", lambda e: e.dma_start(out=wview(s, 512), in_=wsrc(w_xkv[l], ch * 512, (ch + 1) * 512)),
                          writes=[("WR", s)])
                    return s

                def comp_m(s, ch=ch):
                    W = wview(s, 512)
                    pj_banks["b"] = [2, 3]
                    if ch == 0:
                        P.fence("dve", lambda e: e.memset(DUMMY[:, 5:6], 0.0), S_3, S_M + ("MEMKT", "MEMV"))
                        for mt_ in range(2):
                            P.dma("pool", lambda e, mt_=mt_: e.dma_start(out=XB[:, mt_, :], in_=memp[mt_ * 128:(mt_ + 1) * 128, :]),
                                  writes=[("XB", mt_)])
                            transpose_to([XB[:, mt_, k * 128:(k + 1) * 128] for k in range(8)],
                                         [(MEMT[:, :, mt_ * 128:(mt_ + 1) * 128], 0, 8)], 128,
                                         reads=[("XB", mt_)], writes=[("MEMT",)])
                    for mt in range(2):
                        bk = mt
                        out = bank(bk)
                        mm_group(out, [(MEMT[:, k, mt * 128:(mt + 1) * 128], W[:, k, :]) for k in range(8)],
                                 reads=[("MEMT",), ("WR", s)], writes=psk(bk, 0, 512))
                        mb = (ch * 2 + mt) % 2
                        P.act(lambda e, out=out, mb=mb: e.copy(out=MKF[:, mb, :], in_=out), reads=psk(bk, 0, 512),
                              writes=[("MKF", mb)])
                        if ch >= 2:
                            P.dve(lambda e, out=out, mt=mt: e.tensor_copy(out=MEMV[:, mt, (ch - 2) * 512:(ch - 1) * 512], in_=out),
                                  reads=psk(bk, 0, 512), writes=[("MEMV",)])
                        P.dma("sp", lambda e, mb=mb, mt=mt: e.dma_start(
                            out=memkvp[l, mt * 128:(mt + 1) * 128, ch * 512:(ch + 1) * 512], in_=MKF[:, mb, :]),
                            reads=[("MKF", mb)], final=True)
                    if ch < 2:
                        for sub in range(4):
                            ps = pj_slot()
                            bk, c0 = ps, 0
                            out = bank(bk, c0, 256)
                            mm_group(out, [(W[:, k, sub * 128:(sub + 1) * 128], MEMT[:, k, :]) for k in range(8)],
                                     reads=[("MEMT",), ("WR", s)], writes=psk(bk, c0, 256))
                            P.act(lambda e, out=out, sub=sub: e.copy(out=MEMKT[:, ch * 4 + sub, :], in_=out),
                                  reads=psk(bk, c0, 256), writes=[("MEMKT",)])
                items.append((load_m, comp_m))

            for wc in range(2):
                def load_xq(wc=wc):
                    s = take_slots(1)[0]
                    P.dma("pool", lambda e: e.dma_start(out=wview(s, 512), in_=wsrc(w_xq[l], wc * 512, (wc + 1) * 512)),
                          writes=[("WR", s)])
                    return s

                def comp_xq(s, wc=wc):
                    W = wview(s, 512)
                    if wc == 0:
                        P.fence("dve", lambda e: e.memset(DUMMY[:, 1:2], 0.0), ("ATT", "ATTC", "SGUT"), ("QXT",))
                    n = 0
                    for tg in range(4):
                        for sub in range(4):
                            bk = 4 + (n % 2)
                            n += 1
                            out = bank(bk)
                            pk = psk(bk, 0, 512)
                            mm_group(out, [(W[:, k, sub * 128:(sub + 1) * 128], XT[:, k, tg * 512:(tg + 1) * 512]) for k in range(8)],
                                     reads=[("XT", t_) for t_ in range(tg * 4, tg * 4 + 4)] + [("WR", s)], writes=pk)
                            dst = QXT[:, wc * 4 + sub, tg * 512:(tg + 1) * 512]
                            wk = [("QXT", wc * 4 + sub, t_) for t_ in range(tg * 4, tg * 4 + 4)]
                            if n % 2:
                                P.act(lambda e, dst=dst, out=out: e.copy(out=dst, in_=out), reads=pk, writes=wk)
                            else:
                                P.dve(lambda e, dst=dst, out=out: e.tensor_copy(out=dst, in_=out), reads=pk, writes=wk)
                    out = bank(4, 0, 64)
                    pk = psk(4, 0, 64)

                    def fs(e):
                        ins = None
                        for sub in range(4):
                            for k in range(8):
                                ins = e.matmul(out[:, sub * NS:(sub + 1) * NS], lhsT=W[:, k, sub * 128:(sub + 1) * 128],
                                               rhs=XST[:, k, :], start=(k == 0), stop=(k == 7))
                        return ins
                    P.pe(fs, reads=[("XST",), ("WR", s)], writes=pk)
                    P.act(lambda e: e.copy(out=QXST[:, wc * 4:(wc + 1) * 4, :], in_=out.rearrange("p (c t) -> p c t", t=NS)),
                          reads=pk, writes=[("QXST", wc)])
                items.append((load_xq, comp_xq))

            def load_xo():
                ss = take_slots(2)
                for i, s in enumerate(ss):
                    P.dma("pool", lambda e, i=i, s=s: e.dma_start(out=wview(s, 512), in_=wsrc(w_xo[l], i * 512, (i + 1) * 512)),
                          writes=[("WR", s)])
                return ss

            def comp_xo(ss):
                P.fence("dve", lambda e: e.memset(DUMMY[:, 6:7], 0.0), S_M, S_X)
                xitems = [(tg, h) for tg in range(4) for h in range(4)]

                def xa(n):
                    tg, h = xitems[n]
                    tk = slice(tg * 512, (tg + 1) * 512)
                    sb_ = 0 if n % 2 == 0 else 5
                    S2 = PSA[:, sb_ * 512: sb_ * 512 + 1024]
                    kS = [("ps", sb_), ("ps", sb_ + 1)]

                    def fS(e):
                        ins = None
                        for mt in range(2):
                            for dc in range(2):
                                ins = e.matmul(bank(sb_ + mt), lhsT=MEMKT[:, 2 * h + dc, mt * 128:(mt + 1) * 128],
                                               rhs=QXT[:, 2 * h + dc, tk], start=(dc == 0), stop=(dc == 1))
                        return ins
                    P.pe(fS, reads=[("MEMKT",)] + [("QXT", 2 * h + dc_, t_) for dc_ in range(2) for t_ in range(tg * 4, tg * 4 + 4)],
                         writes=kS)
                    pb = n % 2
                    PmA = PM[:, pb, :]
                    P.act(lambda e: e.activation(out=PmA, in_=S2, func=AF.Exp, scale=1.0 / 16.0), reads=kS, writes=[("PM", pb)])

                def xb(n):
                    tg, h = xitems[n]
                    tk = slice(tg * 512, (tg + 1) * 512)
                    pb = n % 2
                    rb = n % 2
                    PmA = PM[:, pb, :]
                    Dps = bank(2)
                    kD = [("ps", 2)]

                    def fD(e):
                        e.matmul(Dps, lhsT=ONES, rhs=PmA[:, 0:512], start=True, stop=False)
                        return e.matmul(Dps, lhsT=ONES, rhs=PmA[:, 512:1024], start=False, stop=True)
                    P.pe(fD, reads=[("PM", pb), ("CB",)], writes=kD)
                    P.act(lambda e: e.activation(out=RDEN[:, rb, :], in_=Dps, func=AF.Ln), reads=kD, writes=[("RDEN", rb)])
                    P.act(lambda e: e.activation(out=RDEN[:, rb, :], in_=RDEN[:, rb, :], func=AF.Exp, scale=-1.0),
                          reads=[("RDEN", rb)], writes=[("RDEN", rb)])
                    for dc in range(2):
                        Ops = bank(3 + dc)
                        kO = [("ps", 3 + dc)]

                        def fO(e, dc=dc, Ops=Ops):
                            e.matmul(Ops, lhsT=MEMV[:, 0, (2 * h + dc) * 128:(2 * h + dc + 1) * 128], rhs=PmA[:, 0:512],
                                     start=True, stop=False)
                            return e.matmul(Ops, lhsT=MEMV[:, 1, (2 * h + dc) * 128:(2 * h + dc + 1) * 128],
                                            rhs=PmA[:, 512:1024], start=False, stop=True)
                        P.pe(fO, reads=[("PM", pb), ("MEMV",)], writes=kO)
                        P.dve(lambda e, dc=dc, Ops=Ops: e.tensor_tensor(out=QXT[:, 2 * h + dc, tk], in0=Ops, in1=RDEN[:, rb, :],
                                                                        op=ALU.mult),
                              reads=kO + [("RDEN", rb)], writes=[("QXT", 2 * h + dc, t_) for t_ in range(tg * 4, tg * 4 + 4)])

                xa(0)
                for n in range(len(xitems)):
                    if n + 1 < len(xitems):
                        xa(n + 1)
                    xb(n)
                P.fence("dve", lambda e: e.memset(DUMMY[:, 0:1], 0.0), ("PM",), ("CMK",))
                for b in range(4):
                    P.dma("pool", lambda e, b=b: e.dma_start(out=CMK[:], in_=cmem[l, b].rearrange("(m p) c -> p m c", p=128)[:, :, 0:1024]),
                          writes=[("CMK",)])
                    P.dma("pool", lambda e, b=b: e.dma_start(out=SCV[:], in_=cmem[l, b].rearrange("(m p) c -> p m c", p=128)[:, :, 1024:2048]),
                          writes=[("SCV",)])
                    for mt in range(2):
                        transpose_to([CMK[:, mt, ch * 128:(ch + 1) * 128] for ch in range(8)],
                                     [(MKTS[:, :, mt * 128:(mt + 1) * 128], 0, 8)], 128,
                                     reads=[("CMK",)], writes=[("MKTS",)])
                    Sx = bank(5, 0, 32)
                    Dx = bank(5, 128, 16)
                    Ox = bank(5, 256, 32)
                    kx = [("ps", 5), ("ps", 5), ("ps", 5)]

                    def fS(e, b=b):
                        ins = None
                        for mt in range(2):
                            for h in range(4):
                                for dc in range(2):
                                    ins = e.matmul(Sx[:, mt * 16 + h * 4: mt * 16 + h * 4 + 4],
                                                   lhsT=MKTS[:, 2 * h + dc, mt * 128:(mt + 1) * 128],
                                                   rhs=QXST[:, 2 * h + dc, b * 4:(b + 1) * 4], start=(dc == 0), stop=(dc == 1))
                        return ins
                    P.pe(fS, reads=[("MKTS",), ("QXST", 0), ("QXST", 1)], writes=[kx[0]])
                    P.act(lambda e: e.activation(out=SPM[:].rearrange("p a c -> p (a c)"), in_=Sx, func=AF.Exp, scale=1.0 / 16.0),
                          reads=[kx[0]], writes=[("SPM",)])

                    def fO(e, b=b, SCV=SCV):
                        e.matmul(Dx, lhsT=ONES, rhs=SPM[:, 0, :], start=True, stop=False)
                        ins = e.matmul(Dx, lhsT=ONES, rhs=SPM[:, 1, :], start=False, stop=True)
                        for h in range(4):
                            for dc in range(2):
                                ch = 2 * h + dc
                                for mt in range(2):
                                    ins = e.matmul(Ox[:, ch * 4:(ch + 1) * 4], lhsT=SCV[:, mt, ch * 128:(ch + 1) * 128],
                                                   rhs=SPM[:, mt, h * 4:(h + 1) * 4], start=(mt == 0), stop=(mt == 1))
                        return ins
                    P.pe(fO, reads=[("SPM",), ("SCV",), ("CB",)], writes=[kx[1], kx[2]])
                    P.dve(lambda e: e.reciprocal(out=SRD2[:], in_=Dx), reads=[kx[1]], writes=[("SRD2",)])
                    P.dve(lambda e, b=b: e.tensor_tensor(
                        out=QXST[:, :, b * 4:(b + 1) * 4].rearrange("p (h dc) i -> p h dc i", dc=2),
                        in0=Ox.rearrange("p (h dc i) -> p h dc i", dc=2, i=4),
                        in1=SRD2[:].rearrange("p (h i) -> p h i", i=4).unsqueeze(2).broadcast_to([128, 4, 2, 4]), op=ALU.mult),
                        reads=[kx[2], ("SRD2",)], writes=[("QXST", 0), ("QXST", 1)])
                tokmajor_ln_phase(ss, lambda k, t: QXT[:, k, t * 128:(t + 1) * 128],
                                  lambda t: [("QXT", cc, t) for cc in range(8)],
                                  lambda k: QXST[:, k, :], [("QXST", 0), ("QXST", 1)], 1, False, False)
            items.append((load_xo, comp_xo))

            for tg in range(4):
                for hc in range(8):
                    def load_up(hc=hc):
                        s = take_slots(1)[0]
                        P.dma("pool", lambda e: e.dma_start(out=wview(s, 512), in_=wsrc(w_up[l], hc * 512, (hc + 1) * 512)),
                              writes=[("WR", s)])
                        return s

                    def comp_up(s, tg=tg, hc=hc):
                        W = wview(s, 512)
                        tr["banks"] = [6, 7]
                        if hc >= 2 and mlp_pend:
                            ln_transposes(mlp_pend.pop(0))
                        if tg == 0 and hc == 0:
                            P.fence("dve", lambda e: e.memset(DUMMY[:, 2:3], 0.0), ("QXT",), ("HT",))
                            P.fence("dve", lambda e: e.memset(DUMMY[:, 7:8], 0.0), S_X, S_3)
                        for sub in range(4):
                            n = hc * 4 + sub
                            bk = 4 + (n % 2)
                            out = bank(bk)
                            pk = psk(bk, 0, 512)
                            mm_group(out, [(W[:, k, sub * 128:(sub + 1) * 128], XT[:, k, tg * 512:(tg + 1) * 512]) for k in range(8)],
                                     reads=[("XT", t_) for t_ in range(tg * 4, tg * 4 + 4)] + [("WR", s)], writes=pk)
                            rb = n % 4
                            P.act(lambda e, out=out, rb=rb: e.activation(out=RELU[:, rb, :], in_=out, func=AF.Relu), reads=pk,
                                  writes=[("RELU", rb)])
                            eng = P.dve
                            eng(lambda e, n=n, rb=rb: e.tensor_tensor(out=HT[:, n, :], in0=RELU[:, rb, :], in1=RELU[:, rb, :],
                                                                      op=ALU.mult),
                                reads=[("RELU", rb)], writes=[("HT", n)])
                            if sub == 1 and mlp_ln_pend:
                                mlp_ln_pend.pop(0)()
                        if tg == 0:
                            out = bank(5, 0, 64)
                            pk = psk(5, 0, 64)

                            def fs(e):
                                ins = None
                                for sub in range(4):
                                    for k in range(8):
                                        ins = e.matmul(out[:, sub * NS:(sub + 1) * NS], lhsT=W[:, k, sub * 128:(sub + 1) * 128],
                                                       rhs=XST[:, k, :], start=(k == 0), stop=(k == 7))
                                return ins
                            P.pe(fs, reads=[("XST",), ("WR", s)], writes=pk)
                            P.act(lambda e: e.activation(out=RELU[:, 0, 0:64], in_=out, func=AF.Relu), reads=pk,
                                  writes=[("RELU", 0)])
                            P.dve(lambda e: e.tensor_tensor(out=HTS[:, hc * 4:(hc + 1) * 4, :],
                                                            in0=RELU[:, 0, 0:64].rearrange("p (c t) -> p c t", t=NS),
                                                            in1=RELU[:, 0, 0:64].rearrange("p (c t) -> p c t", t=NS), op=ALU.mult),
                                  reads=[("RELU", 0)], writes=[("HTS",)])
                    items.append((load_up, comp_up))
                if tg == 0:
                    items.append((None, lambda s_: load_ln_tables(l, 2)))
                for hf in range(2):
                    for k4 in range(8):
                        def load_dn(hf=hf, k4=k4):
                            s = take_slots(1)[0]
                            P.dma("pool", lambda e: e.dma_start(out=wview(s, 512, 4), in_=wsrc(w_down[l], hf * 512, (hf + 1) * 512, k4 * 4, 4)),
                                  writes=[("WR", s)])
                            return s

                        def comp_dn(s, tg=tg, hf=hf, k4=k4):
                            W = wview(s, 512, 4)
                            with_s = (tg == 0)

                            def f(e):
                                ins = None
                                for kk in range(4):
                                    kc = k4 * 4 + kk
                                    for tl in range(4):
                                        ins = e.matmul(bank(tl), lhsT=HT[:, kc, tl * 128:(tl + 1) * 128], rhs=W[:, kk, :],
                                                       start=(kc == 0), stop=(kc == 31))
                                    if with_s:
                                        ins = e.matmul(bank(4)[0:NS, :], lhsT=HTS[:, kc, :], rhs=W[:, kk, :],
                                                       start=(kc == 0), stop=(kc == 31))
                                return ins
                            wr = [k for tl in range(4) for k in psk(tl, 0, 512)] + (psk(4, 0, 512) if with_s else [])
                            P.pe(f, reads=[("HT", k4 * 4 + kk) for kk in range(4)] + [("WR", s)] + ([("HTS",)] if with_s else []),
                                 writes=wr)
                            if k4 == 7:
                                tls = list(range(4 + (1 if with_s else 0)))
                                cx = {}
                                for tl in tls:
                                    sample = (tl == 4)
                                    t = tg * 4 + tl
                                    npart = NS if sample else 128
                                    xv = XS[:] if sample else X[:, t, :]
                                    xk = [("XS",)] if sample else [("X", t)]
                                    xh = xv[:, hf * 512:(hf + 1) * 512]
                                    psv = bank(tl)[0:npart, :]
                                    stk = ("STATM", tl)
                                    stt = STATM[0:npart, tl, :]
                                    cx[tl] = (xv, npart, xk, stt, stk, sample, t)
                                    P.dve(lambda e, xh=xh, psv=psv: e.scalar_tensor_tensor(out=xh, in0=xh, scalar=ALPHA, in1=psv,
                                                                                           op0=ALU.mult, op1=ALU.add),
                                          reads=psk(tl, 0, 512) + xk, writes=xk)
                                    P.dve(lambda e, xh=xh, stt=stt, hf=hf: e.bn_stats(stt[:, hf * 6:(hf + 1) * 6], xh), reads=xk,
                                          writes=[stk])
                                    if hf == 1:
                                        ln_s1_stats(xv, stt, stk, npart)
                                if hf == 1:
                                    for tl in tls:
                                        def fin(c_=cx[tl], tl=tl):
                                            xv, npart, xk, stt, stk, sample, t = c_
                                            ln_s2(xv, npart, xk, stt, stk)
                                            ln_s3(xv, npart, xk, stt, stk)
                                            if sample:
                                                mlp_pend.append(ln_stage_sample(last_layer, True))
                                            else:
                                                mlp_pend.append(ln_stage_tile(t, last_layer, True, (XBM[:, tl, :, :], [("XBM", tl)])))
                                        mlp_ln_pend.append(fin)
                                    if tg == 3:
                                        while mlp_ln_pend:
                                            mlp_ln_pend.pop(0)()
                        items.append((load_dn, comp_dn))
            if not last_layer:
                def comp_end(s_):
                    while mlp_pend:
                        ln_transposes(mlp_pend.pop(0))
                    P.fence("dve", lambda e: e.memset(DUMMY[:, 3:4], 0.0), ("X",), AB_PREF)
                    P.fence("dve", lambda e: e.memset(DUMMY[:, 4:5], 0.0), ("HT",), ("ATT", "SGUT"))
                items.append((None, comp_end))


        for l in range(DEPTH):
            layer(l)

        slots = [None] * len(items)
        LOOK = 2
        for i0 in range(min(LOOK, len(items))):
            if items[i0][0] is not None:
                slots[i0] = items[i0][0]()
        for i, (ld, comp) in enumerate(items):
            if stop_after is not None and i >= stop_after:
                break
            if i + LOOK < len(items) and items[i + LOOK][0] is not None:
                slots[i + LOOK] = items[i + LOOK][0]()
            comp(slots[i])
            if stop_after is not None and i + 1 >= stop_after:
                break
        P.emit()
    return nc


def _const_tables():
    cb = np.zeros((128, NCB), np.float32)
    p = np.arange(128)[:, None]
    f = np.arange(128)[None, :]
    cb[:, CB_ID:CB_ID + 128] = (p == f)
    cb[:, CB_MASK:CB_MASK + 128] = (p >= f)
    cb[:, CB_MASK + 128:CB_MASK + 256] = (p <= f)
    cb[:, CB_ONES:CB_ONES + 128] = 1.0
    col = np.arange(32)
    ci = col % 4
    cbb = (col % 16) // 4
    cb[:, CB_MP0:CB_MP0 + 32] = (np.arange(128)[:, None] >= ci[None, :])
    kk = np.arange(16)
    kb_, ki = kk // 4, kk % 4
    same_b = (kb_[:, None] == cbb[None, :])
    cb[0:16, CB_MN0:CB_MN0 + 32] = same_b & (ki[:, None] <= ci[None, :])
    cb[0:16, CB_MN1:CB_MN1 + 32] = same_b & (ki[:, None] == ci[None, :])
    cb[0:16, CB_SGM:CB_SGM + 16] = (kb_[:, None] == kb_[None, :]) & (ki[:, None] <= ki[None, :])

    cf = np.zeros((128, NCF), np.float32)
    half = 8
    inv = (np.float32(500000.0) ** (-(np.arange(half, dtype=np.float32) / np.float32(half)))).astype(np.float32)

    def cs(pos):
        ang = (pos.astype(np.float32)[:, None] * inv[None, :]).astype(np.float32)
        return np.cos(ang.astype(np.float64)).astype(np.float32), np.sin(ang.astype(np.float64)).astype(np.float32)
    for t in range(NT):
        c_, s_ = cs(np.arange(128) + 128 * t)
        cf[:, CF_CS1 + t * 16: CF_CS1 + t * 16 + 8] = c_
        cf[:, CF_CS1 + t * 16 + 8: CF_CS1 + t * 16 + 16] = s_
        cf[:, CF_CS2 + t * 16: CF_CS2 + t * 16 + 8] = -s_
        cf[:, CF_CS2 + t * 16 + 8: CF_CS2 + t * 16 + 16] = c_
    c_, s_ = cs(PAST + (np.arange(16) % 4))
    cf[0:16, CF_SS1:CF_SS1 + 8] = c_
    cf[0:16, CF_SS1 + 8:CF_SS1 + 16] = s_
    cf[0:16, CF_SS2:CF_SS2 + 8] = -s_
    cf[0:16, CF_SS2 + 8:CF_SS2 + 16] = c_
    return cb, cf


_CACHE = {}


def kernel(x_prompt, x_sample, cache_kv_w128, cache_kv_w512, cache_kv_w2048, cache_mem_kv, mem_prompt,
           w_in, sgu_ln_g, sgu_ln_b, w_spatial, b_spatial, w_mix_out, ln1_g, ln1_b,
           w_xq, w_xkv, w_xo, ln2_g, ln2_b, w_up, w_down, ln3_g, ln3_b, _stop_after=None):
    f = lambda a: np.ascontiguousarray(np.asarray(a, dtype=np.float32))
    key = ("nc", _stop_after)
    if key not in _CACHE:
        _CACHE[key] = build_program(_stop_after)
    nc = _CACHE[key]
    cb, cf = _const_tables()
    shared = {
        "w_in": f(w_in), "sgu_ln_g": f(sgu_ln_g), "sgu_ln_b": f(sgu_ln_b), "w_spatial": f(w_spatial),
        "b_spatial": f(b_spatial), "w_mix_out": f(w_mix_out), "ln1_g": f(ln1_g), "ln1_b": f(ln1_b),
        "w_xq": f(w_xq), "w_xkv": f(w_xkv), "w_xo": f(w_xo), "ln2_g": f(ln2_g), "ln2_b": f(ln2_b),
        "w_up": f(w_up), "w_down": f(w_down), "ln3_g": f(ln3_g), "ln3_b": f(ln3_b), "cb": cb, "cf": cf,
    }
    xpr, xsa = f(x_prompt), f(x_sample)
    c128, c512, c2048, cm, mp = f(cache_kv_w128), f(cache_kv_w512), f(cache_kv_w2048), f(cache_mem_kv), f(mem_prompt)
    in_maps = []
    for c in range(NCORES):
        bs = slice(4 * c, 4 * c + 4)
        m = dict(shared)
        m["xp"] = xpr[c]
        m["xs"] = np.ascontiguousarray(xsa[bs].reshape(NS, D))
        m["c128"] = np.ascontiguousarray(c128[:, bs].reshape(DEPTH, 4, 128, 512))
        m["c512"] = np.ascontiguousarray(c512[:, bs].reshape(DEPTH, 4, 512, 512))
        m["c2048"] = np.ascontiguousarray(c2048[:, bs].reshape(DEPTH, 4, 2048, 512))
        m["cmem"] = np.ascontiguousarray(cm[:, bs].reshape(DEPTH, 4, 256, 2048))
        m["memp"] = mp[c]
        in_maps.append(m)
    res = run_bass_kernel_spmd(nc, in_maps, core_ids=list(range(NCORES)))
    R = res.results

    def cat_p(name, shape_tail):
        return np.stack([np.asarray(R[c][name], np.float32).reshape((DEPTH,) + shape_tail) for c in range(NCORES)], axis=1)

    def cat_s(name, shape_tail):
        return np.concatenate([np.asarray(R[c][name], np.float32).reshape((DEPTH, 4, 4) + shape_tail) for c in range(NCORES)], axis=1)

    y_p = np.stack([np.asarray(R[c]["yp"], np.float32) for c in range(NCORES)], axis=0)
    y_s = np.concatenate([np.asarray(R[c]["ys"], np.float32).reshape(4, 4, D) for c in range(NCORES)], axis=0)
    return (y_p, y_s,
            cat_p("kv128p", (128, 2, 4, 64)), cat_p("kv512p", (512, 2, 4, 64)), cat_p("kv2048p", (2048, 2, 4, 64)),
            cat_p("memkvp", (256, 2, 4, 256)),
            cat_s("kv128s", (2, 4, 64)), cat_s("kv512s", (2, 4, 64)), cat_s("kv2048s", (2, 4, 64)),
            cat_s("chunkv", (256,)))
```
